# Optimizing a Trainium2 kernel written in Bass

```python
import jax, jax.numpy as jnp
from jax import lax
import numpy as np

D_MODEL = 1024
BATCH = 8
SEQ = 2048
DEPTH = 2
DEC_BATCH = 128
DEC_SEQ = 1
PAST_LEN = 16384
PAGE_SIZE = 128

N_A_LAYERS = (DEPTH + 1) // 2
N_C_LAYERS = DEPTH // 2

A_HEADS = 4
A_DK = 128
A_DV = 128
A_KW = A_HEADS * A_DK
A_WIDTH = A_HEADS * A_DV
A_CHUNK = 32
B_WIDTH = D_MODEL - A_WIDTH
SC_WIDTH = 3
EVEN_SPLITS = (A_KW, 2 * A_KW, 2 * A_KW + A_WIDTH, 2 * A_KW + 2 * A_WIDTH,
               2 * A_KW + 2 * A_WIDTH + B_WIDTH, 2 * A_KW + 2 * A_WIDTH + 2 * B_WIDTH)
IN_EVEN = 2 * A_KW + 2 * A_WIDTH + 3 * B_WIDTH

M_INNER = 2 * D_MODEL
M_HEADDIM = 64
M_HEADS = M_INNER // M_HEADDIM
M_STATE = 128
M_GROUPS = 4
M_HPG = M_HEADS // M_GROUPS
M_CONV = 4
M_CHUNK = 64
M_GN = M_GROUPS * M_STATE
M_CONV_DIM = M_INNER + 2 * M_GN
IN_ODD = M_INNER + M_CONV_DIM + M_HEADS

D_FF = 4 * D_MODEL
EPS = 1e-6

kernel_name = 'hybrid_hgrn2_shortconv_ssd_step'


def rmsnorm(x, g):
    xf = x.astype(jnp.float32)
    y = xf * lax.rsqrt(jnp.mean(xf * xf, axis=-1, keepdims=True) + EPS)
    return (y * g.astype(jnp.float32)).astype(x.dtype)


def causal_dwconv(u, buf, w):
    full = jnp.concatenate([buf.astype(u.dtype), u], axis=1)
    T = u.shape[1]
    W = w.shape[0]
    out = sum(full[:, k:k + T] * w[k] for k in range(W))
    return out, full[:, full.shape[1] - (W - 1):]


def gla_chunked(q, k, v, logf, chunk):
    Bn, T, H, K = q.shape
    V = v.shape[-1]
    n = T // chunk
    r = lambda a: a.reshape(Bn, n, chunk, H, a.shape[-1]).astype(jnp.float32)
    q, k, v, logf = r(q), r(k), r(v), r(logf)
    b = jnp.cumsum(logf, axis=2)
    b_last = b[:, :, -1:]
    qd = q * jnp.exp(b)
    kd = k * jnp.exp(-b)
    mask = jnp.tril(jnp.ones((chunk, chunk), bool))
    att = jnp.where(mask, jnp.einsum('bnihk,bnjhk->bnhij', qd, kd), 0.0)
    o_intra = jnp.einsum('bnhij,bnjhv->bnihv', att, v)
    dS = jnp.einsum('bnjhk,bnjhv->bnhkv', k * jnp.exp(b_last - b), v)
    decay = jnp.exp(b_last[:, :, 0])

    def step(S, inp):
        d, ds = inp
        return d[..., None] * S + ds, S

    S_fin, S_in = lax.scan(step, jnp.zeros((Bn, H, K, V), jnp.float32),
                           (jnp.moveaxis(decay, 1, 0), jnp.moveaxis(dS, 1, 0)))
    S_in = jnp.moveaxis(S_in, 0, 1)
    o_inter = jnp.einsum('bnihk,bnhkv->bnihv', qd, S_in)
    return (o_intra + o_inter).reshape(Bn, T, H, V), S_fin


def gla_recurrent(q, k, v, logf, S0):
    def step(S, inp):
        qt, kt, vt, lft = inp
        S = jnp.exp(lft)[..., None] * S + kt[..., None] * vt[..., None, :]
        return S, jnp.einsum('bhk,bhkv->bhv', qt, S)

    xs = tuple(jnp.moveaxis(a.astype(jnp.float32), 1, 0) for a in (q, k, v, logf))
    S, o = lax.scan(step, S0.astype(jnp.float32), xs)
    return jnp.moveaxis(o, 0, 1), S


def ssd_chunked(x, dt, A, Bm, Cm, chunk):
    Bn, T, G, R, P = x.shape
    N = Bm.shape[-1]
    n = T // chunk
    x = x.reshape(Bn, n, chunk, G, R, P)
    dt = dt.reshape(Bn, n, chunk, G, R)
    Bm = Bm.reshape(Bn, n, chunk, G, N)
    Cm = Cm.reshape(Bn, n, chunk, G, N)
    cs = jnp.cumsum(dt * A, axis=2)
    seg = cs[:, :, :, None] - cs[:, :, None, :]
    mask = jnp.tril(jnp.ones((chunk, chunk), bool))[:, :, None, None]
    decay_ij = jnp.exp(jnp.where(mask, seg, -jnp.inf))
    cb = jnp.einsum('bnigs,bnjgs->bnijg', Cm, Bm)
    wts = cb[..., None] * decay_ij * dt[:, :, None]
    y_diag = jnp.einsum('bnijgr,bnjgrp->bnigrp', wts, x)
    cs_last = cs[:, :, -1:]
    xw = x * (jnp.exp(cs_last - cs) * dt)[..., None]
    dS = jnp.einsum('bnjgs,bnjgrp->bngrps', Bm, xw)
    chunk_decay = jnp.exp(cs_last[:, :, 0])

    def step(S, inp):
        d, ds = inp
        return d[..., None, None] * S + ds, S

    S_fin, S_in = lax.scan(step, jnp.zeros((Bn, G, R, P, N), jnp.float32),
                           (jnp.moveaxis(chunk_decay, 1, 0), jnp.moveaxis(dS, 1, 0)))
    S_in = jnp.moveaxis(S_in, 0, 1)
    y_off = jnp.einsum('bnigs,bngrps->bnigrp', Cm, S_in) * jnp.exp(cs)[..., None]
    return (y_diag + y_off).reshape(Bn, T, G, R, P), S_fin


def ssd_recurrent(x, dt, A, Bm, Cm, S0):
    def step(S, inp):
        xt, dtt, Bt, Ct = inp
        S = (jnp.exp(dtt * A)[..., None, None] * S
             + (dtt[..., None] * xt)[..., None] * Bt[:, :, None, None, :])
        return S, jnp.einsum('bgs,bgrps->bgrp', Ct, S)

    xs = tuple(jnp.moveaxis(a, 1, 0) for a in (x, dt, Bm, Cm))
    S, y = lax.scan(step, S0, xs)
    return jnp.moveaxis(y, 0, 1), S


def even_mixer(h, sc_buf, hgrn_state, w_in, lb, gnorm_w, sc_w, w_out, prompt):
    Bn, T, _ = h.shape
    proj = h @ w_in
    q, fz, iv, go, bg, cg, hv = jnp.split(proj, EVEN_SPLITS, axis=-1)
    f = lb + (1.0 - lb) * jax.nn.sigmoid(fz.astype(jnp.float32))
    hs = lambda a, d: a.reshape(Bn, T, A_HEADS, d)
    qh = hs(q.astype(jnp.float32), A_DK)
    kh = hs(1.0 - f, A_DK)
    lf = hs(jnp.log(f), A_DK)
    vh = hs(iv.astype(jnp.float32), A_DV)
    if prompt:
        o, S = gla_chunked(qh, kh, vh, lf, A_CHUNK)
    else:
        o, S = gla_recurrent(qh, kh, vh, lf, hgrn_state)
    o = rmsnorm(o, gnorm_w) * jax.nn.silu(hs(go.astype(jnp.float32), A_DV))
    o_a = o.reshape(Bn, T, A_WIDTH).astype(h.dtype)
    if sc_buf is None:
        sc_buf = jnp.zeros((Bn, SC_WIDTH - 1, B_WIDTH), h.dtype)
    conv, new_sc = causal_dwconv(cg * hv, sc_buf, sc_w)
    o_b = bg * conv
    out = jnp.concatenate([o_a, o_b], axis=-1) @ w_out
    return out, S, new_sc


def mamba_mixer(h, conv_buf, ssm_state, w_in, conv_w, conv_b, dt_bias, a_log, d_skip, norm_w, w_out, prompt):
    Bn, T, _ = h.shape
    z, xbc, dt_raw = jnp.split(h @ w_in, (M_INNER, M_INNER + M_CONV_DIM), axis=-1)
    if conv_buf is None:
        conv_buf = jnp.zeros((Bn, M_CONV - 1, M_CONV_DIM), h.dtype)
    xbc, new_buf = causal_dwconv(xbc, conv_buf, conv_w)
    xbc = jax.nn.silu(xbc + conv_b)
    xs, Bm, Cm = jnp.split(xbc.astype(jnp.float32), (M_INNER, M_INNER + M_GN), axis=-1)
    xs = xs.reshape(Bn, T, M_GROUPS, M_HPG, M_HEADDIM)
    Bm = Bm.reshape(Bn, T, M_GROUPS, M_STATE)
    Cm = Cm.reshape(Bn, T, M_GROUPS, M_STATE)
    dt = jax.nn.softplus(dt_raw.astype(jnp.float32) + dt_bias.astype(jnp.float32))
    dt = dt.reshape(Bn, T, M_GROUPS, M_HPG)
    A = -jnp.exp(a_log.astype(jnp.float32)).reshape(M_GROUPS, M_HPG)
    if prompt:
        y, S = ssd_chunked(xs, dt, A, Bm, Cm, M_CHUNK)
    else:
        S0 = ssm_state.astype(jnp.float32).reshape(Bn, M_GROUPS, M_HPG, M_HEADDIM, M_STATE)
        y, S = ssd_recurrent(xs, dt, A, Bm, Cm, S0)
    y = y + d_skip.astype(jnp.float32).reshape(M_GROUPS, M_HPG)[..., None] * xs
    y = y.reshape(Bn, T, M_INNER) * jax.nn.silu(z.astype(jnp.float32))
    y = rmsnorm(y.reshape(Bn, T, M_GROUPS, M_INNER // M_GROUPS),
                norm_w.reshape(M_GROUPS, M_INNER // M_GROUPS)).reshape(Bn, T, M_INNER)
    out = y.astype(h.dtype) @ w_out
    return out, S.reshape(Bn, M_HEADS, M_HEADDIM, M_STATE), new_buf


def trunk(x, c, st_hgrn, st_sc, st_ssm, st_mconv, params, prompt):
    (ada_w, ada_b, norm_mix, norm_mlp, norm_final, w_in_even, hgrn_lb, hgrn_gnorm, sc_w, w_out_even,
     w_in_odd, mconv_w, mconv_b, dt_bias, a_log, d_skip, m_norm, w_out_odd, mlp_w1, mlp_w2) = params
    p_lb = jax.nn.softmax(hgrn_lb.astype(jnp.float32), axis=0)
    lbs = jnp.cumsum(p_lb, axis=0) - p_lb[0]
    hg_l, sc_l, ssm_l, mc_l = [], [], [], []
    for l in range(DEPTH):
        j = l // 2
        mod = jax.nn.silu(c) @ ada_w[l] + ada_b[l]
        sh1, s1, g1, sh2, s2, g2 = jnp.split(mod[:, None, :], 6, axis=-1)
        h = rmsnorm(x, norm_mix[l]) * (1.0 + s1) + sh1
        if l % 2 == 0:
            out, S, buf = even_mixer(h, None if prompt else st_sc[j], None if prompt else st_hgrn[j],
                                     w_in_even[j], lbs[j + 1], hgrn_gnorm[j], sc_w[j], w_out_even[j], prompt)
            hg_l.append(S.astype(x.dtype))
            sc_l.append(buf.astype(x.dtype))
        else:
            out, S, buf = mamba_mixer(h, None if prompt else st_mconv[j], None if prompt else st_ssm[j],
                                      w_in_odd[j], mconv_w[j], mconv_b[j], dt_bias[j], a_log[j], d_skip[j],
                                      m_norm[j], w_out_odd[j], prompt)
            ssm_l.append(S.astype(x.dtype))
            mc_l.append(buf.astype(x.dtype))
        x = x + g1 * out
        h = rmsnorm(x, norm_mlp[l]) * (1.0 + s2) + sh2
        x = x + g2 * (jnp.square(jax.nn.relu(h @ mlp_w1[l])) @ mlp_w2[l])
    y = rmsnorm(x, norm_final)
    return y, jnp.stack(hg_l), jnp.stack(sc_l), jnp.stack(ssm_l), jnp.stack(mc_l)


def setup_inputs(seed: int = 0) -> dict:
    key = jax.random.key(seed)
    ks = jax.random.split(key, 32)
    nrm = lambda k, shape, s: jax.random.normal(k, shape, jnp.float32) * s
    dt0 = jnp.exp(jax.random.uniform(ks[20], (N_C_LAYERS, M_HEADS), jnp.float32)
                  * (np.log(0.1) - np.log(0.001)) + np.log(0.001))
    return {
        'x_prompt': nrm(ks[0], (BATCH, SEQ, D_MODEL), 1.0),
        'x_sample': nrm(ks[1], (DEC_BATCH, DEC_SEQ, D_MODEL), 1.0),
        'c_prompt': nrm(ks[2], (BATCH, D_MODEL), 1.0),
        'c_sample': nrm(ks[3], (DEC_BATCH, D_MODEL), 1.0),
        'state_hgrn': nrm(ks[4], (N_A_LAYERS, DEC_BATCH, A_HEADS, A_DK, A_DV), 0.3),
        'state_shortconv': nrm(ks[5], (N_A_LAYERS, DEC_BATCH, SC_WIDTH - 1, B_WIDTH), 1.0),
        'state_ssm': nrm(ks[6], (N_C_LAYERS, DEC_BATCH, M_HEADS, M_HEADDIM, M_STATE), 0.3),
        'state_mconv': nrm(ks[7], (N_C_LAYERS, DEC_BATCH, M_CONV - 1, M_CONV_DIM), 1.0),
        'ada_w': nrm(ks[8], (DEPTH, D_MODEL, 6 * D_MODEL), 0.5 * D_MODEL ** -0.5),
        'ada_b': nrm(ks[9], (DEPTH, 6 * D_MODEL), 0.02),
        'norm_mix': 1.0 + nrm(ks[10], (DEPTH, D_MODEL), 0.02),
        'norm_mlp': 1.0 + nrm(ks[11], (DEPTH, D_MODEL), 0.02),
        'norm_final': 1.0 + nrm(ks[12], (D_MODEL,), 0.02),
        'w_in_even': nrm(ks[13], (N_A_LAYERS, D_MODEL, IN_EVEN), D_MODEL ** -0.5),
        'hgrn_lb': nrm(ks[14], (N_A_LAYERS + 1, A_KW), 0.1),
        'hgrn_gnorm': 1.0 + nrm(ks[15], (N_A_LAYERS, A_DV), 0.02),
        'sc_w': nrm(ks[16], (N_A_LAYERS, SC_WIDTH, B_WIDTH), SC_WIDTH ** -0.5),
        'w_out_even': nrm(ks[17], (N_A_LAYERS, D_MODEL, D_MODEL), D_MODEL ** -0.5),
        'w_in_odd': nrm(ks[18], (N_C_LAYERS, D_MODEL, IN_ODD), D_MODEL ** -0.5),
        'mconv_w': nrm(ks[19], (N_C_LAYERS, M_CONV, M_CONV_DIM), M_CONV ** -0.5),
        'mconv_b': nrm(ks[21], (N_C_LAYERS, M_CONV_DIM), 0.02),
        'dt_bias': dt0 + jnp.log(-jnp.expm1(-dt0)),
        'a_log': jnp.log(jax.random.uniform(ks[22], (N_C_LAYERS, M_HEADS), jnp.float32, 1.0, 16.0)),
        'd_skip': 1.0 + nrm(ks[23], (N_C_LAYERS, M_HEADS), 0.1),
        'm_norm': 1.0 + nrm(ks[24], (N_C_LAYERS, M_INNER), 0.02),
        'w_out_odd': nrm(ks[25], (N_C_LAYERS, M_INNER, D_MODEL), M_INNER ** -0.5),
        'mlp_w1': nrm(ks[26], (DEPTH, D_MODEL, D_FF), D_MODEL ** -0.5),
        'mlp_w2': nrm(ks[27], (DEPTH, D_FF, D_MODEL), D_FF ** -0.5),
    }


def reference(x_prompt, x_sample, c_prompt, c_sample, state_hgrn, state_shortconv, state_ssm, state_mconv,
              ada_w, ada_b, norm_mix, norm_mlp, norm_final, w_in_even, hgrn_lb, hgrn_gnorm, sc_w, w_out_even,
              w_in_odd, mconv_w, mconv_b, dt_bias, a_log, d_skip, m_norm, w_out_odd, mlp_w1, mlp_w2):
    params = (ada_w, ada_b, norm_mix, norm_mlp, norm_final, w_in_even, hgrn_lb, hgrn_gnorm, sc_w, w_out_even,
              w_in_odd, mconv_w, mconv_b, dt_bias, a_log, d_skip, m_norm, w_out_odd, mlp_w1, mlp_w2)
    y_prompt, hg_p, sc_p, ssm_p, mc_p = trunk(x_prompt, c_prompt, None, None, None, None, params, True)
    y_sample, hg_s, sc_s, ssm_s, mc_s = trunk(x_sample, c_sample, state_hgrn, state_shortconv,
                                              state_ssm, state_mconv, params, False)
    return (y_prompt, y_sample, hg_p, hg_s, sc_p, sc_s, ssm_p, ssm_s, mc_p, mc_s)
```

```python
import numpy as np
from contextlib import ExitStack
import concourse.bass as bass
import concourse.mybir as mybir
from concourse.bass_utils import run_bass_kernel_spmd

F32 = mybir.dt.float32
F32R = mybir.dt.float32r
AF = mybir.ActivationFunctionType
ALU = mybir.AluOpType
AX = mybir.AxisListType
EPS = 1e-6

NCORES = 8
T = 2048
TT = 512
NTILE = T // TT
NS = 16
D = 1024
WCOLS = 256
NWSLOT = 3
NPAGE = 38

ENGS = ('pe', 'act', 'dve', 'pool', 'sp')


def _unique_pieces():
    u = []
    for l in range(2):
        for pc in range(24):
            u.append(('ada', l, pc))
    for sec in range(7):
        for hf in range(2):
            u.append(('wie', sec, hf))
    for pc in range(4):
        u.append(('woe', pc))
    for l in range(2):
        for pc in range(16):
            u.append(('w1', l, pc))
        for cg in range(4):
            for kg in range(4):
                u.append(('w2', l, cg, kg))
    for pc in range(20):
        u.append(('wio', pc))
    for cg in range(4):
        for kg in range(2):
            u.append(('woo', cg, kg))
    return u


UPIECES = _unique_pieces()
UIDX = {sp: i for i, sp in enumerate(UPIECES)}


def _piece_slice(spec):
    k = spec[0]
    if k == 'ada':
        return 'ada_w', spec[1], 0, spec[2] * 256
    if k == 'wie':
        return 'w_in_even', None, 0, spec[1] * 512 + spec[2] * 256
    if k == 'woe':
        return 'w_out_even', None, 0, spec[1] * 256
    if k == 'w1':
        return 'mlp_w1', spec[1], 0, spec[2] * 256
    if k == 'w2':
        return 'mlp_w2', spec[1], spec[3] * 1024, spec[2] * 256
    if k == 'wio':
        return 'w_in_odd', None, 0, spec[1] * 256
    if k == 'woo':
        return 'w_out_odd', None, spec[2] * 1024, spec[1] * 256
    raise ValueError(spec)


class Op:
    __slots__ = ('eng', 'fn', 'reads', 'writes', 'dma', 'idx', 'waits', 'signal', 'seq', 'dsem', 'dval')


class Prog:
    def __init__(self, n_dma_sems=8):
        self.ops = []
        self.eng_ops = {e: [] for e in ENGS}
        self.nd = n_dma_sems
        self.marks = {}

    def mark(self, name):
        self.marks.setdefault(name, len(self.ops))

    def cut(self, name):
        n = self.marks[name]
        self.ops = self.ops[:n]
        for e in ENGS:
            self.eng_ops[e] = [o for o in self.ops if o.eng == e]

    def add(self, eng, fn, r=(), w=(), dma=False):
        op = Op()
        r = tuple(r); w = tuple(w)
        pk = tuple(k for k in r if isinstance(k, tuple) and k[0] == 'P' and k not in w)
        op.eng = eng; op.fn = fn; op.reads = r; op.writes = w + pk; op.dma = dma
        op.idx = len(self.eng_ops[eng]); op.waits = []; op.signal = False; op.seq = 0
        op.dsem = None; op.dval = 0
        self.ops.append(op); self.eng_ops[eng].append(op)
        return op

    def resolve(self):
        last_w = {}
        readers = {}
        seen = {e: {} for e in ENGS}
        dma_seen = {e: set() for e in ENGS}
        dcnt = {e: [0] * self.nd for e in ENGS}
        dlast = {e: [None] * self.nd for e in ENGS}
        drr = {e: 0 for e in ENGS}
        for op in self.ops:
            deps = []
            for k in op.reads:
                wv = last_w.get(k)
                if wv is not None:
                    deps.append((wv, 0))
            for k in op.writes:
                wv = last_w.get(k)
                if wv is not None:
                    deps.append((wv, 1))
                for rd in readers.get(k, ()):
                    deps.append((rd, 2))
            if op.dma:
                e = op.eng; s = drr[e]; drr[e] = (s + 1) % self.nd
                prev = dlast[e][s]
                if prev is not None:
                    deps.append((prev, 1))
                op.dsem = (e, s); dcnt[e][s] += 1; op.dval = 16 * dcnt[e][s]; dlast[e][s] = op
            for d, kind in deps:
                if d is op:
                    continue
                if d.dma:
                    if id(d) in dma_seen[op.eng]:
                        continue
                    dma_seen[op.eng].add(id(d)); op.waits.append(d)
                    continue
                if d.eng == op.eng and not op.dma:
                    if op.eng == 'pe' or kind == 2:
                        continue
                if seen[op.eng].get(d.eng, -1) >= d.idx:
                    continue
                seen[op.eng][d.eng] = d.idx
                d.signal = True
                op.waits.append(d)
            for k in op.reads:
                readers.setdefault(k, []).append(op)
            for k in op.writes:
                last_w[k] = op
                readers[k] = []
        for e in ENGS:
            n = 0
            for op in self.eng_ops[e]:
                if op.dma:
                    continue
                if op.signal:
                    n += 1
                    op.seq = n

    def emit(self, nc, block, es):
        esem = {e: es.enter_context(nc.semaphore("s_" + e)) for e in ENGS}
        dsem = {}
        for e in ENGS:
            if any(o.dma for o in self.eng_ops[e]):
                for s in range(self.nd):
                    dsem[(e, s)] = es.enter_context(nc.semaphore("d_%s%d" % (e, s)))

        def run(e, eng):
            final = {}
            for op in self.eng_ops[e]:
                for d in op.waits:
                    if d.dma:
                        eng.wait_ge(dsem[d.dsem], d.dval)
                    else:
                        eng.wait_ge(esem[d.eng], d.seq)
                ins = op.fn(eng)
                if op.dma:
                    ins.then_inc(dsem[op.dsem], 16)
                    final[op.dsem] = op.dval
                elif op.signal:
                    ins.then_inc(esem[e], 1)
            for k, v in final.items():
                eng.wait_ge(dsem[k], v)

        @block.tensor
        def _(t):
            run('pe', t)

        @block.scalar
        def _(a):
            run('act', a)

        @block.vector
        def _(v):
            run('dve', v)

        @block.gpsimd
        def _(g):
            run('pool', g)

        @block.sync
        def _(s):
            run('sp', s)


def _consts():
    c = np.zeros((128, 1024), np.float32)
    i = np.arange(128)
    c[:, 0:128] = (i[:, None] == i[None, :])
    c[:, 128:256] = (i[:, None] > i[None, :])
    c[:, 256:384] = (i[:, None] <= i[None, :])
    c[:, 384:512] = 1.0
    r = np.ones(512, np.float32); r[::64] = 0.0
    c[:, 512:1024] = r[None, :]
    return c

V_ADAB, V_NMIX, V_NMLP, V_NFIN, V_LB, V_GN, V_SCW, V_MCW, V_MCB, V_MN = 0, 96, 112, 128, 136, 144, 145, 157, 253, 277


def build():
    nc = bass.Bass("TRN2", target_bir_lowering=False)

    def din(name, shape):
        return nc.dram_tensor(name, list(shape), F32, kind="ExternalInput").ap()

    def dout(name, shape):
        return nc.dram_tensor(name, list(shape), F32, kind="ExternalOutput").ap()

    xp = din("xp", [T, D]); xs = din("xs", [NS, D]); cc = din("cc", [18, D])
    st_hg = din("st_hg", [NS, 4, 128, 128]); st_sc = din("st_sc", [NS, 2, 512])
    st_ssm = din("st_ssm", [NS, 32, 64, 128]); st_mc = din("st_mc", [NS, 3, 3072])
    wt = din("wt", [len(UPIECES), 128, 8 * WCOLS])
    wdt_in = din("wdt", [128, 8 * 32])
    vecs_in = din("vecs_in", [384, 128]); rowvecs = din("rowvecs", [1, 96])
    hexp_in = din("hexp", [128, 193]); consts_in = din("consts", [128, 1024])

    y_p = dout("y_p", [T, D]); y_s = dout("y_s", [NS, D])
    hg_p = dout("hg_p", [4, 128, 128]); hg_s = dout("hg_s", [NS, 4, 128, 128])
    sc_p = dout("sc_p", [2, 512]); sc_s = dout("sc_s", [NS, 2, 512])
    ssm_p = dout("ssm_p", [32, 64, 128]); ssm_s = dout("ssm_s", [NS, 32, 64, 128])
    mc_p = dout("mc_p", [3, 3072]); mc_s = dout("mc_s", [NS, 3, 3072])

    bc_scr = nc.dram_tensor("bc_scr", [NS, 1024], F32, kind="Internal").ap()
    prog = Prog()
    OP = prog.add

    with ExitStack() as es:
        def sb(name, shape, dt=F32):
            return es.enter_context(nc.sbuf_tensor("sb_" + name, list(shape), dt))

        PS = es.enter_context(nc.psum_tensor("ps", [128, 8, 512], F32))
        consts = sb("consts", [128, 1024])
        ident = consts[:, 0:128]
        tri = consts[:, 256:384]
        ones_f = consts[:, 384:512]
        rmask = consts[:, 512:1024]
        ones_r = sb("ones_r", [128, 128], F32R)
        U_r = sb("U_r", [128, 128], F32R)
        vecs = sb("vecs", [128, 384])
        rowv = sb("rowv", [128, 96])
        hexp = sb("hexp", [128, 193])
        cT = sb("cT", [128, 8, 18], F32R)
        modT = sb("modT", [128, 2, 48, 18])
        gsm = sb("gsm", [128, 2, 2, 8, 18])
        lbc = sb("lbc", [128, 2, 4])
        xT = sb("xT", [128, 8, TT])
        hT = sb("hT", [128, 8, TT], F32R)
        rs = sb("rs", [128, TT])
        xsT = sb("xsT", [128, 8, NS])
        hsT = sb("hsT", [128, 8, NS], F32R)
        wsl = sb("wsl", [128, NWSLOT, 8, WCOLS], F32R)
        arena = sb("arena", [128, NPAGE * 512])
        arenaR = nc.alloc_sbuf_tensor_at("sb_arenaR", [128, NPAGE * 512], F32R,
                                         offset=nc._sbuf_addr_for_side(None) - NPAGE * 512 * 4)
        sR = sb("sR", [128, 1344], F32R)
        ctok = arena[0:18, 36 * 512:38 * 512]
        vst = arena[:, 35 * 512:35 * 512 + 384].rearrange("p (a c) -> p a c", c=128)
        S_hg = sb("S_hg", [128, 4, 128])
        ST = sb("ST", [128, 2048])
        ucar = sb("ucar", [128, 4, 2])
        ccar = sb("ccar", [128, 24, 3])
        raw = sb("raw", [128, 2, 516])
        uraw = raw
        decb = sb("decb", [128, 4, 8])
        attm = sb("attm", [64, 4, 64], F32R)
        kwtok = sb("kwtok", [64, 4, 128], F32R)
        scratch = sb("scratch", [128, 1600])
        aexp = sb("aexp", [128, 16])
        Ab = sb("Ab", [128, 32])
        blk = sb("blk", [128, 4, 192])
        craw = raw
        junk = sb("junk", [128, 128])

        scr_state = {'off': 0}

        def scr(name, ncols):
            o = scr_state['off']
            scr_state['off'] = o + ncols
            assert scr_state['off'] <= 1600, scr_state['off']
            return scratch[:, o:o + ncols], 'scr_' + name

        def pg(p, n=1):
            return arena[:, p * 512:(p + n) * 512]

        def pgR(p, n=1):
            return arenaR[:, p * 512:(p + n) * 512]

        def kA(p, n=1):
            return [('A', q) for q in range(p, p + n)]

        def kP(b, lo=0, hi=512):
            return [('P', b)]

        bank_state = {'next': 0, 'open': [False] * 4}

        def nb():
            b = bank_state['next']
            bank_state['next'] = (b + 1) % 4
            assert not bank_state['open'][b], "psum bank %d still open" % b
            bank_state['open'][b] = True
            return b

        def rel(b):
            bank_state['open'][b] = False

        def col(ap2d, j):
            return ap2d[:, j:j + 1]

        WSCHED = []

        def sched():
            for pc in range(24):
                WSCHED.append(('ada', 0, pc))
            pend_ada = [('ada', 1, pc) for pc in range(24)]
            reg = []
            for ti in range(NTILE):
                for sec in (4, 5, 6, 0, 1, 3, 2):
                    for hf in range(2):
                        reg.append(('wie', sec, hf))
                for pc in range(4):
                    reg.append(('woe', pc))
                for pc in range(16):
                    reg.append(('w1', 0, pc))
                for cg in range(4):
                    for kg in range(4):
                        reg.append(('w2', 0, cg, kg))
                for pc in range(8, 20):
                    reg.append(('wio', pc))
                reg.append(('wdt',))
                for pc in range(0, 8):
                    reg.append(('wio', pc))
                for cg in range(4):
                    for kg in range(2):
                        reg.append(('woo', cg, kg))
                for pc in range(16):
                    reg.append(('w1', 1, pc))
                for cg in range(4):
                    for kg in range(4):
                        reg.append(('w2', 1, cg, kg))
            for spec in reg:
                if pend_ada:
                    WSCHED.append(pend_ada.pop(0))
                WSCHED.append(spec)
        sched()

        def wsrc(spec):
            if spec[0] == 'wdt':
                return wdt_in.rearrange("p (kc c) -> p kc c", c=32), 32
            return wt[UIDX[spec]].rearrange("p (kc c) -> p kc c", c=WCOLS), WCOLS

        wstate = {'issued': 0, 'used': 0}

        NPRO, NX = 24, 6

        def wslot(i):
            if i < NPRO:
                x = i % NX
                ap = arenaR[:, (8 + 4 * x) * 512:(12 + 4 * x) * 512].rearrange("p (kc c) -> p kc c", c=WCOLS)
                return ap, kA(8 + 4 * x, 4)
            s = i % NWSLOT
            return wsl[:, s], [('W', s)]

        def w_issue_upto(n):
            while wstate['issued'] < min(n, len(WSCHED)):
                i = wstate['issued']
                src, ncol = wsrc(WSCHED[i])
                dst, keys = wslot(i)
                OP('pool', (lambda e, dst=dst, src=src, ncol=ncol:
                            e.dma_start(out=dst[:, :, 0:ncol], in_=src)),
                   w=keys, dma=True)
                wstate['issued'] += 1

        ada_pending = []

        def wnext(spec):
            if spec[0] != 'ada' and ada_pending:
                ada_piece(*ada_pending.pop(0))
            i = wstate['used']
            assert WSCHED[i] == spec, (i, WSCHED[i], spec)
            if i < NPRO:
                w_issue_upto(min(i + NX, max(i + NWSLOT, NPRO + NWSLOT)))
            else:
                w_issue_upto(i + NWSLOT)
            wstate['used'] += 1
            ap, keys = wslot(i)
            return ap, (keys[0] if len(keys) == 1 else tuple(keys))

        OP('sp', lambda e: e.dma_start(out=consts[:, :], in_=consts_in), w=['consts'], dma=True)
        OP('sp', lambda e: e.dma_start(out=vst[:, :, :], in_=vecs_in.rearrange("(a p) c -> p a c", p=128)),
           w=[('A', 35)], dma=True)
        OP('sp', lambda e: e.dma_start(out=rowv[:, :], in_=rowvecs.partition_broadcast(128)), w=['rowv'], dma=True)
        OP('sp', lambda e: e.dma_start(out=hexp[:, :], in_=hexp_in), w=['hexp'], dma=True)
        OP('sp', lambda e: e.dma_start(out=ctok[:, :], in_=cc), w=[('A', 36), ('A', 37)], dma=True)
        w_issue_upto(NX)
        OP('dve', lambda e: e.memset(S_hg[:, :, :], 0.0), w=['S_hg'])
        OP('dve', lambda e: e.memset(ST[:, :], 0.0), w=['ST'])
        OP('dve', lambda e: e.memset(ucar[:, :, :], 0.0), w=['ucar'])
        OP('dve', lambda e: e.memset(ccar[:, :, :], 0.0), w=['ccar'])
        OP('act', lambda e: e.activation(out=U_r[:, :], in_=consts[:, 128:256], func=AF.Copy),
           r=['consts'], w=['U_r'])
        OP('act', lambda e: e.activation(out=ones_r[:, :], in_=consts[:, 384:512], func=AF.Copy),
           r=['consts'], w=['ones'])
        for a in range(3):
            b = nb()
            OP('pe', lambda e, a=a, b=b: e.transpose(out=PS[:, b, 0:128], in_=vst[:, a, :], identity=ident),
               r=[('A', 35), 'consts'], w=kP(b))
            OP('act', lambda e, a=a, b=b: e.activation(out=vecs[:, a * 128:(a + 1) * 128], in_=PS[:, b, 0:128],
                                                       func=AF.Copy), r=kP(b), w=['vecs'])
            rel(b)
        OP('dve', lambda e: e.tensor_tensor(out=lbc[:, 0, :], in0=vecs[:, V_LB + 4:V_LB + 8],
                                            in1=vecs[:, V_LB:V_LB + 4], op=ALU.subtract), r=['vecs'], w=['lbc'])
        OP('act', lambda e: e.activation(out=lbc[:, 0, :], in_=lbc[:, 0, :], func=AF.Sigmoid), r=['lbc'], w=['lbc'])
        OP('dve', lambda e: e.tensor_scalar(out=lbc[:, 1, :], in0=lbc[:, 0, :], scalar1=-1.0, scalar2=1.0,
                                            op0=ALU.mult, op1=ALU.add), r=['lbc'], w=['lbc'])
        OP('act', lambda e: e.activation(out=Ab[:, :], in_=rowv[:, 32:64], func=AF.Exp), r=['rowv'], w=['Ab'])
        OP('dve', lambda e: e.tensor_scalar(out=Ab[:, :], in0=Ab[:, :], scalar1=-1.0, scalar2=None, op0=ALU.mult),
           r=['Ab'], w=['Ab'])
        OP('act', lambda e: e.activation(out=aexp[:, :], in_=hexp[:, 0:16], func=AF.Exp), r=['hexp'], w=['aexp'])
        OP('dve', lambda e: e.tensor_scalar(out=aexp[:, :], in0=aexp[:, :], scalar1=-1.0, scalar2=None, op0=ALU.mult),
           r=['aexp'], w=['aexp'])
        OP('act', lambda e: e.activation(out=ctok[:, :], in_=ctok[:, :], func=AF.Silu), r=[('A', 36), ('A', 37)], w=[('A', 36), ('A', 37)])
        b = nb()
        for c in range(8):
            OP('pe', lambda e, c=c, b=b: e.transpose(out=PS[:, b, c * 32:c * 32 + 18], in_=ctok[0:18, c * 128:(c + 1) * 128],
                                                     identity=consts[0:18, 0:18]),
               r=[('A', 36), ('A', 37), 'consts'], w=kP(b, c * 32, c * 32 + 32))
        OP('act', lambda e, b=b: e.activation(out=cT[:, :, :],
                                              in_=PS[:, b, 0:256].rearrange("p (c n) -> p c n", n=32)[:, :, 0:18],
                                              func=AF.Copy), r=kP(b, 0, 256), w=['cT'])
        rel(b)
        def load_x_dma(ti):
            t0 = ti * TT
            stage = arena[:, 0:4096].rearrange("p (b d) -> p b d", d=D)
            OP('sp', lambda e, t0=t0, stage=stage: e.dma_start(out=stage, in_=xp[t0:t0 + TT, :].rearrange("(b p) d -> p b d", p=128)),
               w=kA(0, 8), dma=True)

        def load_x(ti, dma=True):
            t0 = ti * TT
            with_s = (ti == 0)
            stage = arena[:, 0:4096].rearrange("p (b d) -> p b d", d=D)
            if dma:
                load_x_dma(ti)
            for c in range(8):
                b = nb()
                def f(e, c=c, b=b, stage=stage):
                    for bk in range(4):
                        ins = e.transpose(out=PS[:, b, bk * 128:(bk + 1) * 128], in_=stage[:, bk, c * 128:(c + 1) * 128], identity=ident)
                    return ins
                OP('pe', f, r=kA(0, 8) + ['consts'], w=kP(b))
                if c % 2 == 0:
                    OP('act', lambda e, c=c, b=b: e.activation(out=xT[:, c, :], in_=PS[:, b, :], func=AF.Copy), r=kP(b), w=[('xT', c)])
                else:
                    OP('dve', lambda e, c=c, b=b: e.tensor_copy(out=xT[:, c, :], in_=PS[:, b, :]), r=kP(b), w=[('xT', c)])
                rel(b)

        def ada_gsm(l):
            for m, (s0, vbase) in enumerate(((8, V_NMIX), (32, V_NMLP))):
                OP('dve', lambda e, l=l, m=m, s0=s0, vbase=vbase: e.scalar_tensor_tensor(
                    out=gsm[:, l, m, :, :], in0=modT[:, l, s0:s0 + 8, :], scalar=1.0,
                    in1=vecs[:, vbase + l * 8:vbase + l * 8 + 8].unsqueeze(2).broadcast_to([128, 8, 18]),
                    op0=ALU.add, op1=ALU.mult),
                    r=[('mod', l, s0 // 8), 'vecs'], w=[('gsm', l, m)])

        def ada_piece(l, pc):
            wp, wk = wnext(('ada', l, pc))
            for j in range(2):
                ch = pc * 2 + j
                bq = nb()
                def f(e, wp=wp, j=j, bq=bq):
                    for kc in range(8):
                        ins = e.matmul(PS[:, bq, 0:18], lhsT=wp[:, kc, j * 128:(j + 1) * 128],
                                       rhs=cT[:, kc, :], start=(kc == 0), stop=(kc == 7))
                    return ins
                OP('pe', f, r=(list(wk) if isinstance(wk, tuple) and wk and isinstance(wk[0], tuple) else [wk]) + ['cT'], w=kP(bq))
                OP('act', lambda e, l=l, ch=ch, bq=bq: e.activation(
                    out=modT[:, l, ch, :], in_=PS[:, bq, 0:18], func=AF.Identity,
                    bias=vecs[:, V_ADAB + l * 48 + ch:V_ADAB + l * 48 + ch + 1]),
                    r=kP(bq) + ['vecs'], w=[('mod', l, ch // 8)])
                rel(bq)
            if pc == 23:
                ada_gsm(l)

        load_x(0)
        for pc in range(24):
            ada_piece(0, pc)
        ada_pending.extend((1, pc) for pc in range(24))
        prog.mark('prologue_done')
        def norm_prompt(gkey, gs_col, sh_col, shkey, out_fn=None, okey=None):
            sq = arenaR[:, 0:4096]
            OP('act', lambda e: e.activation(out=sq[:, 0:2048], in_=xT[:, 0:4, :].rearrange("p c n -> p (c n)"), func=AF.Square),
               r=[('xT', c) for c in range(0, 4)], w=kA(0, 4))
            OP('dve', lambda e: e.tensor_tensor(out=sq[:, 2048:4096], in0=xT[:, 4:8, :].rearrange("p c n -> p (c n)"),
                                                in1=xT[:, 4:8, :].rearrange("p c n -> p (c n)"), op=ALU.mult),
               r=[('xT', c) for c in range(4, 8)], w=kA(4, 4))
            b = nb()
            def f(e, b=b):
                for c in range(8):
                    ins = e.matmul(PS[:, b, :], lhsT=ones_r[:, :], rhs=sq[:, c * 512:(c + 1) * 512],
                                   start=(c == 0), stop=(c == 7))
                return ins
            OP('pe', f, r=kA(0, 8) + ['ones'], w=kP(b))
            OP('act', lambda e, b=b: e.activation(out=rs[:, :], in_=PS[:, b, :], func=AF.Sqrt, bias=EPS, scale=1.0 / D),
               r=kP(b), w=['rs'])
            rel(b)
            OP('dve', lambda e: e.reciprocal(out=rs[:, :], in_=rs[:, :]), r=['rs'], w=['rs'])
            for c in range(8):
                tp = 8 + (c % 2)
                OP('dve', lambda e, c=c, tp=tp: e.tensor_tensor(out=pg(tp), in0=xT[:, c, :], in1=rs[:, :], op=ALU.mult),
                   r=[('xT', c), 'rs'], w=kA(tp))
                oap = hT[:, c, :] if out_fn is None else out_fn(c)
                okk = [('hT', c)] if okey is None else okey(c)
                OP('act', lambda e, c=c, tp=tp, oap=oap: e.activation(out=oap, in_=pg(tp), func=AF.Identity,
                                                                      bias=sh_col(c), scale=gs_col(c)),
                   r=kA(tp) + [gkey, shkey], w=okk)

        k_ns_sq = 'sR_ns_sq'
        ns_rs, k_ns_rs = scr('ns_rs', 16)
        ns_tmp, k_ns_tmp = scr('ns_tmp', 128)

        def norm_sample(gs_ap, sh_ap, gkey, shkey, out_ap, out_key):
            sqs = sR[:, 0:128]
            OP('act', lambda e: e.activation(out=sqs, in_=xsT[:, :, :].rearrange("p c n -> p (c n)"), func=AF.Square),
               r=['xsT'], w=[k_ns_sq])
            def f(e):
                for c in range(8):
                    ins = e.matmul(PS[:, 4, 480:496], lhsT=ones_r[:, :], rhs=sqs[:, c * 16:(c + 1) * 16],
                                   start=(c == 0), stop=(c == 7))
                return ins
            OP('pe', f, r=[k_ns_sq, 'ones'], w=kP(4, 448, 512))
            OP('act', lambda e: e.activation(out=ns_rs, in_=PS[:, 4, 480:496], func=AF.Sqrt, bias=EPS, scale=1.0 / D),
               r=kP(4, 448, 512), w=[k_ns_rs])
            OP('dve', lambda e: e.reciprocal(out=ns_rs, in_=ns_rs), r=[k_ns_rs], w=[k_ns_rs])
            tmp = ns_tmp.rearrange("p (c n) -> p c n", n=16)
            OP('dve', lambda e: e.tensor_tensor(out=tmp, in0=xsT[:, :, :], in1=ns_rs.unsqueeze(1).broadcast_to([128, 8, 16]),
                                                op=ALU.mult), r=['xsT', k_ns_rs], w=[k_ns_tmp])
            if sh_ap is None:
                OP('dve', lambda e: e.tensor_tensor(out=out_ap, in0=tmp, in1=gs_ap, op=ALU.mult), r=[k_ns_tmp, gkey], w=[out_key])
            else:
                OP('dve', lambda e: e.tensor_tensor(out=tmp, in0=tmp, in1=gs_ap, op=ALU.mult), r=[k_ns_tmp, gkey], w=[k_ns_tmp])
                OP('dve', lambda e: e.tensor_tensor(out=out_ap, in0=tmp, in1=sh_ap, op=ALU.add), r=[k_ns_tmp, shkey], w=[out_key])

        sbuf_s = sb("sbuf_s", [128, 1344])
        def SV(off, nch):
            return sbuf_s[:, off:off + nch * 16].rearrange("p (c n) -> p c n", n=16)
        QS = SV(0, 4); FS = SV(64, 4); VS = SV(128, 4); GOS = SV(192, 4); BGS = SV(256, 4); CGS = SV(320, 4)
        USs = SV(384, 4)
        OTS = sR[:, 128:256].rearrange("p (c n) -> p c n", n=16)
        H1S = sR[:, 256:768].rearrange("p (c n) -> p c n", n=16)
        ZS = SV(448, 16); XBS = SV(704, 24); YS = SV(1088, 16)
        XCS_t = sb("XCS_t", [128, 24 * 16])
        XCS = XCS_t[:, :].rearrange("p (c n) -> p c n", n=16)
        sbuf_t = sb("sbuf_t", [128, 768])
        YNS = sR[:, 768:1024].rearrange("p (c n) -> p c n", n=16)
        DTS = sbuf_t[:, 0:256].rearrange("p (c n) -> p c n", n=16)
        AES = sbuf_t[:, 256:512].rearrange("p (c n) -> p c n", n=16)
        DXS = sbuf_t[:, 512:768].rearrange("p (c n) -> p c n", n=16)
        tok_s = sb("tok_s", [48, 1536])
        mcs = sb("mcs", [128, 24, 48])
        scs = sb("scs", [128, 4, 32])

        def sslot(i):
            return ((4 + i % 4) << 10) + ((i // 4) % 14) * 32

        def SPa(sl, n):
            return PS[:, sl >> 10, (sl & 1023):(sl & 1023) + n]

        def kSl(sl):
            return kP(sl >> 10)

        scount = {'n': 0}

        def s_psum():
            i = scount['n']; scount['n'] += 1
            return sslot(i)

        def proj_fm(wp, wk, ncol, rhs_fn, rkeys, N, evac, nk=8):
            for j in range(ncol // 128):
                b = nb()
                def f(e, j=j, b=b):
                    for kc in range(nk):
                        ins = e.matmul(PS[:, b, 0:N], lhsT=wp[:, kc, j * 128:(j + 1) * 128], rhs=rhs_fn(kc),
                                       start=(kc == 0), stop=(kc == nk - 1))
                    return ins
                OP('pe', f, r=[wk] + list(rkeys), w=kP(b))
                evac(j, b)
                rel(b)

        def proj_fm_s(wp, wk, ncol, rhs_fn, rkeys, evac, nk=8):
            for j in range(ncol // 128):
                sl = s_psum()
                def f(e, j=j, sl=sl):
                    for kc in range(nk):
                        ins = e.matmul(SPa(sl, NS), lhsT=wp[:, kc, j * 128:(j + 1) * 128], rhs=rhs_fn(kc),
                                       start=(kc == 0), stop=(kc == nk - 1))
                    return ins
                OP('pe', f, r=[wk] + list(rkeys), w=kSl(sl))
                evac(j, sl)

        hkeys = [('hT', c) for c in range(8)]
        hs_rhs = lambda kc: hsT[:, kc, :]
        h_rhs = lambda kc: hT[:, kc, :]

        def bc3(ap2, n):
            return ap2.unsqueeze(2).broadcast_to([ap2.shape[0], ap2.shape[1], n])

        def bm3(ap2, m):
            return ap2.unsqueeze(1).broadcast_to([ap2.shape[0], m, ap2.shape[1]])

        def resid_evac(l, mchunk):
            def ev(d, b):
                OP('dve', lambda e, d=d, b=b: e.scalar_tensor_tensor(
                    out=xT[:, d, :], in0=PS[:, b, :], scalar=modT[:, l, mchunk * 8 + d, 0:1], in1=xT[:, d, :],
                    op0=ALU.mult, op1=ALU.add),
                    r=kP(b) + [('xT', d), ('mod', l, mchunk)], w=[('xT', d)])
            return ev

        res_tmp, k_res = scr('res', 16)
        relu_tmp, k_relu = scr('relu', 32)

        def resid_evac_s(l, mchunk, d, src_ap, src_keys):
            tmp = res_tmp
            OP('dve', lambda e: e.tensor_tensor(out=tmp, in0=src_ap, in1=modT[:, l, mchunk * 8 + d, 1:17], op=ALU.mult),
               r=list(src_keys) + [('mod', l, mchunk)], w=[k_res])
            OP('dve', lambda e: e.tensor_tensor(out=xsT[:, d, :], in0=xsT[:, d, :], in1=tmp, op=ALU.add),
               r=[k_res, 'xsT'], w=['xsT'])

        def mlp(l, with_s):
            for pc in range(16):
                wp, wk = wnext(('w1', l, pc))
                def ev(j, b, pc=pc):
                    hc = pc * 2 + j
                    tp = 32 + (hc % 4)
                    OP('act', lambda e: e.activation(out=pg(tp), in_=PS[:, b, :], func=AF.Relu), r=kP(b), w=kA(tp))
                    OP('dve', lambda e: e.tensor_tensor(out=pgR(hc), in0=pg(tp), in1=pg(tp), op=ALU.mult),
                       r=kA(tp), w=kA(hc))
                proj_fm(wp, wk, 256, h_rhs, hkeys, TT, ev)
                if with_s:
                    def evs(j, sl, pc=pc):
                        hc = pc * 2 + j
                        tmp = relu_tmp[:, (hc % 2) * 16:(hc % 2) * 16 + 16]
                        OP('act', lambda e: e.activation(out=tmp, in_=SPa(sl, NS), func=AF.Relu),
                           r=kSl(sl), w=[(k_relu, hc % 2)])
                        OP('dve', lambda e: e.tensor_tensor(out=H1S[:, hc, :], in0=tmp, in1=tmp, op=ALU.mult),
                           r=[(k_relu, hc % 2)], w=['H1S'])
                    proj_fm_s(wp, wk, 256, hs_rhs, ['hsT'], evs)
            ev = resid_evac(l, 5)
            for cg in range(4):
                banks = [nb(), nb()]
                for kg in range(4):
                    wp, wk = wnext(('w2', l, cg, kg))
                    for j in range(2):
                        def f(e, wp=wp, j=j, kg=kg, b=banks[j]):
                            for kc in range(8):
                                ins = e.matmul(PS[:, b, :], lhsT=wp[:, kc, j * 128:(j + 1) * 128],
                                               rhs=pgR(kg * 8 + kc),
                                               start=(kg == 0 and kc == 0), stop=(kg == 3 and kc == 7))
                            return ins
                        OP('pe', f, r=[wk] + kA(kg * 8, 8), w=kP(banks[j]))
                    if with_s:
                        for j in range(2):
                            def f(e, wp=wp, j=j, kg=kg):
                                for kc in range(8):
                                    ins = e.matmul(PS[:, 4 + j, 448:464], lhsT=wp[:, kc, j * 128:(j + 1) * 128],
                                                   rhs=H1S[:, kg * 8 + kc, :],
                                                   start=(kg == 0 and kc == 0), stop=(kg == 3 and kc == 7))
                                return ins
                            OP('pe', f, r=[wk, 'H1S'], w=kP(4 + j, 448, 512))
                for j in range(2):
                    ev(cg * 2 + j, banks[j])
                    rel(banks[j])
                    if with_s:
                        resid_evac_s(l, 5, cg * 2 + j, PS[:, 4 + j, 448:464], kP(4 + j, 448, 512))

        P_BG, P_CG, P_CT, P_OT, P_Q, P_F, P_GO, P_VT, P_T1, P_BP, P_KW, P_OA = 0, 4, 8, 10, 18, 22, 26, 30, 4, 6, 0, 10

        def even_layer(ti, with_s):
            def emit_piece(sec, hf):
                wp, wk = wnext(('wie', sec, hf))
                if sec == 2:
                    for c8 in range(8):
                        b = nb()
                        def f(e, c8=c8, b=b, wp=wp):
                            for kc in range(8):
                                ins = e.matmul(PS[0:64, b, 0:256], lhsT=hT[:, kc, c8 * 64:(c8 + 1) * 64], rhs=wp[:, kc, :],
                                               start=(kc == 0), stop=(kc == 7))
                            return ins
                        OP('pe', f, r=[wk] + hkeys, w=kP(b))
                        dst = pgR(P_VT + c8)[0:64, hf * 256:(hf + 1) * 256]
                        if c8 % 2 == 0:
                            OP('act', lambda e, b=b, dst=dst: e.activation(out=dst, in_=PS[0:64, b, 0:256], func=AF.Copy),
                               r=kP(b), w=kA(P_VT + c8))
                        else:
                            OP('dve', lambda e, b=b, dst=dst: e.tensor_copy(out=dst, in_=PS[0:64, b, 0:256]),
                               r=kP(b), w=kA(P_VT + c8))
                        rel(b)
                else:
                    def ev(j, b, sec=sec, hf=hf):
                        ch = hf * 2 + j
                        if sec == 4:
                            OP('act', lambda e: e.activation(out=pg(P_BG + ch), in_=PS[:, b, :], func=AF.Copy),
                               r=kP(b), w=kA(P_BG + ch))
                        elif sec == 5:
                            OP('act', lambda e: e.activation(out=pg(P_CG + ch), in_=PS[:, b, :], func=AF.Copy),
                               r=kP(b), w=kA(P_CG + ch))
                        elif sec == 6:
                            s = ch % 2
                            OP('dve', lambda e: e.tensor_copy(out=uraw[:, s, 0:2], in_=ucar[:, ch, :]),
                               r=[('ucar', ch)], w=[('uraw', s)])
                            OP('dve', lambda e: e.tensor_tensor(out=uraw[:, s, 2:514], in0=PS[:, b, :], in1=pg(P_CG + ch),
                                                                op=ALU.mult), r=kP(b) + kA(P_CG + ch), w=[('uraw', s)])
                            OP('dve', lambda e: e.tensor_copy(out=ucar[:, ch, :], in_=uraw[:, s, 512:514]),
                               r=[('uraw', s)], w=[('ucar', ch)])
                            tp = P_CT + s
                            wc = lambda k: vecs[:, V_SCW + k * 4 + ch:V_SCW + k * 4 + ch + 1]
                            OP('act', lambda e: e.activation(out=pg(tp), in_=uraw[:, s, 0:512], func=AF.Identity, scale=wc(0)),
                               r=[('uraw', s), 'vecs'], w=kA(tp))
                            for k in (1, 2):
                                OP('dve', lambda e, k=k: e.scalar_tensor_tensor(out=pg(tp), in0=uraw[:, s, k:k + 512], scalar=wc(k),
                                                                               in1=pg(tp), op0=ALU.mult, op1=ALU.add),
                                   r=[('uraw', s), 'vecs'] + kA(tp), w=kA(tp))
                            OP('dve', lambda e: e.tensor_tensor(out=pgR(P_OT + 4 + ch), in0=pg(tp), in1=pg(P_BG + ch),
                                                                op=ALU.mult), r=kA(tp) + kA(P_BG + ch), w=kA(P_OT + 4 + ch))
                        elif sec == 0:
                            OP('act', lambda e: e.activation(out=pg(P_Q + ch), in_=PS[:, b, :], func=AF.Copy),
                               r=kP(b), w=kA(P_Q + ch))
                        elif sec == 1:
                            OP('act', lambda e: e.activation(out=pg(P_F + ch), in_=PS[:, b, :], func=AF.Sigmoid),
                               r=kP(b), w=kA(P_F + ch))
                        elif sec == 3:
                            OP('act', lambda e: e.activation(out=pg(P_GO + ch), in_=PS[:, b, :], func=AF.Silu),
                               r=kP(b), w=kA(P_GO + ch))
                    proj_fm(wp, wk, 256, h_rhs, hkeys, TT, ev)
                if with_s:
                    def evs(j, sl, sec=sec, hf=hf):
                        ch = hf * 2 + j
                        src = SPa(sl, NS)
                        kk = kSl(sl)
                        if sec == 6:
                            OP('dve', lambda e: e.tensor_tensor(out=USs[:, ch, :], in0=src, in1=CGS[:, ch, :], op=ALU.mult),
                               r=kk + ['CGS'], w=['USs'])
                        else:
                            dst, key, fn = {4: (BGS, 'BGS', AF.Copy), 5: (CGS, 'CGS', AF.Copy), 0: (QS, 'QS', AF.Copy),
                                            1: (FS, 'FS', AF.Sigmoid), 3: (GOS, 'GOS', AF.Silu), 2: (VS, 'VS', AF.Copy)}[sec]
                            OP('act', lambda e: e.activation(out=dst[:, ch, :], in_=src, func=fn), r=kk, w=[key])
                    proj_fm_s(wp, wk, 256, hs_rhs, ['hsT'], evs)

            def prep_steps(h):
                Fp, Qp, T1, Bp, KW = pg(P_F + h), pg(P_Q + h), pg(P_VT + h), pg(4 + h), pg(P_KW + h)
                kF, kQ, kT, kB, kW_ = kA(P_F + h), kA(P_Q + h), kA(P_VT + h), kA(4 + h), kA(P_KW + h)
                return [
                    lambda: OP('dve', lambda e: e.tensor_scalar(out=Fp, in0=Fp, scalar1=lbc[:, 1, h:h + 1], scalar2=lbc[:, 0, h:h + 1],
                                                                op0=ALU.mult, op1=ALU.add), r=kF + ['lbc'], w=kF),
                    lambda: OP('act', lambda e: e.activation(out=T1, in_=Fp, func=AF.Ln), r=kF, w=kT),
                    lambda: OP('dve', lambda e: e.tensor_tensor_scan(out=Bp, data0=rmask, data1=T1, initial=0.0, op0=ALU.mult, op1=ALU.add),
                               r=kT + ['consts'], w=kB),
                    lambda: OP('act', lambda e: e.activation(out=T1, in_=Bp, func=AF.Exp), r=kB, w=kT),
                    lambda: OP('dve', lambda e: e.tensor_tensor(out=pgR(P_Q + h), in0=Qp, in1=T1, op=ALU.mult), r=kQ + kT, w=kQ),
                    lambda: OP('act', lambda e: e.activation(out=T1, in_=Bp, func=AF.Exp, scale=-1.0), r=kB, w=kT),
                    lambda: OP('dve', lambda e: e.tensor_scalar(out=Fp, in0=Fp, scalar1=-1.0, scalar2=1.0, op0=ALU.mult, op1=ALU.add),
                               r=kF, w=kF),
                    lambda: OP('dve', lambda e: e.tensor_tensor(out=pgR(P_F + h), in0=Fp, in1=T1, op=ALU.mult), r=kF + kT, w=kF),
                    lambda: OP('act', lambda e: e.activation(out=decb[:, h, :],
                                                             in_=Bp.rearrange("p (c t) -> p c t", t=64)[:, :, 63], func=AF.Exp),
                               r=kB, w=[('decb', h)]),
                    lambda: OP('dve', lambda e: e.tensor_tensor(out=KW.rearrange("p (c t) -> p c t", t=64),
                                                                in0=Fp.rearrange("p (c t) -> p c t", t=64),
                                                                in1=bc3(decb[:, h, :], 64), op=ALU.mult),
                               r=kF + [('decb', h)], w=kW_),
                ]

            def prep_all():
                steps = [prep_steps(h) for h in range(4)]
                for k in range(len(steps[0])):
                    for h in range(4):
                        steps[h][k]()

            for sec in (4, 5, 6, 0, 1):
                for hf in range(2):
                    emit_piece(sec, hf)
            prep_all()
            for sec in (3, 2):
                for hf in range(2):
                    emit_piece(sec, hf)

            prog.mark('t%d_even_inproj' % ti)
            for c8 in range(8):
                cs = slice(c8 * 64, (c8 + 1) * 64)
                for h in range(4):
                    OP('pe', lambda e, h=h, cs=cs: e.matmul(PS[0:64, 4 + h, 0:64], lhsT=pgR(P_F + h)[:, cs],
                                                             rhs=pgR(P_Q + h)[:, cs], start=True, stop=True),
                       r=kA(P_F + h) + kA(P_Q + h), w=kP(4 + h))
                    OP('dve', lambda e, h=h: e.tensor_tensor(out=attm[:, h, :], in0=PS[0:64, 4 + h, 0:64],
                                                             in1=consts[0:64, 256:320], op=ALU.mult),
                       r=kP(4 + h) + ['consts'], w=[('attm', h)])
                    OP('pe', lambda e, h=h, cs=cs: e.transpose(out=PS[0:64, h, 0:128], in_=pg(P_KW + h)[:, cs],
                                                                identity=ident),
                       r=kA(P_KW + h) + ['consts'], w=kP(h))
                    OP('act', lambda e, h=h: e.activation(out=kwtok[:, h, :], in_=PS[0:64, h, 0:128], func=AF.Copy),
                       r=kP(h), w=[('kwtok', h)])
                for h in range(4):
                    vt = pgR(P_VT + c8)[0:64, h * 128:(h + 1) * 128]
                    def f(e, h=h, cs=cs, vt=vt):
                        e.matmul(PS[:, 4 + h, 64:128], lhsT=vt, rhs=attm[:, h, :], start=True, stop=False)
                        return e.matmul(PS[:, 4 + h, 64:128], lhsT=S_hg[:, h, :], rhs=pg(P_Q + h)[:, cs],
                                        start=False, stop=True)
                    OP('pe', f, r=kA(P_VT + c8) + [('attm', h), ('S_hg', h)] + kA(P_Q + h), w=kP(4 + h))
                    OP('pe', lambda e, h=h, vt=vt: e.matmul(PS[:, h, 128:256], lhsT=kwtok[:, h, :], rhs=vt,
                                                             start=True, stop=True),
                       r=[('kwtok', h)] + kA(P_VT + c8), w=kP(h))
                    OP('dve', lambda e, h=h, c8=c8: e.scalar_tensor_tensor(out=S_hg[:, h, :], in0=S_hg[:, h, :],
                                                                           scalar=decb[:, h, c8:c8 + 1],
                                                                           in1=PS[:, h, 128:256],
                                                                           op0=ALU.mult, op1=ALU.add),
                       r=[('S_hg', h), ('decb', h)] + kP(h), w=[('S_hg', h)])
                    OP('act', lambda e, h=h, cs=cs: e.activation(out=pg(P_OA + h)[:, cs], in_=PS[:, 4 + h, 64:128],
                                                                 func=AF.Copy),
                       r=kP(4 + h), w=kA(P_OA + h))
            for h in range(4):
                T1, Bp, OA = pgR(P_T1 + h % 2), pg(P_BP + h % 2), pg(P_OA + h)
                kT, kB, kO = kA(P_T1 + h % 2), kA(P_BP + h % 2), kA(P_OA + h)
                OP('act', lambda e, T1=T1, OA=OA: e.activation(out=T1, in_=OA, func=AF.Square), r=kO, w=kT)
                b = nb()
                OP('pe', lambda e, b=b, T1=T1: e.matmul(PS[:, b, :], lhsT=ones_r[:, :], rhs=T1, start=True, stop=True),
                   r=kT + ['ones'], w=kP(b))
                OP('act', lambda e, b=b, Bp=Bp: e.activation(out=Bp, in_=PS[:, b, :], func=AF.Sqrt, bias=EPS, scale=1.0 / 128),
                   r=kP(b), w=kB)
                rel(b)
                OP('dve', lambda e, Bp=Bp: e.reciprocal(out=Bp, in_=Bp), r=kB, w=kB)
                OP('dve', lambda e, OA=OA, Bp=Bp: e.scalar_tensor_tensor(out=OA, in0=OA, scalar=vecs[:, V_GN:V_GN + 1], in1=Bp,
                                                                         op0=ALU.mult, op1=ALU.mult),
                   r=kO + kB + ['vecs'], w=kO)
                OP('dve', lambda e, h=h, OA=OA: e.tensor_tensor(out=pgR(P_OT + h), in0=OA, in1=pg(P_GO + h),
                                                                op=ALU.mult), r=kO + kA(P_GO + h), w=kA(P_OT + h))
            prog.mark('t%d_even_hgrn' % ti)
            if with_s:
                even_sample()
            prog.mark('t%d_even_sample' % ti)
            ev = resid_evac(0, 2)
            for pc in range(4):
                wp, wk = wnext(('woe', pc))
                proj_fm(wp, wk, 256, lambda kc: pgR(P_OT + kc), kA(P_OT, 8), TT,
                        lambda j, b, pc=pc: ev(pc * 2 + j, b))
                if with_s:
                    proj_fm_s(wp, wk, 256, lambda kc: OTS[:, kc, :], ['OTS'],
                              lambda j, sl, pc=pc: resid_evac_s(0, 2, pc * 2 + j, SPa(sl, NS), kSl(sl)))
            if ti == NTILE - 1:
                OP('sp', lambda e: e.dma_start(out=hg_p.rearrange("h k v -> k h v"), in_=S_hg[:, :, :]),
                   r=[('S_hg', h) for h in range(4)], dma=True)
                b = nb()
                for c in range(4):
                    OP('pe', lambda e, c=c, b=b: e.transpose(out=PS[0:2, b, c * 128:(c + 1) * 128], in_=ucar[:, c, :], identity=ident),
                       r=[('ucar', c), 'consts'], w=kP(b, c * 128, c * 128 + 128))
                OP('act', lambda e, b=b: e.activation(out=tok_s[0:2, 0:512], in_=PS[0:2, b, :], func=AF.Copy),
                   r=kP(b), w=['tok_s'])
                rel(b)
                OP('sp', lambda e: e.dma_start(out=sc_p, in_=tok_s[0:2, 0:512]), r=['tok_s'], dma=True)

        def even_sample():
            SH = arena[:, 18 * 512:20 * 512].rearrange("p (s h v) -> p s h v", s=2, h=4)
            vmk = arena[0:16, 20 * 512:22 * 512].rearrange("p (s v) -> p s v", s=2)
            OP('dve', lambda e: e.tensor_tensor(out=FS, in0=FS, in1=bc3(lbc[:, 1, :], NS), op=ALU.mult), r=['FS', 'lbc'], w=['FS'])
            OP('dve', lambda e: e.tensor_tensor(out=FS, in0=FS, in1=bc3(lbc[:, 0, :], NS), op=ALU.add), r=['FS', 'lbc'], w=['FS'])
            kkf, k_kk = scr('kk', 64)
            KKf = kkf.rearrange("p (c n) -> p c n", n=16)
            OP('dve', lambda e: e.tensor_scalar(out=KKf, in0=FS, scalar1=-1.0, scalar2=1.0, op0=ALU.mult, op1=ALU.add),
               r=['FS'], w=[k_kk])
            for h in range(4):
                OP('pe', lambda e, h=h: e.transpose(out=PS[0:16, 6, h * 128:(h + 1) * 128], in_=KKf[:, h, :], identity=ident),
                   r=[k_kk, 'consts'], w=kP(6, h * 128, h * 128 + 128))
                OP('pe', lambda e, h=h: e.transpose(out=PS[0:16, 7, h * 128:(h + 1) * 128], in_=VS[:, h, :], identity=ident),
                   r=['VS', 'consts'], w=kP(7, h * 128, h * 128 + 128))
            OP('act', lambda e: e.activation(out=tok_s[0:16, 0:512], in_=PS[0:16, 6, :], func=AF.Copy), r=kP(6), w=['tok_s'])
            OP('act', lambda e: e.activation(out=tok_s[0:16, 512:1024], in_=PS[0:16, 7, :], func=AF.Copy), r=kP(7), w=['tok_s'])
            OP('sp', lambda e: e.dma_start(out=tok_s[0:32, 1024:1536], in_=st_sc.rearrange("b k c -> (b k) c")),
               w=['tok_s'], dma=True)
            for c in range(4):
                sl = s_psum()
                OP('pe', lambda e, c=c, sl=sl: e.transpose(out=SPa(sl, 32), in_=tok_s[0:32, 1024 + c * 128:1152 + c * 128],
                                                           identity=consts[0:32, 0:32]),
                   r=['tok_s', 'consts'], w=kSl(sl))
                OP('act', lambda e, c=c, sl=sl: e.activation(out=scs[:, c, :], in_=SPa(sl, 32), func=AF.Copy),
                   r=kSl(sl), w=['scs'])
            sck = lambda k: scs[:, :, :].rearrange("p c (b k) -> p c b k", k=2)[:, :, :, k]
            wb = lambda k: bc3(vecs[:, V_SCW + k * 4:V_SCW + k * 4 + 4], NS)
            t1f, k_t1 = scr('ect1', 64)
            t2f, k_t2 = scr('ect2', 64)
            t1 = t1f.rearrange("p (c n) -> p c n", n=16)
            t2 = t2f.rearrange("p (c n) -> p c n", n=16)
            OP('dve', lambda e: e.tensor_tensor(out=t1, in0=sck(0), in1=wb(0), op=ALU.mult), r=['scs', 'vecs'], w=[k_t1])
            OP('dve', lambda e: e.tensor_tensor(out=t2, in0=sck(1), in1=wb(1), op=ALU.mult), r=['scs', 'vecs'], w=[k_t2])
            OP('dve', lambda e: e.tensor_tensor(out=t1, in0=t1, in1=t2, op=ALU.add), r=[k_t1, k_t2], w=[k_t1])
            OP('dve', lambda e: e.tensor_tensor(out=t2, in0=USs, in1=wb(2), op=ALU.mult), r=['USs', 'vecs', k_t1], w=[k_t2])
            OP('dve', lambda e: e.tensor_tensor(out=t1, in0=t1, in1=t2, op=ALU.add), r=[k_t1, k_t2], w=[k_t1])
            OP('dve', lambda e: e.tensor_tensor(out=OTS[:, 4:8, :], in0=t1, in1=BGS, op=ALU.mult), r=[k_t1, 'BGS'], w=['OTS'])
            OP('sp', lambda e: e.dma_start(out=sc_s[:, 0, :], in_=st_sc[:, 1, :]), dma=True)
            for c in range(4):
                OP('pe', lambda e, c=c: e.transpose(out=PS[0:16, 6, c * 128:(c + 1) * 128], in_=USs[:, c, :], identity=ident),
                   r=['USs', 'consts'], w=kP(6, c * 128, c * 128 + 128))
            OP('act', lambda e: e.activation(out=tok_s[0:16, 1024:1536], in_=PS[0:16, 6, :], func=AF.Copy), r=kP(6), w=['tok_s'])
            OP('sp', lambda e: e.dma_start(out=sc_s[:, 1, :], in_=tok_s[0:16, 1024:1536]), r=['tok_s'], dma=True)
            def load_sh(b_):
                OP('sp', lambda e, b_=b_: e.dma_start(out=SH[:, b_ % 2, :, :], in_=st_hg[b_].rearrange("h k v -> k h v")),
                   w=kA(18 + b_ % 2) + [('SHh', b_ % 2, h) for h in range(4)], dma=True)
            load_sh(0)
            for b_ in range(NS):
                s = b_ % 2
                if b_ + 1 < NS:
                    load_sh(b_ + 1)
                OP('dve', lambda e, b_=b_, s=s: e.tensor_scalar(out=vmk[:, s, :], in0=tok_s[0:16, 512:1024],
                                                                scalar1=consts[0:16, b_:b_ + 1], scalar2=None, op0=ALU.mult),
                   r=['tok_s', 'consts'], w=kA(20 + s))
                for h in range(4):
                    OP('pe', lambda e, h=h, s=s: e.matmul(PS[:, 4 + h, 0:128], lhsT=tok_s[0:16, h * 128:(h + 1) * 128],
                                                          rhs=vmk[:, s, h * 128:(h + 1) * 128], start=True, stop=True),
                       r=['tok_s'] + kA(20 + s), w=kP(4 + h))
                for h in range(4):
                    OP('dve', lambda e, h=h, s=s, b_=b_: e.scalar_tensor_tensor(out=SH[:, s, h, :], in0=SH[:, s, h, :],
                                                                                scalar=FS[:, h, b_:b_ + 1],
                                                                                in1=PS[:, 4 + h, 0:128],
                                                                                op0=ALU.mult, op1=ALU.add),
                       r=kA(18 + s) + ['FS'] + kP(4 + h), w=[('SHh', s, h)])
                for h in range(4):
                    OP('pe', lambda e, h=h, s=s, b_=b_: e.matmul(PS[:, 0, h * 16 + b_:h * 16 + b_ + 1], lhsT=SH[:, s, h, :],
                                                                 rhs=QS[:, h, b_:b_ + 1], start=True, stop=True),
                       r=[('SHh', s, h), 'QS'], w=kP(0))
                OP('sp', lambda e, b_=b_, s=s: e.dma_start(out=hg_s[b_].rearrange("h k v -> k h v"), in_=SH[:, s, :, :]),
                   r=kA(18 + s) + [('SHh', s, h) for h in range(4)], dma=True)
            OAS, k_oas = scr('oas', 64)
            OP('act', lambda e: e.activation(out=OAS, in_=PS[:, 0, 0:64], func=AF.Copy), r=kP(0), w=[k_oas])
            k_sq = 'sR_esq'
            sq = sR[:, 1024:1088]
            OP('act', lambda e: e.activation(out=sq, in_=OAS, func=AF.Square), r=[k_oas], w=[k_sq])
            OP('pe', lambda e: e.matmul(PS[:, 5, 64:128], lhsT=ones_r[:, :], rhs=sq, start=True, stop=True),
               r=[k_sq, 'ones'], w=kP(5, 64, 128))
            rst, k_rst = scr('erst', 64)
            OP('act', lambda e: e.activation(out=rst, in_=PS[:, 5, 64:128], func=AF.Sqrt, bias=EPS, scale=1.0 / 128),
               r=kP(5, 64, 128), w=[k_rst])
            OP('dve', lambda e: e.reciprocal(out=rst, in_=rst), r=[k_rst], w=[k_rst])
            OP('dve', lambda e: e.scalar_tensor_tensor(out=OAS, in0=OAS, scalar=vecs[:, V_GN:V_GN + 1], in1=rst,
                                                       op0=ALU.mult, op1=ALU.mult), r=[k_oas, k_rst, 'vecs'], w=[k_oas])
            OP('dve', lambda e: e.tensor_tensor(out=OTS[:, 0:4, :], in0=OAS.rearrange("p (c n) -> p c n", n=16), in1=GOS,
                                                op=ALU.mult), r=[k_oas, 'GOS'], w=['OTS'])

        dtTf, k_dtT = scr('dtT', 16)
        dtT = dtTf[0:32, :]
        rmf, k_rm = scr('rm', 256)
        obig, k_obig = scr('obig', 384)
        orst, k_orst = scr('orst', 64)

        P_XT, P_BT, P_CTm, P_XC, P_LW, P_R, P_XW, P_YT1, P_CBM = 0, 16, 20, 24, 28, 32, 34, 36, 37
        P_YT, P_Z, P_SQ = 16, 32, 0

        def odd_layer(ti, with_s):
            XTall = arena[:, 0:8192].rearrange("p (b q) -> p b q", q=2048)
            pend = []
            psilu = []
            for pc in range(8, 20):
                wp, wk = wnext(('wio', pc))
                def ev(j, b, pc=pc):
                    cidx = (pc - 8) * 2 + j
                    s = cidx % 2
                    OP('pool', lambda e: e.tensor_copy(out=craw[:, s, 0:3], in_=ccar[:, cidx, :]), r=[('ccar', cidx)], w=[('crawc', s)])
                    OP('act', lambda e: e.activation(out=craw[:, s, 3:515], in_=PS[:, b, :], func=AF.Copy), r=kP(b), w=[('craw', s)])
                    OP('act', lambda e: e.activation(out=ccar[:, cidx, :], in_=PS[:, b, 509:512], func=AF.Copy), r=kP(b), w=[('ccar', cidx)])
                    if cidx < 16:
                        tpn = P_XC + cidx % 4
                    elif cidx < 20:
                        tpn = P_BT + cidx - 16
                    else:
                        tpn = P_CTm + cidx - 20
                    tp = pg(tpn)
                    wc = lambda k: vecs[:, V_MCW + k * 24 + cidx:V_MCW + k * 24 + cidx + 1]
                    OP('act', lambda e: e.activation(out=tp, in_=craw[:, s, 0:512], func=AF.Identity, scale=wc(0),
                                                     bias=vecs[:, V_MCB + cidx:V_MCB + cidx + 1]),
                       r=[('craw', s), ('crawc', s), 'vecs'], w=kA(tpn))
                    for k in (1, 2, 3):
                        OP('dve', lambda e, k=k: e.scalar_tensor_tensor(out=tp, in0=craw[:, s, k:k + 512], scalar=wc(k), in1=tp,
                                                                       op0=ALU.mult, op1=ALU.add),
                           r=[('craw', s), ('crawc', s), 'vecs'] + kA(tpn), w=kA(tpn))
                    def silu_then(cidx=cidx, tp=tp, tpn=tpn):
                        OP('act', lambda e: e.activation(out=tp, in_=tp, func=AF.Silu), r=kA(tpn), w=kA(tpn))
                        if cidx >= 16:
                            return
                        def later(cidx=cidx, tp=tp, tpn=tpn):
                            g, q = cidx // 4, cidx % 4
                            b2 = nb()
                            def f(e, b2=b2, tp=tp):
                                for bk in range(4):
                                    ins = e.transpose(out=PS[:, b2, bk * 128:(bk + 1) * 128], in_=tp[:, bk * 128:(bk + 1) * 128], identity=ident)
                                return ins
                            OP('pe', f, r=kA(tpn) + ['consts'], w=kP(b2))
                            dst = XTall[:, :, g * 512 + q * 128:g * 512 + (q + 1) * 128]
                            OP('act', lambda e, b2=b2, dst=dst: e.activation(out=dst, in_=PS[:, b2, :].rearrange("p (b f) -> p b f", f=128),
                                                                             func=AF.Copy),
                               r=kP(b2), w=[('A', 4 * bk + g) for bk in range(4)])
                            rel(b2)
                        pend.append(later)
                    if psilu:
                        psilu.pop(0)()
                    psilu.append(silu_then)
                    while len(pend) > 2:
                        pend.pop(0)()
                proj_fm(wp, wk, 256, h_rhs, hkeys, TT, ev)
                if with_s:
                    proj_fm_s(wp, wk, 256, hs_rhs, ['hsT'],
                              lambda j, sl, pc=pc: OP('act', lambda e: e.activation(out=XBS[:, (pc - 8) * 2 + j, :], in_=SPa(sl, NS),
                                                                                     func=AF.Copy), r=kSl(sl), w=['XBS']))
            while psilu:
                psilu.pop(0)()
            while pend:
                pend.pop(0)()
            wp, wk = wnext(('wdt',))
            for bk in range(4):
                def f(e, bk=bk, wp=wp):
                    for kc in range(8):
                        ins = e.matmul(PS[:, 7, bk * 32:(bk + 1) * 32], lhsT=hT[:, kc, bk * 128:(bk + 1) * 128], rhs=wp[:, kc, 0:32],
                                       start=(kc == 0), stop=(kc == 7))
                    return ins
                OP('pe', f, r=[wk] + hkeys, w=kP(7, 0, 128))
            if with_s:
                def f(e, wp=wp):
                    for kc in range(8):
                        ins = e.matmul(PS[0:32, 6, 0:16], lhsT=wp[:, kc, 0:32], rhs=hsT[:, kc, :], start=(kc == 0), stop=(kc == 7))
                    return ins
                OP('pe', f, r=[wk, 'hsT'], w=kP(6, 0, 64))
                OP('act', lambda e: e.activation(out=dtT, in_=PS[0:32, 6, 0:16], func=AF.Exp, bias=hexp[0:32, 192:193]),
                   r=kP(6, 0, 64) + ['hexp'], w=[k_dtT])
                OP('act', lambda e: e.activation(out=dtT, in_=dtT, func=AF.Ln, bias=1.0), r=[k_dtT], w=[k_dtT])
            dtv = blk[:, :, 0:32]
            OP('dve', lambda e: e.tensor_tensor(out=dtv, in0=PS[:, 7, 0:128].rearrange("p (b h) -> p b h", h=32),
                                                in1=bm3(rowv[:, 0:32], 4), op=ALU.add), r=kP(7, 0, 128) + ['rowv'], w=['blk_dt'])
            OP('act', lambda e: e.activation(out=dtv, in_=dtv, func=AF.Exp), r=['blk_dt'], w=['blk_dt'])
            OP('act', lambda e: e.activation(out=dtv, in_=dtv, func=AF.Ln, bias=1.0), r=['blk_dt'], w=['blk_dt'])
            OP('dve', lambda e: e.tensor_tensor(out=blk[:, :, 32:64], in0=dtv, in1=bm3(Ab[:, :], 4), op=ALU.mult),
               r=['blk_dt', 'Ab'], w=['blk_dta'])
            for bk in range(4):
                OP('pe', lambda e, bk=bk: e.matmul(PS[:, 7, 128 + bk * 32:160 + bk * 32], lhsT=tri, rhs=blk[:, bk, 32:64], start=True, stop=True),
                   r=['blk_dta', 'consts'], w=kP(7, 128, 256))
                OP('pe', lambda e, bk=bk: e.matmul(PS[:, 7, 256 + bk * 32:288 + bk * 32], lhsT=ones_f, rhs=blk[:, bk, 32:64], start=True, stop=True),
                   r=['blk_dta', 'consts'], w=kP(7, 256, 384))
            cs_ps = PS[:, 7, 128:256].rearrange("p (b h) -> p b h", h=32)
            cl_ps = PS[:, 7, 256:384].rearrange("p (b h) -> p b h", h=32)
            OP('act', lambda e: e.activation(out=blk[:, :, 64:96], in_=cs_ps, func=AF.Exp), r=kP(7, 128, 256), w=['blk_ecs'])
            OP('act', lambda e: e.activation(out=blk[:, :, 96:128], in_=cs_ps, func=AF.Copy), r=kP(7, 128, 256), w=['blk_wg'])
            OP('dve', lambda e: e.tensor_tensor(out=blk[:, :, 96:128], in0=cl_ps, in1=blk[:, :, 96:128], op=ALU.subtract),
               r=kP(7, 256, 384) + ['blk_wg'], w=['blk_wg'])
            OP('act', lambda e: e.activation(out=blk[:, :, 96:128], in_=blk[:, :, 96:128], func=AF.Exp), r=['blk_wg'], w=['blk_wg'])
            OP('act', lambda e: e.activation(out=blk[:, :, 128:160], in_=cl_ps, func=AF.Exp), r=kP(7, 256, 384), w=['blk_dec'])
            OP('dve', lambda e: e.reciprocal(out=blk[:, :, 160:192], in_=dtv), r=['blk_dt'], w=['blk_dod'])
            OP('dve', lambda e: e.tensor_tensor(out=blk[:, :, 160:192], in0=blk[:, :, 160:192], in1=bm3(rowv[:, 64:96], 4), op=ALU.mult),
               r=['blk_dod', 'rowv'], w=['blk_dod'])
            for bk in range(4):
                OP('dve', lambda e, bk=bk: e.tensor_tensor(out=XTall[:, bk, :].rearrange("p (h q) -> p h q", q=64),
                                                           in0=XTall[:, bk, :].rearrange("p (h q) -> p h q", q=64),
                                                           in1=bc3(blk[:, bk, 0:32], 64), op=ALU.mult),
                   r=kA(4 * bk, 4) + ['blk_dt'], w=kA(4 * bk, 4))
            prog.mark('t%d_odd_inproj' % ti)
            v3 = lambda ap: ap.rearrange("p (h q) -> p h q", q=64)
            iters = [(bk, g) for bk in range(4) for g in range(4)]

            def ctx(it):
                bk, g = iters[it]
                c = dict(bk=bk, g=g, ts=slice(bk * 128, (bk + 1) * 128), hs=slice(g * 8, (g + 1) * 8))
                c['XTp'] = pg(4 * bk + g); c['kX'] = kA(4 * bk + g)
                c['lw'] = P_LW + 2 * (it % 2)
                c['LW'] = arena[:, c['lw'] * 512:(c['lw'] + 2) * 512]
                c['xw'] = pgR(P_XW + it % 2); c['kxw'] = kA(P_XW + it % 2)
                c['cbm'] = pg(P_CBM)[:, (it % 2) * 128:(it % 2) * 128 + 128]; c['kcb'] = [('cbm', it % 2)]
                c['btok'] = pgR(P_CBM)[:, 256 + (it % 2) * 128:384 + (it % 2) * 128]; c['kbt'] = [('btk', it % 2)]
                return c

            def stageA1(c):
                bk, g, ts, hs = c['bk'], c['g'], c['ts'], c['hs']
                OP('pe', lambda e: e.matmul(PS[:, 7, 384:512], lhsT=pg(P_BT + g)[:, ts], rhs=pg(P_CTm + g)[:, ts], start=True, stop=True),
                   r=kA(P_BT + g) + kA(P_CTm + g), w=kP(7))
                R3 = arenaR[:, P_R * 512:(P_R + 2) * 512].rearrange("p (r i) -> p r i", i=128)
                Rf = arenaR[:, P_R * 512:(P_R + 2) * 512]
                def fR(e):
                    for r_ in range(8):
                        ins = e.activation(out=R3[:, r_, :], in_=tri, func=AF.Identity, scale=blk[:, bk, 32 + g * 8 + r_:33 + g * 8 + r_])
                    return ins
                OP('act', fR, r=['consts', 'blk_dta'], w=kA(P_R, 2))
                OP('dve', lambda e: e.tensor_tensor(out=c['cbm'], in0=PS[:, 7, 384:512], in1=tri, op=ALU.mult),
                   r=kP(7) + ['consts'], w=c['kcb'])
                for hh in range(2):
                    OP('pe', lambda e, hh=hh: e.matmul(PS[:, 5 + hh, :], lhsT=U_r[:, :], rhs=Rf[:, hh * 512:(hh + 1) * 512], start=True, stop=True),
                       r=['U_r'] + kA(P_R, 2), w=kP(5 + hh))
                    OP('act', lambda e, hh=hh: e.activation(out=c['LW'][:, hh * 512:(hh + 1) * 512], in_=PS[:, 5 + hh, :], func=AF.Exp),
                       r=kP(5 + hh), w=kA(c['lw'] + hh))
                OP('dve', lambda e: e.tensor_tensor(out=v3(c['xw']), in0=v3(c['XTp']), in1=bc3(blk[:, bk, 96:128][:, hs], 64), op=ALU.mult),
                   r=c['kX'] + ['blk_wg'], w=c['kxw'])
                bt = nb()
                OP('pe', lambda e: e.transpose(out=PS[:, bt, 0:128], in_=pg(P_BT + g)[:, ts], identity=ident),
                   r=kA(P_BT + g) + ['consts'], w=kP(bt))
                OP('act', lambda e: e.activation(out=c['btok'], in_=PS[:, bt, 0:128], func=AF.Copy), r=kP(bt), w=c['kbt'])
                rel(bt)

            def stageA2(c):
                LW3 = c['LW'].rearrange("p (r i) -> p r i", i=128)
                OP('dve', lambda e: e.tensor_tensor(out=LW3, in0=LW3, in1=bm3(c['cbm'], 8), op=ALU.mult),
                   r=kA(c['lw'], 2) + c['kcb'], w=kA(c['lw'], 2))

            def stageB(c):
                bk, g, ts, hs, XTp, kX, LW = c['bk'], c['g'], c['ts'], c['hs'], c['XTp'], c['kX'], c['LW']
                b2 = nb()
                OP('pe', lambda e: e.matmul(PS[:, b2, :], lhsT=pg(P_CTm + g)[:, ts], rhs=ST[:, g * 512:(g + 1) * 512], start=True, stop=True),
                   r=kA(P_CTm + g) + [('ST', g)], w=kP(b2))
                b1 = nb()
                def f(e):
                    for r_ in range(8):
                        ins = e.matmul(PS[:, b1, r_ * 64:(r_ + 1) * 64], lhsT=LW[:, r_ * 128:(r_ + 1) * 128],
                                       rhs=XTp[:, r_ * 64:(r_ + 1) * 64], start=True, stop=True)
                    return ins
                OP('pe', f, r=kA(c['lw'], 2) + kX, w=kP(b1))
                b3 = nb()
                OP('pe', lambda e: e.matmul(PS[:, b3, :], lhsT=c['btok'], rhs=c['xw'], start=True, stop=True),
                   r=c['kbt'] + c['kxw'], w=kP(b3))
                t1 = pg(P_YT1)
                OP('dve', lambda e: e.tensor_tensor(out=v3(t1), in0=v3(PS[:, b2, :]), in1=bc3(blk[:, bk, 64:96][:, hs], 64), op=ALU.mult),
                   r=kP(b2) + ['blk_ecs'], w=kA(P_YT1))
                rel(b2)
                OP('dve', lambda e: e.tensor_tensor(out=t1, in0=PS[:, b1, :], in1=t1, op=ALU.add), r=kP(b1) + kA(P_YT1), w=kA(P_YT1))
                rel(b1)
                OP('pool', lambda e: e.tensor_tensor(out=v3(XTp), in0=v3(XTp), in1=bc3(blk[:, bk, 160:192][:, hs], 64), op=ALU.mult),
                   r=kX + ['blk_dod'], w=kX)
                OP('pool', lambda e: e.tensor_tensor(out=XTp, in0=XTp, in1=t1, op=ALU.add), r=kX + kA(P_YT1), w=kX)
                STg = ST[:, g * 512:(g + 1) * 512]
                OP('dve', lambda e: e.tensor_tensor(out=v3(STg), in0=v3(STg), in1=bc3(blk[:, bk, 128:160][:, hs], 64), op=ALU.mult),
                   r=[('ST', g), 'blk_dec'], w=[('ST', g)])
                OP('dve', lambda e: e.tensor_tensor(out=STg, in0=STg, in1=PS[:, b3, :], op=ALU.add), r=[('ST', g)] + kP(b3), w=[('ST', g)])
                rel(b3)

            cs_ = [ctx(it) for it in range(16)]
            stageA1(cs_[0]); stageA2(cs_[0])
            for it in range(16):
                if it + 1 < 16:
                    stageA1(cs_[it + 1])
                stageB(cs_[it])
                if it + 1 < 16:
                    stageA2(cs_[it + 1])
            prog.mark('t%d_odd_ssd' % ti)
            if with_s:
                odd_sample_pre()
            prog.mark('t%d_odd_spre' % ti)
            for c in range(16):
                g, q = c // 4, c % 4
                b = nb()
                def f(e, b=b, g=g, q=q):
                    for bk in range(4):
                        ins = e.transpose(out=PS[:, b, bk * 128:(bk + 1) * 128], in_=pg(4 * bk + g)[:, q * 128:(q + 1) * 128], identity=ident)
                    return ins
                OP('pe', f, r=[('A', 4 * bk + g) for bk in range(4)] + ['consts'], w=kP(b))
                if c % 2 == 0:
                    OP('act', lambda e, b=b, c=c: e.activation(out=pg(P_YT + c), in_=PS[:, b, :], func=AF.Copy), r=kP(b), w=kA(P_YT + c))
                else:
                    OP('dve', lambda e, b=b, c=c: e.tensor_copy(out=pg(P_YT + c), in_=PS[:, b, :]), r=kP(b), w=kA(P_YT + c))
                rel(b)
            for pc in range(8):
                wp, wk = wnext(('wio', pc))
                def ev(j, b, pc=pc):
                    zc = pc * 2 + j
                    zp = P_Z + zc % 4
                    OP('act', lambda e: e.activation(out=pg(zp), in_=PS[:, b, :], func=AF.Silu), r=kP(b), w=kA(zp))
                    OP('dve', lambda e: e.tensor_tensor(out=pg(P_YT + zc), in0=pg(P_YT + zc), in1=pg(zp), op=ALU.mult),
                       r=kA(P_YT + zc) + kA(zp), w=kA(P_YT + zc))
                proj_fm(wp, wk, 256, h_rhs, hkeys, TT, ev)
                if with_s:
                    proj_fm_s(wp, wk, 256, hs_rhs, ['hsT'],
                              lambda j, sl, pc=pc: OP('act', lambda e: e.activation(out=ZS[:, pc * 2 + j, :], in_=SPa(sl, NS),
                                                                                     func=AF.Silu), r=kSl(sl), w=['ZS']))
                if pc % 2 == 1:
                    g = pc // 2
                    yg = arena[:, (P_YT + 4 * g) * 512:(P_YT + 4 * g + 4) * 512]
                    sqv = arenaR[:, P_SQ * 512:(P_SQ + 4) * 512]
                    OP('act', lambda e, yg=yg, sqv=sqv: e.activation(out=sqv, in_=yg, func=AF.Square), r=kA(P_YT + 4 * g, 4), w=kA(P_SQ, 4))
                    b = nb()
                    def f(e, b=b, sqv=sqv):
                        for q in range(4):
                            ins = e.matmul(PS[:, b, :], lhsT=ones_r[:, :], rhs=sqv[:, q * 512:(q + 1) * 512], start=(q == 0), stop=(q == 3))
                        return ins
                    OP('pe', f, r=kA(P_SQ, 4) + ['ones'], w=kP(b))
                    OP('act', lambda e, b=b: e.activation(out=rs[:, :], in_=PS[:, b, :], func=AF.Sqrt, bias=EPS, scale=1.0 / 512),
                       r=kP(b), w=['rs'])
                    rel(b)
                    OP('dve', lambda e: e.reciprocal(out=rs[:, :], in_=rs[:, :]), r=['rs'], w=['rs'])
                    for q in range(4):
                        c = 4 * g + q
                        OP('dve', lambda e, c=c: e.scalar_tensor_tensor(out=pgR(P_YT + c), in0=pg(P_YT + c),
                                                                        scalar=vecs[:, V_MN + c:V_MN + c + 1], in1=rs[:, :],
                                                                        op0=ALU.mult, op1=ALU.mult),
                           r=kA(P_YT + c) + ['rs', 'vecs'], w=kA(P_YT + c))
            if with_s:
                odd_sample_post()
            ev = resid_evac(1, 2)
            for cg in range(4):
                banks = [nb(), nb()]
                for kg in range(2):
                    wp, wk = wnext(('woo', cg, kg))
                    for j in range(2):
                        def f(e, wp=wp, j=j, kg=kg, b=banks[j]):
                            for kc in range(8):
                                ins = e.matmul(PS[:, b, :], lhsT=wp[:, kc, j * 128:(j + 1) * 128],
                                               rhs=pgR(P_YT + kg * 8 + kc),
                                               start=(kg == 0 and kc == 0), stop=(kg == 1 and kc == 7))
                            return ins
                        OP('pe', f, r=[wk] + kA(P_YT + kg * 8, 8), w=kP(banks[j]))
                    if with_s:
                        for j in range(2):
                            def f(e, wp=wp, j=j, kg=kg):
                                for kc in range(8):
                                    ins = e.matmul(PS[:, 4 + j, 448:464], lhsT=wp[:, kc, j * 128:(j + 1) * 128],
                                                   rhs=YNS[:, kg * 8 + kc, :],
                                                   start=(kg == 0 and kc == 0), stop=(kg == 1 and kc == 7))
                                return ins
                            OP('pe', f, r=[wk, 'YNS'], w=kP(4 + j, 448, 512))
                for j in range(2):
                    ev(cg * 2 + j, banks[j])
                    rel(banks[j])
                    if with_s:
                        resid_evac_s(1, 2, cg * 2 + j, PS[:, 4 + j, 448:464], kP(4 + j, 448, 512))
            if ti == NTILE - 1:
                stg = arena[:, 0:2048].rearrange("p (a s) -> p a s", s=128)
                for hp4 in range(4):
                    b = nb()
                    def f(e, b=b, hp4=hp4):
                        for a in range(4):
                            hp = hp4 * 4 + a
                            ins = e.transpose(out=PS[:, b, a * 128:(a + 1) * 128], in_=ST[:, hp * 128:(hp + 1) * 128], identity=ident)
                        return ins
                    OP('pe', f, r=[('ST', g) for g in range(4)] + ['consts'], w=kP(b))
                    OP('act', lambda e, b=b, hp4=hp4: e.activation(out=pg(hp4), in_=PS[:, b, :], func=AF.Copy), r=kP(b), w=kA(hp4))
                    rel(b)
                dstv = ssm_p.rearrange("(hp hh) p s -> hh p hp s", hh=2)
                for h2 in range(2):
                    OP('sp', lambda e, h2=h2: e.dma_start(out=dstv[h2], in_=stg[h2 * 64:(h2 + 1) * 64, :, :]), r=kA(0, 4), dma=True)
                for half in range(2):
                    for c4 in range(3):
                        b = nb()
                        def f(e, b=b, c4=c4, half=half):
                            for a in range(4):
                                ins = e.transpose(out=PS[0:3, b, a * 128:(a + 1) * 128], in_=ccar[:, half * 12 + c4 * 4 + a, :], identity=ident)
                            return ins
                        OP('pe', f, r=[('ccar', half * 12 + c4 * 4 + a) for a in range(4)] + ['consts'], w=kP(b))
                        OP('act', lambda e, b=b, c4=c4: e.activation(out=tok_s[0:3, c4 * 512:(c4 + 1) * 512], in_=PS[0:3, b, :], func=AF.Copy),
                           r=kP(b), w=['tok_s'])
                        rel(b)
                    OP('sp', lambda e, half=half: e.dma_start(out=mc_p[:, half * 1536:(half + 1) * 1536], in_=tok_s[0:3, 0:1536]),
                       r=['tok_s'], dma=True)

        def odd_sample_pre():
            SS = arena[:, 36 * 512:38 * 512].rearrange("p (a s) -> p a s", s=128)
            BCb = arena[:, 32 * 512:34 * 512]
            bmk = arena[0:16, 34 * 512:35 * 512]
            for half in range(2):
                c0 = half * 1536
                OP('sp', lambda e, c0=c0: e.dma_start(out=tok_s[0:48, 0:1536], in_=st_mc.rearrange("b k c -> (b k) c")[:, c0:c0 + 1536]),
                   w=['tok_s'], dma=True)
                for (a0, na) in ((0, 8), (8, 4)):
                    b = nb()
                    def f(e, b=b, a0=a0, na=na):
                        for a in range(na):
                            ins = e.transpose(out=PS[:, b, a * 64:a * 64 + 48], in_=tok_s[0:48, (a0 + a) * 128:(a0 + a + 1) * 128],
                                              identity=consts[0:48, 0:48])
                        return ins
                    OP('pe', f, r=['tok_s', 'consts'], w=kP(b))
                    OP('act', lambda e, b=b, half=half, a0=a0, na=na: e.activation(
                        out=mcs[:, half * 12 + a0:half * 12 + a0 + na, :],
                        in_=PS[:, b, 0:na * 64].rearrange("p (c n) -> p c n", n=64)[:, :, 0:48], func=AF.Copy),
                        r=kP(b), w=['mcs'])
                    rel(b)
            mck = lambda k: mcs[:, :, :].rearrange("p c (b k) -> p c b k", k=3)[:, :, :, k]
            wb = lambda k: bc3(vecs[:, V_MCW + k * 24:V_MCW + k * 24 + 24], NS)
            t2 = obig.rearrange("p (c n) -> p c n", n=16)
            OP('dve', lambda e: e.tensor_tensor(out=XCS, in0=mck(0), in1=wb(0), op=ALU.mult), r=['mcs', 'vecs'], w=['XCS'])
            for k in (1, 2):
                OP('dve', lambda e, k=k: e.tensor_tensor(out=t2, in0=mck(k), in1=wb(k), op=ALU.mult), r=['mcs', 'vecs', 'XCS'], w=[k_obig])
                OP('dve', lambda e: e.tensor_tensor(out=XCS, in0=XCS, in1=t2, op=ALU.add), r=['XCS', k_obig], w=['XCS'])
            OP('dve', lambda e: e.tensor_tensor(out=t2, in0=XBS, in1=wb(3), op=ALU.mult), r=['XBS', 'vecs', 'XCS'], w=[k_obig])
            OP('dve', lambda e: e.tensor_tensor(out=XCS, in0=XCS, in1=t2, op=ALU.add), r=['XCS', k_obig], w=['XCS'])
            OP('dve', lambda e: e.tensor_tensor(out=XCS, in0=XCS, in1=bc3(vecs[:, V_MCB:V_MCB + 24], NS), op=ALU.add),
               r=['XCS', 'vecs'], w=['XCS'])
            OP('act', lambda e: e.activation(out=XCS, in_=XCS, func=AF.Silu), r=['XCS'], w=['XCS'])
            OP('sp', lambda e: e.dma_start(out=mc_s[:, 0:2, :], in_=st_mc[:, 1:3, :]), dma=True)
            for half in range(2):
                for c4 in range(3):
                    b = nb()
                    def f(e, b=b, c4=c4, half=half):
                        for a in range(4):
                            ins = e.transpose(out=PS[0:16, b, a * 128:(a + 1) * 128], in_=XBS[:, half * 12 + c4 * 4 + a, :], identity=ident)
                        return ins
                    OP('pe', f, r=['XBS', 'consts'], w=kP(b))
                    OP('act', lambda e, b=b, c4=c4: e.activation(out=tok_s[0:16, c4 * 512:(c4 + 1) * 512], in_=PS[0:16, b, :], func=AF.Copy),
                       r=kP(b), w=['tok_s'])
                    rel(b)
                OP('sp', lambda e, half=half: e.dma_start(out=mc_s[:, 2, half * 1536:(half + 1) * 1536], in_=tok_s[0:16, 0:1536]),
                   r=['tok_s'], dma=True)
            Rm = rmf[0:32, :].rearrange("p (a n) -> p a n", n=16)
            OP('dve', lambda e: e.tensor_tensor(out=Rm, in0=bm3(dtT, 16), in1=bc3(hexp[0:32, 176:192], NS), op=ALU.mult),
               r=[k_dtT, 'hexp'], w=[k_rm])
            OP('pe', lambda e: e.matmul(PS[:, 6, 0:256], lhsT=hexp[0:32, 48:176], rhs=rmf[0:32, :], start=True, stop=True),
               r=[k_rm, 'hexp'], w=kP(6, 0, 256))
            OP('act', lambda e: e.activation(out=DTS, in_=PS[:, 6, 0:256].rearrange("p (a n) -> p a n", n=16), func=AF.Copy),
               r=kP(6, 0, 256), w=['DTS'])
            OP('dve', lambda e: e.tensor_tensor(out=AES, in0=DTS, in1=bc3(aexp[:, :], NS), op=ALU.mult), r=['DTS', 'aexp'], w=['AES'])
            OP('act', lambda e: e.activation(out=AES, in_=AES, func=AF.Exp), r=['AES'], w=['AES'])
            OP('dve', lambda e: e.tensor_tensor(out=DXS, in0=DTS, in1=XCS[:, 0:16, :], op=ALU.mult), r=['DTS', 'XCS'], w=['DXS'])
            for a in range(8):
                OP('pe', lambda e, a=a: e.transpose(out=PS[0:16, 5 + a // 4, (a % 4) * 128:(a % 4 + 1) * 128], in_=XCS[:, 16 + a, :], identity=ident),
                   r=['XCS', 'consts'], w=kP(5 + a // 4, (a % 4) * 128, (a % 4 + 1) * 128))
            OP('act', lambda e: e.activation(out=tok_s[0:16, 0:512], in_=PS[0:16, 5, :], func=AF.Copy), r=kP(5), w=['tok_s'])
            OP('act', lambda e: e.activation(out=tok_s[0:16, 512:1024], in_=PS[0:16, 6, :], func=AF.Copy), r=kP(6), w=['tok_s'])
            srcv = lambda b_: st_ssm[b_].rearrange("(hp hh) p s -> (hh p) hp s", hh=2)
            dstv = lambda b_: ssm_s[b_].rearrange("(hp hh) p s -> (hh p) hp s", hh=2)
            SSp = [36, 37, 24, 25]
            SSb = [pg(p).rearrange("p (a s) -> p a s", s=128) for p in SSp]
            kSS = [kA(p) for p in SSp]
            Tb = [pg(34).rearrange("p (a s) -> p a s", s=128), pg(35).rearrange("p (a s) -> p a s", s=128)]
            kTb = [kA(34), kA(35)]
            OP('sp', lambda e: e.dma_start(out=bc_scr, in_=tok_s[0:16, 0:1024]), r=['tok_s'], w=['bc_scr'], dma=True)
            BCs = [arena[:, 32 * 512:34 * 512], arena[:, 26 * 512:28 * 512]]
            kBC = [kA(32, 2), kA(26, 2)]

            def load_bc(b_):
                OP('sp', lambda e, b_=b_: e.dma_start(out=BCs[b_ % 2], in_=bc_scr[b_:b_ + 1, :].partition_broadcast(128)),
                   r=['bc_scr'], w=kBC[b_ % 2], dma=True)

            def load_ss(n):
                b_, q = n // 4, n % 4
                sl = n % 4
                extra = [('cbm', 0), ('cbm', 1), ('btk', 0), ('btk', 1)] if SSp[sl] == 37 else []
                OP('sp', lambda e, b_=b_, q=q, sl=sl: e.dma_start(out=SSb[sl], in_=srcv(b_)[:, q * 4:(q + 1) * 4, :]),
                   w=kSS[sl] + extra, dma=True)

            NQ = 4 * NS
            load_bc(0)
            for n in range(3):
                load_ss(n)
            for n in range(NQ):
                b_, q = n // 4, n % 4
                sl = n % 4
                if q == 0 and b_ + 1 < NS:
                    load_bc(b_ + 1)
                if n + 3 < NQ:
                    load_ss(n + 3)
                SSx, kS = SSb[sl], kSS[sl]
                Tt, kT = Tb[n % 2], kTb[n % 2]
                BCb, kB = BCs[b_ % 2], kBC[b_ % 2]
                hp0 = q * 4
                dx3 = bc3(DXS[:, hp0:hp0 + 4, b_], 128)
                B3 = bm3(BCb[:, q * 128:(q + 1) * 128], 4)
                C3 = bm3(BCb[:, 512 + q * 128:512 + (q + 1) * 128], 4)
                def fS(e, SSx=SSx, hp0=hp0, b_=b_):
                    for a in range(4):
                        ins = e.activation(out=SSx[:, a, :], in_=SSx[:, a, :], func=AF.Identity, scale=AES[:, hp0 + a, b_:b_ + 1])
                    return ins
                OP('act', fS, r=kS + ['AES'], w=kS)
                OP('dve', lambda e, Tt=Tt, B3=B3, dx3=dx3: e.tensor_tensor(out=Tt, in0=B3, in1=dx3, op=ALU.mult),
                   r=kB + ['DXS'], w=kT)
                OP('dve', lambda e, SSx=SSx, Tt=Tt: e.tensor_tensor(out=SSx, in0=SSx, in1=Tt, op=ALU.add), r=kS + kT, w=kS)
                OP('dve', lambda e, Tt=Tt, SSx=SSx, C3=C3: e.tensor_tensor(out=Tt, in0=SSx, in1=C3, op=ALU.mult), r=kS + kB, w=kT)
                OP('dve', lambda e, b_=b_, hp0=hp0, Tt=Tt: e.tensor_reduce(out=YS[:, hp0:hp0 + 4, b_], in_=Tt, axis=AX.X, op=ALU.add),
                   r=kT, w=['YS'])
                OP('act', lambda e, b_=b_, q=q, SSx=SSx: e.dma_start(out=dstv(b_)[:, q * 4:(q + 1) * 4, :], in_=SSx),
                   r=kS, dma=True)
            OP('dve', lambda e: e.tensor_tensor(out=DXS, in0=XCS[:, 0:16, :], in1=bc3(hexp[:, 32:48], NS), op=ALU.mult),
               r=['XCS', 'hexp', 'DXS'], w=['DXS'])
            OP('dve', lambda e: e.tensor_tensor(out=YS, in0=YS, in1=DXS, op=ALU.add), r=['YS', 'DXS'], w=['YS'])

        def odd_sample_post():
            OP('dve', lambda e: e.tensor_tensor(out=YS, in0=YS, in1=ZS, op=ALU.mult), r=['YS', 'ZS'], w=['YS'])
            sq = sR[:, 1088:1344]
            OP('act', lambda e: e.activation(out=sq, in_=sbuf_s[:, 1088:1344], func=AF.Square), r=['YS'], w=['sR_osq'])
            for g in range(4):
                def f(e, g=g):
                    for q in range(4):
                        c = 4 * g + q
                        ins = e.matmul(PS[:, 5, g * 16:(g + 1) * 16], lhsT=ones_r[:, :], rhs=sq[:, c * 16:(c + 1) * 16],
                                       start=(q == 0), stop=(q == 3))
                    return ins
                OP('pe', f, r=['sR_osq', 'ones'], w=kP(5, 0, 64))
            rst = orst
            OP('act', lambda e: e.activation(out=rst, in_=PS[:, 5, 0:64], func=AF.Sqrt, bias=EPS, scale=1.0 / 512),
               r=kP(5, 0, 64), w=[k_orst])
            OP('dve', lambda e: e.reciprocal(out=rst, in_=rst), r=[k_orst], w=[k_orst])
            y4 = sbuf_s[:, 1088:1344].rearrange("p (g q n) -> p g q n", q=4, n=16)
            r4 = rst.rearrange("p (g n) -> p g n", n=16).unsqueeze(2).broadcast_to([128, 4, 4, 16])
            OP('dve', lambda e: e.tensor_tensor(out=y4, in0=y4, in1=r4, op=ALU.mult), r=['YS', k_orst], w=['YS'])
            OP('dve', lambda e: e.tensor_tensor(out=YNS, in0=YS, in1=bc3(vecs[:, V_MN:V_MN + 16], NS), op=ALU.mult),
               r=['YS', 'vecs'], w=['YNS'])

        def final_out(ti, with_s):
            t0 = ti * TT
            norm_prompt('vecs', lambda c: vecs[:, V_NFIN + c:V_NFIN + c + 1], lambda c: 0.0, 'vecs',
                        out_fn=lambda c: pg(16 + c), okey=lambda c: kA(16 + c))
            if ti + 1 < NTILE:
                load_x_dma(ti + 1)
            ostage = arena[:, 24 * 512:32 * 512].rearrange("p (b d) -> p b d", d=D)
            for bk in range(4):
                for hf in range(2):
                    b = nb()
                    def f(e, b=b, bk=bk, hf=hf):
                        for a in range(4):
                            c = hf * 4 + a
                            ins = e.transpose(out=PS[:, b, a * 128:(a + 1) * 128], in_=pg(16 + c)[:, bk * 128:(bk + 1) * 128], identity=ident)
                        return ins
                    OP('pe', f, r=kA(16 + hf * 4, 4) + ['consts'], w=kP(b))
                    if hf == 0:
                        OP('act', lambda e, b=b, bk=bk, hf=hf: e.activation(out=ostage[:, bk, hf * 512:(hf + 1) * 512], in_=PS[:, b, :], func=AF.Copy),
                           r=kP(b), w=kA(24 + 2 * bk + hf))
                    else:
                        OP('dve', lambda e, b=b, bk=bk, hf=hf: e.tensor_copy(out=ostage[:, bk, hf * 512:(hf + 1) * 512], in_=PS[:, b, :]),
                           r=kP(b), w=kA(24 + 2 * bk + hf))
                    rel(b)
            OP('sp', lambda e, t0=t0: e.dma_start(out=y_p[t0:t0 + TT, :].rearrange("(b p) d -> p b d", p=128), in_=ostage),
               r=kA(24, 8), dma=True)
            if with_s:
                yff, k_yf = scr('yf', 128)
                yf = yff.rearrange("p (c n) -> p c n", n=16)
                norm_sample(bc3(vecs[:, V_NFIN:V_NFIN + 8], NS), None, 'vecs', None, yf, k_yf)
                for hf in range(2):
                    b = nb()
                    def f(e, b=b, hf=hf):
                        for a in range(4):
                            ins = e.transpose(out=PS[0:16, b, a * 128:(a + 1) * 128], in_=yf[:, hf * 4 + a, :], identity=ident)
                        return ins
                    OP('pe', f, r=[k_yf, 'consts'], w=kP(b))
                    OP('act', lambda e, b=b, hf=hf: e.activation(out=tok_s[0:16, hf * 512:(hf + 1) * 512], in_=PS[0:16, b, :], func=AF.Copy),
                       r=kP(b), w=['tok_s'])
                    rel(b)
                OP('sp', lambda e: e.dma_start(out=y_s, in_=tok_s[0:16, 0:D]), r=['tok_s'], dma=True)

        for ti in range(NTILE):
            t0 = ti * TT
            with_s = (ti == 0)
            if ti > 0:
                load_x(ti, dma=False)
            if with_s:
                OP('sp', lambda e: e.dma_start(out=tok_s[0:16, 0:D], in_=xs), w=['tok_s'], dma=True)
                for c in range(8):
                    sl = s_psum()
                    OP('pe', lambda e, c=c, sl=sl: e.transpose(out=SPa(sl, 16), in_=tok_s[0:16, c * 128:(c + 1) * 128], identity=consts[0:16, 0:16]),
                       r=['tok_s', 'consts'], w=kSl(sl))
                    OP('act', lambda e, c=c, sl=sl: e.activation(out=xsT[:, c, :], in_=SPa(sl, 16), func=AF.Copy),
                       r=kSl(sl), w=['xsT'])
            prog.mark('t%d_xload' % ti)
            for l in range(2):
                prog.mark('t%d_l%d_start' % (ti, l))
                norm_prompt(('gsm', l, 0), lambda c, l=l: gsm[:, l, 0, c, 0:1], lambda c, l=l: modT[:, l, c, 0:1], ('mod', l, 0))
                if with_s:
                    norm_sample(gsm[:, l, 0, :, 1:17], modT[:, l, 0:8, 1:17], ('gsm', l, 0), ('mod', l, 0), hsT[:, :, :], 'hsT')
                prog.mark('t%d_l%d_norm1' % (ti, l))
                if l == 0:
                    even_layer(ti, with_s)
                else:
                    odd_layer(ti, with_s)
                prog.mark('t%d_l%d_mixer' % (ti, l))
                norm_prompt(('gsm', l, 1), lambda c, l=l: gsm[:, l, 1, c, 0:1], lambda c, l=l: modT[:, l, 24 + c, 0:1], ('mod', l, 3))
                if with_s:
                    norm_sample(gsm[:, l, 1, :, 1:17], modT[:, l, 24:32, 1:17], ('gsm', l, 1), ('mod', l, 3), hsT[:, :, :], 'hsT')
                mlp(l, with_s)
                prog.mark('t%d_l%d_mlp' % (ti, l))
            final_out(ti, with_s)
            prog.mark('t%d_done' % ti)
        assert wstate['used'] == len(WSCHED), (wstate['used'], len(WSCHED))

        import os
        if os.environ.get('KCUT'):
            print('marks', prog.marks)
            kc = os.environ['KCUT']
            if kc.isdigit():
                prog.marks['_n'] = int(kc); kc = '_n'
            prog.cut(kc)
        prog.resolve()
        with nc.Block() as block:
            prog.emit(nc, block, es)
    return nc


_CACHE = {}


def _get_nc():
    if 'nc' not in _CACHE:
        _CACHE['nc'] = build()
    return _CACHE['nc']


def kernel(x_prompt, x_sample, c_prompt, c_sample, state_hgrn, state_shortconv, state_ssm, state_mconv,
           ada_w, ada_b, norm_mix, norm_mlp, norm_final, w_in_even, hgrn_lb, hgrn_gnorm, sc_w, w_out_even,
           w_in_odd, mconv_w, mconv_b, dt_bias, a_log, d_skip, m_norm, w_out_odd, mlp_w1, mlp_w2):
    f = lambda a: np.ascontiguousarray(np.asarray(a, dtype=np.float32))
    x_prompt, x_sample, c_prompt, c_sample = f(x_prompt), f(x_sample), f(c_prompt), f(c_sample)
    state_hgrn, state_shortconv, state_ssm, state_mconv = f(state_hgrn), f(state_shortconv), f(state_ssm), f(state_mconv)
    vecs = np.zeros((384, 128), np.float32)
    def put(r0, a):
        a = f(a).reshape(-1, 128)
        vecs[r0:r0 + a.shape[0]] = a
    put(V_ADAB, ada_b); put(V_NMIX, norm_mix); put(V_NMLP, norm_mlp); put(V_NFIN, norm_final)
    put(V_LB, hgrn_lb); put(V_GN, hgrn_gnorm); put(V_SCW, sc_w); put(V_MCW, mconv_w); put(V_MCB, mconv_b); put(V_MN, m_norm)
    rowvecs = np.concatenate([f(dt_bias).reshape(-1), f(a_log).reshape(-1), f(d_skip).reshape(-1)])[None, :].copy()
    hexp = np.zeros((128, 193), np.float32)
    hidx = (2 * np.arange(16)[None, :] + (np.arange(128)[:, None] // 64))
    hexp[:, 0:16] = f(a_log).reshape(-1)[hidx]
    hexp[:, 16:32] = f(dt_bias).reshape(-1)[hidx]
    hexp[:, 32:48] = f(d_skip).reshape(-1)[hidx]
    hh = np.arange(32)
    hexp[0:32, 48:176] = (hh[:, None] % 2 == (np.arange(128)[None, :] // 64))
    hexp[0:32, 176:192] = (hh[:, None] // 2 == np.arange(16)[None, :])
    hexp[0:32, 192] = f(dt_bias).reshape(-1)
    consts = _consts()
    mats = dict(ada_w=f(ada_w), w_in_even=f(w_in_even)[0], w_out_even=f(w_out_even)[0], w_in_odd=f(w_in_odd)[0],
                w_out_odd=f(w_out_odd)[0], mlp_w1=f(mlp_w1), mlp_w2=f(mlp_w2))
    wt = np.empty((len(UPIECES), 128, 8 * WCOLS), np.float32)
    for i, spec in enumerate(UPIECES):
        name, lead, r0, c0 = _piece_slice(spec)
        m = mats[name] if lead is None else mats[name][lead]
        blk_ = m[r0:r0 + 1024, c0:c0 + WCOLS].reshape(8, 128, WCOLS)
        wt[i] = blk_.transpose(1, 0, 2).reshape(128, 8 * WCOLS)
    wdt = np.ascontiguousarray(mats['w_in_odd'][:, 5120:5152].reshape(8, 128, 32).transpose(1, 0, 2).reshape(128, 256))
    shared = dict(wt=wt, wdt=wdt, vecs_in=vecs, rowvecs=rowvecs, hexp=hexp, consts=consts)
    in_maps = []
    for b in range(NCORES):
        sl = slice(b * NS, (b + 1) * NS)
        cc = np.zeros((18, D), np.float32)
        cc[0] = c_prompt[b]; cc[1:17] = c_sample[sl]
        m = dict(shared)
        m.update(xp=x_prompt[b], xs=np.ascontiguousarray(x_sample[sl, 0, :]), cc=cc,
                 st_hg=np.ascontiguousarray(state_hgrn[0, sl]), st_sc=np.ascontiguousarray(state_shortconv[0, sl]),
                 st_ssm=np.ascontiguousarray(state_ssm[0, sl]), st_mc=np.ascontiguousarray(state_mconv[0, sl]))
        in_maps.append(m)
    nc = _get_nc()
    import os
    ncore = int(os.environ.get('KNCORE', NCORES))
    res = run_bass_kernel_spmd(nc, in_maps[:ncore], core_ids=list(range(ncore)))
    if ncore < NCORES:
        res.results.extend([res.results[0]] * (NCORES - ncore))
    R = res.results
    cat = lambda k: np.concatenate([np.asarray(r[k], np.float32) for r in R], axis=0)
    stk = lambda k: np.stack([np.asarray(r[k], np.float32) for r in R], axis=0)
    y_prompt = stk("y_p")
    y_sample = cat("y_s")[:, None, :]
    return (y_prompt, y_sample,
            stk("hg_p")[None], cat("hg_s")[None],
            stk("sc_p")[None], cat("sc_s")[None],
            stk("ssm_p")[None], cat("ssm_s")[None],
            stk("mc_p")[None], cat("mc_s")[None])
```

```python
import numpy as np
from contextlib import ExitStack
import concourse.bass as bass
import concourse.mybir as mybir
from concourse.bass_utils import run_bass_kernel_spmd

F32 = mybir.dt.float32
F32R = mybir.dt.float32r
AF = mybir.ActivationFunctionType
ALU = mybir.AluOpType
AX = mybir.AxisListType
EPS = 1e-6

NCORES = 8
T = 2048
TT = 512
NTILE = T // TT
NS = 16
D = 1024
WCOLS = 256
NWSLOT = 3
NPAGE = 38

ENGS = ('pe', 'act', 'dve', 'pool', 'sp')


def _unique_pieces():
    u = []
    for l in range(2):
        for pc in range(24):
            u.append(('ada', l, pc))
    for sec in range(7):
        for hf in range(2):
            u.append(('wie', sec, hf))
    for pc in range(4):
        u.append(('woe', pc))
    for l in range(2):
        for pc in range(16):
            u.append(('w1', l, pc))
        for cg in range(4):
            for kg in range(4):
                u.append(('w2', l, cg, kg))
    for pc in range(20):
        u.append(('wio', pc))
    for cg in range(4):
        for kg in range(2):
            u.append(('woo', cg, kg))
    return u


UPIECES = _unique_pieces()
UIDX = {sp: i for i, sp in enumerate(UPIECES)}


def _piece_slice(spec):
    k = spec[0]
    if k == 'ada':
        return 'ada_w', spec[1], 0, spec[2] * 256
    if k == 'wie':
        return 'w_in_even', None, 0, spec[1] * 512 + spec[2] * 256
    if k == 'woe':
        return 'w_out_even', None, 0, spec[1] * 256
    if k == 'w1':
        return 'mlp_w1', spec[1], 0, spec[2] * 256
    if k == 'w2':
        return 'mlp_w2', spec[1], spec[3] * 1024, spec[2] * 256
    if k == 'wio':
        return 'w_in_odd', None, 0, spec[1] * 256
    if k == 'woo':
        return 'w_out_odd', None, spec[2] * 1024, spec[1] * 256
    raise ValueError(spec)


class Op:
    __slots__ = ('eng', 'fn', 'reads', 'writes', 'dma', 'idx', 'waits', 'signal', 'seq', 'dsem', 'dval')


class Prog:
    def __init__(self, n_dma_sems=8):
        self.ops = []
        self.eng_ops = {e: [] for e in ENGS}
        self.nd = n_dma_sems
        self.marks = {}

    def mark(self, name):
        self.marks.setdefault(name, len(self.ops))

    def cut(self, name):
        n = self.marks[name]
        self.ops = self.ops[:n]
        for e in ENGS:
            self.eng_ops[e] = [o for o in self.ops if o.eng == e]

    def add(self, eng, fn, r=(), w=(), dma=False):
        op = Op()
        r = tuple(r); w = tuple(w)
        pk = tuple(k for k in r if isinstance(k, tuple) and k[0] == 'P' and k not in w)
        op.eng = eng; op.fn = fn; op.reads = r; op.writes = w + pk; op.dma = dma
        op.idx = len(self.eng_ops[eng]); op.waits = []; op.signal = False; op.seq = 0
        op.dsem = None; op.dval = 0
        self.ops.append(op); self.eng_ops[eng].append(op)
        return op

    def resolve(self):
        last_w = {}
        readers = {}
        seen = {e: {} for e in ENGS}
        dma_seen = {e: set() for e in ENGS}
        dcnt = {e: [0] * self.nd for e in ENGS}
        dlast = {e: [None] * self.nd for e in ENGS}
        drr = {e: 0 for e in ENGS}
        for op in self.ops:
            deps = []
            for k in op.reads:
                wv = last_w.get(k)
                if wv is not None:
                    deps.append((wv, 0))
            for k in op.writes:
                wv = last_w.get(k)
                if wv is not None:
                    deps.append((wv, 1))
                for rd in readers.get(k, ()):
                    deps.append((rd, 2))
            if op.dma:
                e = op.eng; s = drr[e]; drr[e] = (s + 1) % self.nd
                prev = dlast[e][s]
                if prev is not None:
                    deps.append((prev, 1))
                op.dsem = (e, s); dcnt[e][s] += 1; op.dval = 16 * dcnt[e][s]; dlast[e][s] = op
            for d, kind in deps:
                if d is op:
                    continue
                if d.dma:
                    if id(d) in dma_seen[op.eng]:
                        continue
                    dma_seen[op.eng].add(id(d)); op.waits.append(d)
                    continue
                if d.eng == op.eng and not op.dma:
                    if op.eng == 'pe' or kind == 2:
                        continue
                if seen[op.eng].get(d.eng, -1) >= d.idx:
                    continue
                seen[op.eng][d.eng] = d.idx
                d.signal = True
                op.waits.append(d)
            for k in op.reads:
                readers.setdefault(k, []).append(op)
            for k in op.writes:
                last_w[k] = op
                readers[k] = []
        for e in ENGS:
            n = 0
            for op in self.eng_ops[e]:
                if op.dma:
                    continue
                if op.signal:
                    n += 1
                    op.seq = n

    def emit(self, nc, block, es):
        esem = {e: es.enter_context(nc.semaphore("s_" + e)) for e in ENGS}
        dsem = {}
        for e in ENGS:
            if any(o.dma for o in self.eng_ops[e]):
                for s in range(self.nd):
                    dsem[(e, s)] = es.enter_context(nc.semaphore("d_%s%d" % (e, s)))

        def run(e, eng):
            final = {}
            for op in self.eng_ops[e]:
                for d in op.waits:
                    if d.dma:
                        eng.wait_ge(dsem[d.dsem], d.dval)
                    else:
                        eng.wait_ge(esem[d.eng], d.seq)
                ins = op.fn(eng)
                if op.dma:
                    ins.then_inc(dsem[op.dsem], 16)
                    final[op.dsem] = op.dval
                elif op.signal:
                    ins.then_inc(esem[e], 1)
            for k, v in final.items():
                eng.wait_ge(dsem[k], v)

        @block.tensor
        def _(t):
            run('pe', t)

        @block.scalar
        def _(a):
            run('act', a)

        @block.vector
        def _(v):
            run('dve', v)

        @block.gpsimd
        def _(g):
            run('pool', g)

        @block.sync
        def _(s):
            run('sp', s)


def _consts():
    c = np.zeros((128, 1024), np.float32)
    i = np.arange(128)
    c[:, 0:128] = (i[:, None] == i[None, :])
    c[:, 128:256] = (i[:, None] > i[None, :])
    c[:, 256:384] = (i[:, None] <= i[None, :])
    c[:, 384:512] = 1.0
    r = np.ones(512, np.float32); r[::64] = 0.0
    c[:, 512:1024] = r[None, :]
    return c

V_ADAB, V_NMIX, V_NMLP, V_NFIN, V_LB, V_GN, V_SCW, V_MCW, V_MCB, V_MN = 0, 96, 112, 128, 136, 144, 145, 157, 253, 277


def build():
    nc = bass.Bass("TRN2", target_bir_lowering=False)

    def din(name, shape):
        return nc.dram_tensor(name, list(shape), F32, kind="ExternalInput").ap()

    def dout(name, shape):
        return nc.dram_tensor(name, list(shape), F32, kind="ExternalOutput").ap()

    xp = din("xp", [T, D]); xs = din("xs", [NS, D]); cc = din("cc", [18, D])
    st_hg = din("st_hg", [NS, 4, 128, 128]); st_sc = din("st_sc", [NS, 2, 512])
    st_ssm = din("st_ssm", [NS, 32, 64, 128]); st_mc = din("st_mc", [NS, 3, 3072])
    wt = din("wt", [len(UPIECES), 128, 8 * WCOLS])
    wdt_in = din("wdt", [128, 8 * 32])
    vecs_in = din("vecs_in", [384, 128]); rowvecs = din("rowvecs", [1, 96])
    hexp_in = din("hexp", [128, 193]); consts_in = din("consts", [128, 1024])

    y_p = dout("y_p", [T, D]); y_s = dout("y_s", [NS, D])
    hg_p = dout("hg_p", [4, 128, 128]); hg_s = dout("hg_s", [NS, 4, 128, 128])
    sc_p = dout("sc_p", [2, 512]); sc_s = dout("sc_s", [NS, 2, 512])
    ssm_p = dout("ssm_p", [32, 64, 128]); ssm_s = dout("ssm_s", [NS, 32, 64, 128])
    mc_p = dout("mc_p", [3, 3072]); mc_s = dout("mc_s", [NS, 3, 3072])

    bc_scr = nc.dram_tensor("bc_scr", [NS, 1024], F32, kind="Internal").ap()
    prog = Prog()
    OP = prog.add

    with ExitStack() as es:
        def sb(name, shape, dt=F32):
            return es.enter_context(nc.sbuf_tensor("sb_" + name, list(shape), dt))

        PS = es.enter_context(nc.psum_tensor("ps", [128, 8, 512], F32))
        consts = sb("consts", [128, 1024])
        ident = consts[:, 0:128]
        tri = consts[:, 256:384]
        ones_f = consts[:, 384:512]
        rmask = consts[:, 512:1024]
        ones_r = sb("ones_r", [128, 128], F32R)
        U_r = sb("U_r", [128, 128], F32R)
        vecs = sb("vecs", [128, 384])
        rowv = sb("rowv", [128, 96])
        hexp = sb("hexp", [128, 193])
        cT = sb("cT", [128, 8, 18], F32R)
        modT = sb("modT", [128, 2, 48, 18])
        gsm = sb("gsm", [128, 2, 2, 8, 18])
        lbc = sb("lbc", [128, 2, 4])
        xT = sb("xT", [128, 8, TT])
        hT = sb("hT", [128, 8, TT], F32R)
        rs = sb("rs", [128, TT])
        xsT = sb("xsT", [128, 8, NS])
        hsT = sb("hsT", [128, 8, NS], F32R)
        wsl = sb("wsl", [128, NWSLOT, 8, WCOLS], F32R)
        arena = sb("arena", [128, NPAGE * 512])
        arenaR = nc.alloc_sbuf_tensor_at("sb_arenaR", [128, NPAGE * 512], F32R,
                                         offset=nc._sbuf_addr_for_side(None) - NPAGE * 512 * 4)
        sR = sb("sR", [128, 1344], F32R)
        ctok = arena[0:18, 36 * 512:38 * 512]
        vst = arena[:, 35 * 512:35 * 512 + 384].rearrange("p (a c) -> p a c", c=128)
        S_hg = sb("S_hg", [128, 4, 128])
        ST = sb("ST", [128, 2048])
        ucar = sb("ucar", [128, 4, 2])
        ccar = sb("ccar", [128, 24, 3])
        raw = sb("raw", [128, 2, 516])
        uraw = raw
        decb = sb("decb", [128, 4, 8])
        attm = sb("attm", [64, 4, 64], F32R)
        kwtok = sb("kwtok", [64, 4, 128], F32R)
        scratch = sb("scratch", [128, 1600])
        aexp = sb("aexp", [128, 16])
        Ab = sb("Ab", [128, 32])
        blk = sb("blk", [128, 4, 192])
        craw = raw
        junk = sb("junk", [128, 128])
        sbuf_t = sb("sbuf_t", [128, 768])

        scr_state = {'off': 0}

        def scr(name, ncols):
            o = scr_state['off']
            scr_state['off'] = o + ncols
            assert scr_state['off'] <= 1600, scr_state['off']
            return scratch[:, o:o + ncols], 'scr_' + name

        def pg(p, n=1):
            return arena[:, p * 512:(p + n) * 512]

        def pgR(p, n=1):
            return arenaR[:, p * 512:(p + n) * 512]

        def kA(p, n=1):
            return [('A', q) for q in range(p, p + n)]

        def kP(b, lo=0, hi=512):
            return [('P', b)]

        bank_state = {'next': 0, 'open': [False] * 4}

        def nb():
            b = bank_state['next']
            bank_state['next'] = (b + 1) % 4
            assert not bank_state['open'][b], "psum bank %d still open" % b
            bank_state['open'][b] = True
            return b

        def rel(b):
            bank_state['open'][b] = False

        def col(ap2d, j):
            return ap2d[:, j:j + 1]

        WSCHED = []

        def sched():
            for pc in range(24):
                WSCHED.append(('ada', 0, pc))
            pend_ada = [('ada', 1, pc) for pc in range(24)]
            reg = []
            for ti in range(NTILE):
                for sec in (4, 5, 6, 0, 1, 3, 2):
                    for hf in range(2):
                        reg.append(('wie', sec, hf))
                for pc in range(4):
                    reg.append(('woe', pc))
                for pc in range(16):
                    reg.append(('w1', 0, pc))
                for cg in range(4):
                    for kg in range(4):
                        reg.append(('w2', 0, cg, kg))
                for pc in range(8, 20):
                    reg.append(('wio', pc))
                reg.append(('wdt',))
                for pc in range(0, 8):
                    reg.append(('wio', pc))
                for cg in range(4):
                    for kg in range(2):
                        reg.append(('woo', cg, kg))
                for pc in range(16):
                    reg.append(('w1', 1, pc))
                for cg in range(4):
                    for kg in range(4):
                        reg.append(('w2', 1, cg, kg))
            for spec in reg:
                if pend_ada:
                    WSCHED.append(pend_ada.pop(0))
                WSCHED.append(spec)
        sched()

        def wsrc(spec):
            if spec[0] == 'wdt':
                return wdt_in.rearrange("p (kc c) -> p kc c", c=32), 32
            return wt[UIDX[spec]].rearrange("p (kc c) -> p kc c", c=WCOLS), WCOLS

        wstate = {'issued': 0, 'used': 0}

        def w_issue_upto(n):
            while wstate['issued'] < min(n, len(WSCHED)):
                i = wstate['issued']
                src, ncol = wsrc(WSCHED[i])
                s = i % NWSLOT
                OP('pool', (lambda e, s=s, src=src, ncol=ncol:
                            e.dma_start(out=wsl[:, s, :, 0:ncol], in_=src)),
                   w=[('W', s)], dma=True)
                wstate['issued'] += 1

        ada_pending = []

        def wnext(spec):
            if spec[0] != 'ada' and ada_pending:
                ada_piece(*ada_pending.pop(0))
            i = wstate['used']
            assert WSCHED[i] == spec, (i, WSCHED[i], spec)
            w_issue_upto(i + NWSLOT)
            wstate['used'] += 1
            s = i % NWSLOT
            return wsl[:, s], ('W', s)

        OP('sp', lambda e: e.dma_start(out=consts[:, :], in_=consts_in), w=['consts'], dma=True)
        OP('sp', lambda e: e.dma_start(out=vst[:, :, :], in_=vecs_in.rearrange("(a p) c -> p a c", p=128)),
           w=[('A', 35)], dma=True)
        OP('sp', lambda e: e.dma_start(out=rowv[:, :], in_=rowvecs.partition_broadcast(128)), w=['rowv'], dma=True)
        OP('sp', lambda e: e.dma_start(out=hexp[:, :], in_=hexp_in), w=['hexp'], dma=True)
        OP('sp', lambda e: e.dma_start(out=ctok[:, :], in_=cc), w=[('A', 36), ('A', 37)], dma=True)
        w_issue_upto(NWSLOT)
        OP('dve', lambda e: e.memset(S_hg[:, :, :], 0.0), w=['S_hg'])
        OP('dve', lambda e: e.memset(ST[:, :], 0.0), w=['ST'])
        OP('dve', lambda e: e.memset(ucar[:, :, :], 0.0), w=['ucar'])
        OP('dve', lambda e: e.memset(ccar[:, :, :], 0.0), w=['ccar'])
        OP('act', lambda e: e.activation(out=U_r[:, :], in_=consts[:, 128:256], func=AF.Copy),
           r=['consts'], w=['U_r'])
        OP('act', lambda e: e.activation(out=ones_r[:, :], in_=consts[:, 384:512], func=AF.Copy),
           r=['consts'], w=['ones'])
        for a in range(3):
            b = nb()
            OP('pe', lambda e, a=a, b=b: e.transpose(out=PS[:, b, 0:128], in_=vst[:, a, :], identity=ident),
               r=[('A', 35), 'consts'], w=kP(b))
            OP('act', lambda e, a=a, b=b: e.activation(out=vecs[:, a * 128:(a + 1) * 128], in_=PS[:, b, 0:128],
                                                       func=AF.Copy), r=kP(b), w=['vecs'])
            rel(b)
        OP('dve', lambda e: e.tensor_tensor(out=lbc[:, 0, :], in0=vecs[:, V_LB + 4:V_LB + 8],
                                            in1=vecs[:, V_LB:V_LB + 4], op=ALU.subtract), r=['vecs'], w=['lbc'])
        OP('act', lambda e: e.activation(out=lbc[:, 0, :], in_=lbc[:, 0, :], func=AF.Sigmoid), r=['lbc'], w=['lbc'])
        OP('dve', lambda e: e.tensor_scalar(out=lbc[:, 1, :], in0=lbc[:, 0, :], scalar1=-1.0, scalar2=1.0,
                                            op0=ALU.mult, op1=ALU.add), r=['lbc'], w=['lbc'])
        OP('act', lambda e: e.activation(out=Ab[:, :], in_=rowv[:, 32:64], func=AF.Exp), r=['rowv'], w=['Ab'])
        OP('dve', lambda e: e.tensor_scalar(out=Ab[:, :], in0=Ab[:, :], scalar1=-1.0, scalar2=None, op0=ALU.mult),
           r=['Ab'], w=['Ab'])
        OP('act', lambda e: e.activation(out=aexp[:, :], in_=hexp[:, 0:16], func=AF.Exp), r=['hexp'], w=['aexp'])
        OP('dve', lambda e: e.tensor_scalar(out=aexp[:, :], in0=aexp[:, :], scalar1=-1.0, scalar2=None, op0=ALU.mult),
           r=['aexp'], w=['aexp'])
        OP('act', lambda e: e.activation(out=ctok[:, :], in_=ctok[:, :], func=AF.Silu), r=[('A', 36), ('A', 37)], w=[('A', 36), ('A', 37)])
        b = nb()
        for c in range(8):
            OP('pe', lambda e, c=c, b=b: e.transpose(out=PS[:, b, c * 32:c * 32 + 18], in_=ctok[0:18, c * 128:(c + 1) * 128],
                                                     identity=consts[0:18, 0:18]),
               r=[('A', 36), ('A', 37), 'consts'], w=kP(b, c * 32, c * 32 + 32))
        OP('act', lambda e, b=b: e.activation(out=cT[:, :, :],
                                              in_=PS[:, b, 0:256].rearrange("p (c n) -> p c n", n=32)[:, :, 0:18],
                                              func=AF.Copy), r=kP(b, 0, 256), w=['cT'])
        rel(b)
        def load_x_dma(ti):
            t0 = ti * TT
            stage = arena[:, 0:4096].rearrange("p (b d) -> p b d", d=D)
            OP('sp', lambda e, t0=t0, stage=stage: e.dma_start(out=stage, in_=xp[t0:t0 + TT, :].rearrange("(b p) d -> p b d", p=128)),
               w=kA(0, 8), dma=True)

        def load_x(ti, dma=True):
            t0 = ti * TT
            with_s = (ti == 0)
            stage = arena[:, 0:4096].rearrange("p (b d) -> p b d", d=D)
            if dma:
                load_x_dma(ti)
            for c in range(8):
                b = nb()
                def f(e, c=c, b=b, stage=stage):
                    for bk in range(4):
                        ins = e.transpose(out=PS[:, b, bk * 128:(bk + 1) * 128], in_=stage[:, bk, c * 128:(c + 1) * 128], identity=ident)
                    return ins
                OP('pe', f, r=kA(0, 8) + ['consts'], w=kP(b))
                if c % 2 == 0:
                    OP('act', lambda e, c=c, b=b: e.activation(out=xT[:, c, :], in_=PS[:, b, :], func=AF.Copy), r=kP(b), w=[('xT', c)])
                else:
                    OP('dve', lambda e, c=c, b=b: e.tensor_copy(out=xT[:, c, :], in_=PS[:, b, :]), r=kP(b), w=[('xT', c)])
                rel(b)

        def ada_gsm(l):
            for m, (s0, vbase) in enumerate(((8, V_NMIX), (32, V_NMLP))):
                OP('dve', lambda e, l=l, m=m, s0=s0, vbase=vbase: e.scalar_tensor_tensor(
                    out=gsm[:, l, m, :, :], in0=modT[:, l, s0:s0 + 8, :], scalar=1.0,
                    in1=vecs[:, vbase + l * 8:vbase + l * 8 + 8].unsqueeze(2).broadcast_to([128, 8, 18]),
                    op0=ALU.add, op1=ALU.mult),
                    r=[('mod', l, s0 // 8), 'vecs'], w=[('gsm', l, m)])

        ada_late = []

        def ada_piece(l, pc):
            wp, wk = wnext(('ada', l, pc))
            bq = nb()
            def f(e, wp=wp, bq=bq):
                for kc in range(8):
                    ins = e.matmul(PS[0:18, bq, 0:256], lhsT=cT[:, kc, :], rhs=wp[:, kc, :], start=(kc == 0), stop=(kc == 7))
                return ins
            OP('pe', f, r=[wk, 'cT'], w=kP(bq))
            tp = pc % 2
            tkey = ['DTS', 'AES'][tp]
            tmp = sbuf_t[0:18, tp * 256:(tp + 1) * 256]
            OP('act', lambda e, bq=bq, tmp=tmp: e.activation(out=tmp, in_=PS[0:18, bq, 0:256], func=AF.Copy),
               r=kP(bq), w=[tkey])
            rel(bq)

            def late(l=l, pc=pc, tmp=tmp, tkey=tkey):
                b2 = nb()
                def f2(e, b2=b2, tmp=tmp):
                    for j in range(2):
                        ins = e.transpose(out=PS[:, b2, j * 32:j * 32 + 18], in_=tmp[:, j * 128:(j + 1) * 128],
                                          identity=consts[0:18, 0:18])
                    return ins
                OP('pe', f2, r=[tkey, 'consts'], w=kP(b2))
                for j in range(2):
                    ch = pc * 2 + j
                    OP('act', lambda e, l=l, ch=ch, b2=b2, j=j: e.activation(
                        out=modT[:, l, ch, :], in_=PS[:, b2, j * 32:j * 32 + 18], func=AF.Identity,
                        bias=vecs[:, V_ADAB + l * 48 + ch:V_ADAB + l * 48 + ch + 1]),
                        r=kP(b2) + ['vecs'], w=[('mod', l, ch // 8)])
                rel(b2)
            if ada_late:
                ada_late.pop(0)()
            ada_late.append(late)
            if pc == 23:
                while ada_late:
                    ada_late.pop(0)()
                ada_gsm(l)

        load_x(0)
        for pc in range(24):
            ada_piece(0, pc)
        ada_pending.extend((1, pc) for pc in range(24))
        prog.mark('prologue_done')
        def norm_prompt(gkey, gs_col, sh_col, shkey, out_fn=None, okey=None):
            sq = arenaR[:, 0:4096]
            OP('act', lambda e: e.activation(out=sq[:, 0:2048], in_=xT[:, 0:4, :].rearrange("p c n -> p (c n)"), func=AF.Square),
               r=[('xT', c) for c in range(0, 4)], w=kA(0, 4))
            OP('dve', lambda e: e.tensor_tensor(out=sq[:, 2048:4096], in0=xT[:, 4:8, :].rearrange("p c n -> p (c n)"),
                                                in1=xT[:, 4:8, :].rearrange("p c n -> p (c n)"), op=ALU.mult),
               r=[('xT', c) for c in range(4, 8)], w=kA(4, 4))
            b = nb()
            def f(e, b=b):
                for c in range(8):
                    ins = e.matmul(PS[:, b, :], lhsT=ones_r[:, :], rhs=sq[:, c * 512:(c + 1) * 512],
                                   start=(c == 0), stop=(c == 7))
                return ins
            OP('pe', f, r=kA(0, 8) + ['ones'], w=kP(b))
            OP('act', lambda e, b=b: e.activation(out=rs[:, :], in_=PS[:, b, :], func=AF.Sqrt, bias=EPS, scale=1.0 / D),
               r=kP(b), w=['rs'])
            rel(b)
            OP('dve', lambda e: e.reciprocal(out=rs[:, :], in_=rs[:, :]), r=['rs'], w=['rs'])
            for c in range(8):
                tp = 8 + (c % 2)
                OP('dve', lambda e, c=c, tp=tp: e.tensor_tensor(out=pg(tp), in0=xT[:, c, :], in1=rs[:, :], op=ALU.mult),
                   r=[('xT', c), 'rs'], w=kA(tp))
                oap = hT[:, c, :] if out_fn is None else out_fn(c)
                okk = [('hT', c)] if okey is None else okey(c)
                OP('act', lambda e, c=c, tp=tp, oap=oap: e.activation(out=oap, in_=pg(tp), func=AF.Identity,
                                                                      bias=sh_col(c), scale=gs_col(c)),
                   r=kA(tp) + [gkey, shkey], w=okk)

        k_ns_sq = 'sR_ns_sq'
        ns_rs, k_ns_rs = scr('ns_rs', 16)
        ns_tmp, k_ns_tmp = scr('ns_tmp', 128)

        def norm_sample(gs_ap, sh_ap, gkey, shkey, out_ap, out_key):
            sqs = sR[:, 0:128]
            OP('act', lambda e: e.activation(out=sqs, in_=xsT[:, :, :].rearrange("p c n -> p (c n)"), func=AF.Square),
               r=['xsT'], w=[k_ns_sq])
            def f(e):
                for c in range(8):
                    ins = e.matmul(PS[:, 4, 480:496], lhsT=ones_r[:, :], rhs=sqs[:, c * 16:(c + 1) * 16],
                                   start=(c == 0), stop=(c == 7))
                return ins
            OP('pe', f, r=[k_ns_sq, 'ones'], w=kP(4, 448, 512))
            OP('act', lambda e: e.activation(out=ns_rs, in_=PS[:, 4, 480:496], func=AF.Sqrt, bias=EPS, scale=1.0 / D),
               r=kP(4, 448, 512), w=[k_ns_rs])
            OP('dve', lambda e: e.reciprocal(out=ns_rs, in_=ns_rs), r=[k_ns_rs], w=[k_ns_rs])
            tmp = ns_tmp.rearrange("p (c n) -> p c n", n=16)
            OP('dve', lambda e: e.tensor_tensor(out=tmp, in0=xsT[:, :, :], in1=ns_rs.unsqueeze(1).broadcast_to([128, 8, 16]),
                                                op=ALU.mult), r=['xsT', k_ns_rs], w=[k_ns_tmp])
            if sh_ap is None:
                OP('dve', lambda e: e.tensor_tensor(out=out_ap, in0=tmp, in1=gs_ap, op=ALU.mult), r=[k_ns_tmp, gkey], w=[out_key])
            else:
                OP('dve', lambda e: e.tensor_tensor(out=tmp, in0=tmp, in1=gs_ap, op=ALU.mult), r=[k_ns_tmp, gkey], w=[k_ns_tmp])
                OP('dve', lambda e: e.tensor_tensor(out=out_ap, in0=tmp, in1=sh_ap, op=ALU.add), r=[k_ns_tmp, shkey], w=[out_key])

        sbuf_s = sb("sbuf_s", [128, 1344])
        def SV(off, nch):
            return sbuf_s[:, off:off + nch * 16].rearrange("p (c n) -> p c n", n=16)
        QS = SV(0, 4); FS = SV(64, 4); VS = SV(128, 4); GOS = SV(192, 4); BGS = SV(256, 4); CGS = SV(320, 4)
        USs = SV(384, 4)
        OTS = sR[:, 128:256].rearrange("p (c n) -> p c n", n=16)
        H1S = sR[:, 256:768].rearrange("p (c n) -> p c n", n=16)
        ZS = SV(448, 16); XBS = SV(704, 24); YS = SV(1088, 16)
        XCS_t = sb("XCS_t", [128, 24 * 16])
        XCS = XCS_t[:, :].rearrange("p (c n) -> p c n", n=16)
        YNS = sR[:, 768:1024].rearrange("p (c n) -> p c n", n=16)
        DTS = sbuf_t[:, 0:256].rearrange("p (c n) -> p c n", n=16)
        AES = sbuf_t[:, 256:512].rearrange("p (c n) -> p c n", n=16)
        DXS = sbuf_t[:, 512:768].rearrange("p (c n) -> p c n", n=16)
        tok_s = sb("tok_s", [48, 1536])
        mcs = sb("mcs", [128, 24, 48])
        scs = sb("scs", [128, 4, 32])

        def sslot(i):
            return ((4 + i % 4) << 10) + ((i // 4) % 14) * 32

        def SPa(sl, n):
            return PS[:, sl >> 10, (sl & 1023):(sl & 1023) + n]

        def kSl(sl):
            return kP(sl >> 10)

        scount = {'n': 0}

        def s_psum():
            i = scount['n']; scount['n'] += 1
            return sslot(i)

        def proj_fm(wp, wk, ncol, rhs_fn, rkeys, N, evac, nk=8):
            for j in range(ncol // 128):
                b = nb()
                def f(e, j=j, b=b):
                    for kc in range(nk):
                        ins = e.matmul(PS[:, b, 0:N], lhsT=wp[:, kc, j * 128:(j + 1) * 128], rhs=rhs_fn(kc),
                                       start=(kc == 0), stop=(kc == nk - 1))
                    return ins
                OP('pe', f, r=[wk] + list(rkeys), w=kP(b))
                evac(j, b)
                rel(b)

        def proj_fm_s(wp, wk, ncol, rhs_fn, rkeys, evac, nk=8):
            for j in range(ncol // 128):
                sl = s_psum()
                def f(e, j=j, sl=sl):
                    for kc in range(nk):
                        ins = e.matmul(SPa(sl, NS), lhsT=wp[:, kc, j * 128:(j + 1) * 128], rhs=rhs_fn(kc),
                                       start=(kc == 0), stop=(kc == nk - 1))
                    return ins
                OP('pe', f, r=[wk] + list(rkeys), w=kSl(sl))
                evac(j, sl)

        hkeys = [('hT', c) for c in range(8)]
        hs_rhs = lambda kc: hsT[:, kc, :]
        h_rhs = lambda kc: hT[:, kc, :]

        def bc3(ap2, n):
            return ap2.unsqueeze(2).broadcast_to([ap2.shape[0], ap2.shape[1], n])

        def bm3(ap2, m):
            return ap2.unsqueeze(1).broadcast_to([ap2.shape[0], m, ap2.shape[1]])

        def resid_evac(l, mchunk):
            def ev(d, b):
                OP('dve', lambda e, d=d, b=b: e.scalar_tensor_tensor(
                    out=xT[:, d, :], in0=PS[:, b, :], scalar=modT[:, l, mchunk * 8 + d, 0:1], in1=xT[:, d, :],
                    op0=ALU.mult, op1=ALU.add),
                    r=kP(b) + [('xT', d), ('mod', l, mchunk)], w=[('xT', d)])
            return ev

        res_tmp, k_res = scr('res', 16)
        relu_tmp, k_relu = scr('relu', 32)

        def resid_evac_s(l, mchunk, d, src_ap, src_keys):
            tmp = res_tmp
            OP('dve', lambda e: e.tensor_tensor(out=tmp, in0=src_ap, in1=modT[:, l, mchunk * 8 + d, 1:17], op=ALU.mult),
               r=list(src_keys) + [('mod', l, mchunk)], w=[k_res])
            OP('dve', lambda e: e.tensor_tensor(out=xsT[:, d, :], in0=xsT[:, d, :], in1=tmp, op=ALU.add),
               r=[k_res, 'xsT'], w=['xsT'])

        def mlp(l, with_s):
            for pc in range(16):
                wp, wk = wnext(('w1', l, pc))
                def ev(j, b, pc=pc):
                    hc = pc * 2 + j
                    tp = 32 + (hc % 4)
                    OP('act', lambda e: e.activation(out=pg(tp), in_=PS[:, b, :], func=AF.Relu), r=kP(b), w=kA(tp))
                    OP('dve', lambda e: e.tensor_tensor(out=pgR(hc), in0=pg(tp), in1=pg(tp), op=ALU.mult),
                       r=kA(tp), w=kA(hc))
                proj_fm(wp, wk, 256, h_rhs, hkeys, TT, ev)
                if with_s:
                    def evs(j, sl, pc=pc):
                        hc = pc * 2 + j
                        tmp = relu_tmp[:, (hc % 2) * 16:(hc % 2) * 16 + 16]
                        OP('act', lambda e: e.activation(out=tmp, in_=SPa(sl, NS), func=AF.Relu),
                           r=kSl(sl), w=[(k_relu, hc % 2)])
                        OP('dve', lambda e: e.tensor_tensor(out=H1S[:, hc, :], in0=tmp, in1=tmp, op=ALU.mult),
                           r=[(k_relu, hc % 2)], w=['H1S'])
                    proj_fm_s(wp, wk, 256, hs_rhs, ['hsT'], evs)
            ev = resid_evac(l, 5)
            for cg in range(4):
                banks = [nb(), nb()]
                for kg in range(4):
                    wp, wk = wnext(('w2', l, cg, kg))
                    for j in range(2):
                        def f(e, wp=wp, j=j, kg=kg, b=banks[j]):
                            for kc in range(8):
                                ins = e.matmul(PS[:, b, :], lhsT=wp[:, kc, j * 128:(j + 1) * 128],
                                               rhs=pgR(kg * 8 + kc),
                                               start=(kg == 0 and kc == 0), stop=(kg == 3 and kc == 7))
                            return ins
                        OP('pe', f, r=[wk] + kA(kg * 8, 8), w=kP(banks[j]))
                    if with_s:
                        for j in range(2):
                            def f(e, wp=wp, j=j, kg=kg):
                                for kc in range(8):
                                    ins = e.matmul(PS[:, 4 + j, 448:464], lhsT=wp[:, kc, j * 128:(j + 1) * 128],
                                                   rhs=H1S[:, kg * 8 + kc, :],
                                                   start=(kg == 0 and kc == 0), stop=(kg == 3 and kc == 7))
                                return ins
                            OP('pe', f, r=[wk, 'H1S'], w=kP(4 + j, 448, 512))
                for j in range(2):
                    ev(cg * 2 + j, banks[j])
                    rel(banks[j])
                    if with_s:
                        resid_evac_s(l, 5, cg * 2 + j, PS[:, 4 + j, 448:464], kP(4 + j, 448, 512))

        P_BG, P_CG, P_CT, P_OT, P_Q, P_F, P_GO, P_VT, P_T1, P_BP, P_KW, P_OA = 0, 4, 8, 10, 18, 22, 26, 30, 4, 6, 0, 10

        def even_layer(ti, with_s):
            def emit_piece(sec, hf):
                wp, wk = wnext(('wie', sec, hf))
                if sec == 2:
                    for c8 in range(8):
                        b = nb()
                        def f(e, c8=c8, b=b, wp=wp):
                            for kc in range(8):
                                ins = e.matmul(PS[0:64, b, 0:256], lhsT=hT[:, kc, c8 * 64:(c8 + 1) * 64], rhs=wp[:, kc, :],
                                               start=(kc == 0), stop=(kc == 7))
                            return ins
                        OP('pe', f, r=[wk] + hkeys, w=kP(b))
                        dst = pgR(P_VT + c8)[0:64, hf * 256:(hf + 1) * 256]
                        if c8 % 2 == 0:
                            OP('act', lambda e, b=b, dst=dst: e.activation(out=dst, in_=PS[0:64, b, 0:256], func=AF.Copy),
                               r=kP(b), w=kA(P_VT + c8))
                        else:
                            OP('dve', lambda e, b=b, dst=dst: e.tensor_copy(out=dst, in_=PS[0:64, b, 0:256]),
                               r=kP(b), w=kA(P_VT + c8))
                        rel(b)
                else:
                    def ev(j, b, sec=sec, hf=hf):
                        ch = hf * 2 + j
                        if sec == 4:
                            OP('act', lambda e: e.activation(out=pg(P_BG + ch), in_=PS[:, b, :], func=AF.Copy),
                               r=kP(b), w=kA(P_BG + ch))
                        elif sec == 5:
                            OP('act', lambda e: e.activation(out=pg(P_CG + ch), in_=PS[:, b, :], func=AF.Copy),
                               r=kP(b), w=kA(P_CG + ch))
                        elif sec == 6:
                            s = ch % 2
                            OP('dve', lambda e: e.tensor_copy(out=uraw[:, s, 0:2], in_=ucar[:, ch, :]),
                               r=[('ucar', ch)], w=[('uraw', s)])
                            OP('dve', lambda e: e.tensor_tensor(out=uraw[:, s, 2:514], in0=PS[:, b, :], in1=pg(P_CG + ch),
                                                                op=ALU.mult), r=kP(b) + kA(P_CG + ch), w=[('uraw', s)])
                            OP('dve', lambda e: e.tensor_copy(out=ucar[:, ch, :], in_=uraw[:, s, 512:514]),
                               r=[('uraw', s)], w=[('ucar', ch)])
                            tp = P_CT + s
                            wc = lambda k: vecs[:, V_SCW + k * 4 + ch:V_SCW + k * 4 + ch + 1]
                            OP('act', lambda e: e.activation(out=pg(tp), in_=uraw[:, s, 0:512], func=AF.Identity, scale=wc(0)),
                               r=[('uraw', s), 'vecs'], w=kA(tp))
                            for k in (1, 2):
                                OP('dve', lambda e, k=k: e.scalar_tensor_tensor(out=pg(tp), in0=uraw[:, s, k:k + 512], scalar=wc(k),
                                                                               in1=pg(tp), op0=ALU.mult, op1=ALU.add),
                                   r=[('uraw', s), 'vecs'] + kA(tp), w=kA(tp))
                            OP('dve', lambda e: e.tensor_tensor(out=pgR(P_OT + 4 + ch), in0=pg(tp), in1=pg(P_BG + ch),
                                                                op=ALU.mult), r=kA(tp) + kA(P_BG + ch), w=kA(P_OT + 4 + ch))
                        elif sec == 0:
                            OP('act', lambda e: e.activation(out=pg(P_Q + ch), in_=PS[:, b, :], func=AF.Copy),
                               r=kP(b), w=kA(P_Q + ch))
                        elif sec == 1:
                            OP('act', lambda e: e.activation(out=pg(P_F + ch), in_=PS[:, b, :], func=AF.Sigmoid),
                               r=kP(b), w=kA(P_F + ch))
                        elif sec == 3:
                            OP('act', lambda e: e.activation(out=pg(P_GO + ch), in_=PS[:, b, :], func=AF.Silu),
                               r=kP(b), w=kA(P_GO + ch))
                    proj_fm(wp, wk, 256, h_rhs, hkeys, TT, ev)
                if with_s:
                    def evs(j, sl, sec=sec, hf=hf):
                        ch = hf * 2 + j
                        src = SPa(sl, NS)
                        kk = kSl(sl)
                        if sec == 6:
                            OP('dve', lambda e: e.tensor_tensor(out=USs[:, ch, :], in0=src, in1=CGS[:, ch, :], op=ALU.mult),
                               r=kk + ['CGS'], w=['USs'])
                        else:
                            dst, key, fn = {4: (BGS, 'BGS', AF.Copy), 5: (CGS, 'CGS', AF.Copy), 0: (QS, 'QS', AF.Copy),
                                            1: (FS, 'FS', AF.Sigmoid), 3: (GOS, 'GOS', AF.Silu), 2: (VS, 'VS', AF.Copy)}[sec]
                            OP('act', lambda e: e.activation(out=dst[:, ch, :], in_=src, func=fn), r=kk, w=[key])
                    proj_fm_s(wp, wk, 256, hs_rhs, ['hsT'], evs)

            def prep_steps(h):
                Fp, Qp, T1, Bp, KW = pg(P_F + h), pg(P_Q + h), pg(P_VT + h), pg(4 + h), pg(P_KW + h)
                kF, kQ, kT, kB, kW_ = kA(P_F + h), kA(P_Q + h), kA(P_VT + h), kA(4 + h), kA(P_KW + h)
                return [
                    lambda: OP('dve', lambda e: e.tensor_scalar(out=Fp, in0=Fp, scalar1=lbc[:, 1, h:h + 1], scalar2=lbc[:, 0, h:h + 1],
                                                                op0=ALU.mult, op1=ALU.add), r=kF + ['lbc'], w=kF),
                    lambda: OP('act', lambda e: e.activation(out=T1, in_=Fp, func=AF.Ln), r=kF, w=kT),
                    lambda: OP('dve', lambda e: e.tensor_tensor_scan(out=Bp, data0=rmask, data1=T1, initial=0.0, op0=ALU.mult, op1=ALU.add),
                               r=kT + ['consts'], w=kB),
                    lambda: OP('act', lambda e: e.activation(out=T1, in_=Bp, func=AF.Exp), r=kB, w=kT),
                    lambda: OP('dve', lambda e: e.tensor_tensor(out=pgR(P_Q + h), in0=Qp, in1=T1, op=ALU.mult), r=kQ + kT, w=kQ),
                    lambda: OP('act', lambda e: e.activation(out=T1, in_=Bp, func=AF.Exp, scale=-1.0), r=kB, w=kT),
                    lambda: OP('dve', lambda e: e.tensor_scalar(out=Fp, in0=Fp, scalar1=-1.0, scalar2=1.0, op0=ALU.mult, op1=ALU.add),
                               r=kF, w=kF),
                    lambda: OP('dve', lambda e: e.tensor_tensor(out=pgR(P_F + h), in0=Fp, in1=T1, op=ALU.mult), r=kF + kT, w=kF),
                    lambda: OP('act', lambda e: e.activation(out=decb[:, h, :],
                                                             in_=Bp.rearrange("p (c t) -> p c t", t=64)[:, :, 63], func=AF.Exp),
                               r=kB, w=[('decb', h)]),
                    lambda: OP('dve', lambda e: e.tensor_tensor(out=KW.rearrange("p (c t) -> p c t", t=64),
                                                                in0=Fp.rearrange("p (c t) -> p c t", t=64),
                                                                in1=bc3(decb[:, h, :], 64), op=ALU.mult),
                               r=kF + [('decb', h)], w=kW_),
                ]

            def prep_all():
                steps = [prep_steps(h) for h in range(4)]
                for k in range(len(steps[0])):
                    for h in range(4):
                        steps[h][k]()

            for sec in (4, 5, 6, 0, 1):
                for hf in range(2):
                    emit_piece(sec, hf)
            prep_all()
            for sec in (3, 2):
                for hf in range(2):
                    emit_piece(sec, hf)

            prog.mark('t%d_even_inproj' % ti)
            for c8 in range(8):
                cs = slice(c8 * 64, (c8 + 1) * 64)
                for h in range(4):
                    OP('pe', lambda e, h=h, cs=cs: e.matmul(PS[0:64, 4 + h, 0:64], lhsT=pgR(P_F + h)[:, cs],
                                                             rhs=pgR(P_Q + h)[:, cs], start=True, stop=True),
                       r=kA(P_F + h) + kA(P_Q + h), w=kP(4 + h))
                    OP('dve', lambda e, h=h: e.tensor_tensor(out=attm[:, h, :], in0=PS[0:64, 4 + h, 0:64],
                                                             in1=consts[0:64, 256:320], op=ALU.mult),
                       r=kP(4 + h) + ['consts'], w=[('attm', h)])
                    OP('pe', lambda e, h=h, cs=cs: e.transpose(out=PS[0:64, h, 0:128], in_=pg(P_KW + h)[:, cs],
                                                                identity=ident),
                       r=kA(P_KW + h) + ['consts'], w=kP(h))
                    OP('act', lambda e, h=h: e.activation(out=kwtok[:, h, :], in_=PS[0:64, h, 0:128], func=AF.Copy),
                       r=kP(h), w=[('kwtok', h)])
                for h in range(4):
                    vt = pgR(P_VT + c8)[0:64, h * 128:(h + 1) * 128]
                    def f(e, h=h, cs=cs, vt=vt):
                        e.matmul(PS[:, 4 + h, 64:128], lhsT=vt, rhs=attm[:, h, :], start=True, stop=False)
                        return e.matmul(PS[:, 4 + h, 64:128], lhsT=S_hg[:, h, :], rhs=pg(P_Q + h)[:, cs],
                                        start=False, stop=True)
                    OP('pe', f, r=kA(P_VT + c8) + [('attm', h), ('S_hg', h)] + kA(P_Q + h), w=kP(4 + h))
                    OP('pe', lambda e, h=h, vt=vt: e.matmul(PS[:, h, 128:256], lhsT=kwtok[:, h, :], rhs=vt,
                                                             start=True, stop=True),
                       r=[('kwtok', h)] + kA(P_VT + c8), w=kP(h))
                    OP('dve', lambda e, h=h, c8=c8: e.scalar_tensor_tensor(out=S_hg[:, h, :], in0=S_hg[:, h, :],
                                                                           scalar=decb[:, h, c8:c8 + 1],
                                                                           in1=PS[:, h, 128:256],
                                                                           op0=ALU.mult, op1=ALU.add),
                       r=[('S_hg', h), ('decb', h)] + kP(h), w=[('S_hg', h)])
                    OP('act', lambda e, h=h, cs=cs: e.activation(out=pg(P_OA + h)[:, cs], in_=PS[:, 4 + h, 64:128],
                                                                 func=AF.Copy),
                       r=kP(4 + h), w=kA(P_OA + h))
            for h in range(4):
                T1, Bp, OA = pgR(P_T1 + h % 2), pg(P_BP + h % 2), pg(P_OA + h)
                kT, kB, kO = kA(P_T1 + h % 2), kA(P_BP + h % 2), kA(P_OA + h)
                OP('act', lambda e, T1=T1, OA=OA: e.activation(out=T1, in_=OA, func=AF.Square), r=kO, w=kT)
                b = nb()
                OP('pe', lambda e, b=b, T1=T1: e.matmul(PS[:, b, :], lhsT=ones_r[:, :], rhs=T1, start=True, stop=True),
                   r=kT + ['ones'], w=kP(b))
                OP('act', lambda e, b=b, Bp=Bp: e.activation(out=Bp, in_=PS[:, b, :], func=AF.Sqrt, bias=EPS, scale=1.0 / 128),
                   r=kP(b), w=kB)
                rel(b)
                OP('dve', lambda e, Bp=Bp: e.reciprocal(out=Bp, in_=Bp), r=kB, w=kB)
                OP('dve', lambda e, OA=OA, Bp=Bp: e.scalar_tensor_tensor(out=OA, in0=OA, scalar=vecs[:, V_GN:V_GN + 1], in1=Bp,
                                                                         op0=ALU.mult, op1=ALU.mult),
                   r=kO + kB + ['vecs'], w=kO)
                OP('dve', lambda e, h=h, OA=OA: e.tensor_tensor(out=pgR(P_OT + h), in0=OA, in1=pg(P_GO + h),
                                                                op=ALU.mult), r=kO + kA(P_GO + h), w=kA(P_OT + h))
            prog.mark('t%d_even_hgrn' % ti)
            if with_s:
                even_sample()
            prog.mark('t%d_even_sample' % ti)
            ev = resid_evac(0, 2)
            for pc in range(4):
                wp, wk = wnext(('woe', pc))
                proj_fm(wp, wk, 256, lambda kc: pgR(P_OT + kc), kA(P_OT, 8), TT,
                        lambda j, b, pc=pc: ev(pc * 2 + j, b))
                if with_s:
                    proj_fm_s(wp, wk, 256, lambda kc: OTS[:, kc, :], ['OTS'],
                              lambda j, sl, pc=pc: resid_evac_s(0, 2, pc * 2 + j, SPa(sl, NS), kSl(sl)))
            if ti == NTILE - 1:
                OP('sp', lambda e: e.dma_start(out=hg_p.rearrange("h k v -> k h v"), in_=S_hg[:, :, :]),
                   r=[('S_hg', h) for h in range(4)], dma=True)
                b = nb()
                for c in range(4):
                    OP('pe', lambda e, c=c, b=b: e.transpose(out=PS[0:2, b, c * 128:(c + 1) * 128], in_=ucar[:, c, :], identity=ident),
                       r=[('ucar', c), 'consts'], w=kP(b, c * 128, c * 128 + 128))
                OP('act', lambda e, b=b: e.activation(out=tok_s[0:2, 0:512], in_=PS[0:2, b, :], func=AF.Copy),
                   r=kP(b), w=['tok_s'])
                rel(b)
                OP('sp', lambda e: e.dma_start(out=sc_p, in_=tok_s[0:2, 0:512]), r=['tok_s'], dma=True)

        def even_sample():
            SH = arena[:, 18 * 512:20 * 512].rearrange("p (s h v) -> p s h v", s=2, h=4)
            vmk = arena[0:16, 20 * 512:22 * 512].rearrange("p (s v) -> p s v", s=2)
            OP('dve', lambda e: e.tensor_tensor(out=FS, in0=FS, in1=bc3(lbc[:, 1, :], NS), op=ALU.mult), r=['FS', 'lbc'], w=['FS'])
            OP('dve', lambda e: e.tensor_tensor(out=FS, in0=FS, in1=bc3(lbc[:, 0, :], NS), op=ALU.add), r=['FS', 'lbc'], w=['FS'])
            kkf, k_kk = scr('kk', 64)
            KKf = kkf.rearrange("p (c n) -> p c n", n=16)
            OP('dve', lambda e: e.tensor_scalar(out=KKf, in0=FS, scalar1=-1.0, scalar2=1.0, op0=ALU.mult, op1=ALU.add),
               r=['FS'], w=[k_kk])
            for h in range(4):
                OP('pe', lambda e, h=h: e.transpose(out=PS[0:16, 6, h * 128:(h + 1) * 128], in_=KKf[:, h, :], identity=ident),
                   r=[k_kk, 'consts'], w=kP(6, h * 128, h * 128 + 128))
                OP('pe', lambda e, h=h: e.transpose(out=PS[0:16, 7, h * 128:(h + 1) * 128], in_=VS[:, h, :], identity=ident),
                   r=['VS', 'consts'], w=kP(7, h * 128, h * 128 + 128))
            OP('act', lambda e: e.activation(out=tok_s[0:16, 0:512], in_=PS[0:16, 6, :], func=AF.Copy), r=kP(6), w=['tok_s'])
            OP('act', lambda e: e.activation(out=tok_s[0:16, 512:1024], in_=PS[0:16, 7, :], func=AF.Copy), r=kP(7), w=['tok_s'])
            OP('sp', lambda e: e.dma_start(out=tok_s[0:32, 1024:1536], in_=st_sc.rearrange("b k c -> (b k) c")),
               w=['tok_s'], dma=True)
            for c in range(4):
                sl = s_psum()
                OP('pe', lambda e, c=c, sl=sl: e.transpose(out=SPa(sl, 32), in_=tok_s[0:32, 1024 + c * 128:1152 + c * 128],
                                                           identity=consts[0:32, 0:32]),
                   r=['tok_s', 'consts'], w=kSl(sl))
                OP('act', lambda e, c=c, sl=sl: e.activation(out=scs[:, c, :], in_=SPa(sl, 32), func=AF.Copy),
                   r=kSl(sl), w=['scs'])
            sck = lambda k: scs[:, :, :].rearrange("p c (b k) -> p c b k", k=2)[:, :, :, k]
            wb = lambda k: bc3(vecs[:, V_SCW + k * 4:V_SCW + k * 4 + 4], NS)
            t1f, k_t1 = scr('ect1', 64)
            t2f, k_t2 = scr('ect2', 64)
            t1 = t1f.rearrange("p (c n) -> p c n", n=16)
            t2 = t2f.rearrange("p (c n) -> p c n", n=16)
            OP('dve', lambda e: e.tensor_tensor(out=t1, in0=sck(0), in1=wb(0), op=ALU.mult), r=['scs', 'vecs'], w=[k_t1])
            OP('dve', lambda e: e.tensor_tensor(out=t2, in0=sck(1), in1=wb(1), op=ALU.mult), r=['scs', 'vecs'], w=[k_t2])
            OP('dve', lambda e: e.tensor_tensor(out=t1, in0=t1, in1=t2, op=ALU.add), r=[k_t1, k_t2], w=[k_t1])
            OP('dve', lambda e: e.tensor_tensor(out=t2, in0=USs, in1=wb(2), op=ALU.mult), r=['USs', 'vecs', k_t1], w=[k_t2])
            OP('dve', lambda e: e.tensor_tensor(out=t1, in0=t1, in1=t2, op=ALU.add), r=[k_t1, k_t2], w=[k_t1])
            OP('dve', lambda e: e.tensor_tensor(out=OTS[:, 4:8, :], in0=t1, in1=BGS, op=ALU.mult), r=[k_t1, 'BGS'], w=['OTS'])
            OP('sp', lambda e: e.dma_start(out=sc_s[:, 0, :], in_=st_sc[:, 1, :]), dma=True)
            for c in range(4):
                OP('pe', lambda e, c=c: e.transpose(out=PS[0:16, 6, c * 128:(c + 1) * 128], in_=USs[:, c, :], identity=ident),
                   r=['USs', 'consts'], w=kP(6, c * 128, c * 128 + 128))
            OP('act', lambda e: e.activation(out=tok_s[0:16, 1024:1536], in_=PS[0:16, 6, :], func=AF.Copy), r=kP(6), w=['tok_s'])
            OP('sp', lambda e: e.dma_start(out=sc_s[:, 1, :], in_=tok_s[0:16, 1024:1536]), r=['tok_s'], dma=True)
            def load_sh(b_):
                OP('sp', lambda e, b_=b_: e.dma_start(out=SH[:, b_ % 2, :, :], in_=st_hg[b_].rearrange("h k v -> k h v")),
                   w=kA(18 + b_ % 2) + [('SHh', b_ % 2, h) for h in range(4)], dma=True)
            load_sh(0)
            for b_ in range(NS):
                s = b_ % 2
                if b_ + 1 < NS:
                    load_sh(b_ + 1)
                OP('dve', lambda e, b_=b_, s=s: e.tensor_scalar(out=vmk[:, s, :], in0=tok_s[0:16, 512:1024],
                                                                scalar1=consts[0:16, b_:b_ + 1], scalar2=None, op0=ALU.mult),
                   r=['tok_s', 'consts'], w=kA(20 + s))
                for h in range(4):
                    OP('pe', lambda e, h=h, s=s: e.matmul(PS[:, 4 + h, 0:128], lhsT=tok_s[0:16, h * 128:(h + 1) * 128],
                                                          rhs=vmk[:, s, h * 128:(h + 1) * 128], start=True, stop=True),
                       r=['tok_s'] + kA(20 + s), w=kP(4 + h))
                for h in range(4):
                    OP('dve', lambda e, h=h, s=s, b_=b_: e.scalar_tensor_tensor(out=SH[:, s, h, :], in0=SH[:, s, h, :],
                                                                                scalar=FS[:, h, b_:b_ + 1],
                                                                                in1=PS[:, 4 + h, 0:128],
                                                                                op0=ALU.mult, op1=ALU.add),
                       r=kA(18 + s) + ['FS'] + kP(4 + h), w=[('SHh', s, h)])
                for h in range(4):
                    OP('pe', lambda e, h=h, s=s, b_=b_: e.matmul(PS[:, 0, h * 16 + b_:h * 16 + b_ + 1], lhsT=SH[:, s, h, :],
                                                                 rhs=QS[:, h, b_:b_ + 1], start=True, stop=True),
                       r=[('SHh', s, h), 'QS'], w=kP(0))
                OP('sp', lambda e, b_=b_, s=s: e.dma_start(out=hg_s[b_].rearrange("h k v -> k h v"), in_=SH[:, s, :, :]),
                   r=kA(18 + s) + [('SHh', s, h) for h in range(4)], dma=True)
            OAS, k_oas = scr('oas', 64)
            OP('act', lambda e: e.activation(out=OAS, in_=PS[:, 0, 0:64], func=AF.Copy), r=kP(0), w=[k_oas])
            k_sq = 'sR_esq'
            sq = sR[:, 1024:1088]
            OP('act', lambda e: e.activation(out=sq, in_=OAS, func=AF.Square), r=[k_oas], w=[k_sq])
            OP('pe', lambda e: e.matmul(PS[:, 5, 64:128], lhsT=ones_r[:, :], rhs=sq, start=True, stop=True),
               r=[k_sq, 'ones'], w=kP(5, 64, 128))
            rst, k_rst = scr('erst', 64)
            OP('act', lambda e: e.activation(out=rst, in_=PS[:, 5, 64:128], func=AF.Sqrt, bias=EPS, scale=1.0 / 128),
               r=kP(5, 64, 128), w=[k_rst])
            OP('dve', lambda e: e.reciprocal(out=rst, in_=rst), r=[k_rst], w=[k_rst])
            OP('dve', lambda e: e.scalar_tensor_tensor(out=OAS, in0=OAS, scalar=vecs[:, V_GN:V_GN + 1], in1=rst,
                                                       op0=ALU.mult, op1=ALU.mult), r=[k_oas, k_rst, 'vecs'], w=[k_oas])
            OP('dve', lambda e: e.tensor_tensor(out=OTS[:, 0:4, :], in0=OAS.rearrange("p (c n) -> p c n", n=16), in1=GOS,
                                                op=ALU.mult), r=[k_oas, 'GOS'], w=['OTS'])

        dtTf, k_dtT = scr('dtT', 16)
        dtT = dtTf[0:32, :]
        rmf, k_rm = scr('rm', 256)
        obig, k_obig = scr('obig', 384)
        orst, k_orst = scr('orst', 64)

        P_XT, P_BT, P_CTm, P_XC, P_LW, P_R, P_XW, P_YT1, P_CBM = 0, 16, 20, 24, 28, 32, 34, 36, 37
        P_YT, P_Z, P_SQ = 16, 32, 0

        def odd_layer(ti, with_s):
            XTall = arena[:, 0:8192].rearrange("p (b q) -> p b q", q=2048)
            pend = []
            psilu = []
            for pc in range(8, 20):
                wp, wk = wnext(('wio', pc))
                def ev(j, b, pc=pc):
                    cidx = (pc - 8) * 2 + j
                    s = cidx % 2
                    OP('pool', lambda e: e.tensor_copy(out=craw[:, s, 0:3], in_=ccar[:, cidx, :]), r=[('ccar', cidx)], w=[('crawc', s)])
                    OP('act', lambda e: e.activation(out=craw[:, s, 3:515], in_=PS[:, b, :], func=AF.Copy), r=kP(b), w=[('craw', s)])
                    OP('act', lambda e: e.activation(out=ccar[:, cidx, :], in_=PS[:, b, 509:512], func=AF.Copy), r=kP(b), w=[('ccar', cidx)])
                    if cidx < 16:
                        tpn = P_XC + cidx % 4
                    elif cidx < 20:
                        tpn = P_BT + cidx - 16
                    else:
                        tpn = P_CTm + cidx - 20
                    tp = pg(tpn)
                    wc = lambda k: vecs[:, V_MCW + k * 24 + cidx:V_MCW + k * 24 + cidx + 1]
                    OP('act', lambda e: e.activation(out=tp, in_=craw[:, s, 0:512], func=AF.Identity, scale=wc(0),
                                                     bias=vecs[:, V_MCB + cidx:V_MCB + cidx + 1]),
                       r=[('craw', s), ('crawc', s), 'vecs'], w=kA(tpn))
                    for k in (1, 2, 3):
                        OP('dve', lambda e, k=k: e.scalar_tensor_tensor(out=tp, in0=craw[:, s, k:k + 512], scalar=wc(k), in1=tp,
                                                                       op0=ALU.mult, op1=ALU.add),
                           r=[('craw', s), ('crawc', s), 'vecs'] + kA(tpn), w=kA(tpn))
                    def silu_then(cidx=cidx, tp=tp, tpn=tpn):
                        OP('act', lambda e: e.activation(out=tp, in_=tp, func=AF.Silu), r=kA(tpn), w=kA(tpn))
                        if cidx >= 16:
                            return
                        def later(cidx=cidx, tp=tp, tpn=tpn):
                            g, q = cidx // 4, cidx % 4
                            b2 = nb()
                            def f(e, b2=b2, tp=tp):
                                for bk in range(4):
                                    ins = e.transpose(out=PS[:, b2, bk * 128:(bk + 1) * 128], in_=tp[:, bk * 128:(bk + 1) * 128], identity=ident)
                                return ins
                            OP('pe', f, r=kA(tpn) + ['consts'], w=kP(b2))
                            dst = XTall[:, :, g * 512 + q * 128:g * 512 + (q + 1) * 128]
                            OP('act', lambda e, b2=b2, dst=dst: e.activation(out=dst, in_=PS[:, b2, :].rearrange("p (b f) -> p b f", f=128),
                                                                             func=AF.Copy),
                               r=kP(b2), w=[('A', 4 * bk + g) for bk in range(4)])
                            rel(b2)
                        pend.append(later)
                    if psilu:
                        psilu.pop(0)()
                    psilu.append(silu_then)
                    while len(pend) > 2:
                        pend.pop(0)()
                proj_fm(wp, wk, 256, h_rhs, hkeys, TT, ev)
                if with_s:
                    proj_fm_s(wp, wk, 256, hs_rhs, ['hsT'],
                              lambda j, sl, pc=pc: OP('act', lambda e: e.activation(out=XBS[:, (pc - 8) * 2 + j, :], in_=SPa(sl, NS),
                                                                                     func=AF.Copy), r=kSl(sl), w=['XBS']))
            while psilu:
                psilu.pop(0)()
            while pend:
                pend.pop(0)()
            wp, wk = wnext(('wdt',))
            for bk in range(4):
                def f(e, bk=bk, wp=wp):
                    for kc in range(8):
                        ins = e.matmul(PS[:, 7, bk * 32:(bk + 1) * 32], lhsT=hT[:, kc, bk * 128:(bk + 1) * 128], rhs=wp[:, kc, 0:32],
                                       start=(kc == 0), stop=(kc == 7))
                    return ins
                OP('pe', f, r=[wk] + hkeys, w=kP(7, 0, 128))
            if with_s:
                def f(e, wp=wp):
                    for kc in range(8):
                        ins = e.matmul(PS[0:32, 6, 0:16], lhsT=wp[:, kc, 0:32], rhs=hsT[:, kc, :], start=(kc == 0), stop=(kc == 7))
                    return ins
                OP('pe', f, r=[wk, 'hsT'], w=kP(6, 0, 64))
                OP('act', lambda e: e.activation(out=dtT, in_=PS[0:32, 6, 0:16], func=AF.Exp, bias=hexp[0:32, 192:193]),
                   r=kP(6, 0, 64) + ['hexp'], w=[k_dtT])
                OP('act', lambda e: e.activation(out=dtT, in_=dtT, func=AF.Ln, bias=1.0), r=[k_dtT], w=[k_dtT])
            dtv = blk[:, :, 0:32]
            OP('dve', lambda e: e.tensor_tensor(out=dtv, in0=PS[:, 7, 0:128].rearrange("p (b h) -> p b h", h=32),
                                                in1=bm3(rowv[:, 0:32], 4), op=ALU.add), r=kP(7, 0, 128) + ['rowv'], w=['blk_dt'])
            OP('act', lambda e: e.activation(out=dtv, in_=dtv, func=AF.Exp), r=['blk_dt'], w=['blk_dt'])
            OP('act', lambda e: e.activation(out=dtv, in_=dtv, func=AF.Ln, bias=1.0), r=['blk_dt'], w=['blk_dt'])
            OP('dve', lambda e: e.tensor_tensor(out=blk[:, :, 32:64], in0=dtv, in1=bm3(Ab[:, :], 4), op=ALU.mult),
               r=['blk_dt', 'Ab'], w=['blk_dta'])
            for bk in range(4):
                OP('pe', lambda e, bk=bk: e.matmul(PS[:, 7, 128 + bk * 32:160 + bk * 32], lhsT=tri, rhs=blk[:, bk, 32:64], start=True, stop=True),
                   r=['blk_dta', 'consts'], w=kP(7, 128, 256))
                OP('pe', lambda e, bk=bk: e.matmul(PS[:, 7, 256 + bk * 32:288 + bk * 32], lhsT=ones_f, rhs=blk[:, bk, 32:64], start=True, stop=True),
                   r=['blk_dta', 'consts'], w=kP(7, 256, 384))
            cs_ps = PS[:, 7, 128:256].rearrange("p (b h) -> p b h", h=32)
            cl_ps = PS[:, 7, 256:384].rearrange("p (b h) -> p b h", h=32)
            OP('act', lambda e: e.activation(out=blk[:, :, 64:96], in_=cs_ps, func=AF.Exp), r=kP(7, 128, 256), w=['blk_ecs'])
            OP('act', lambda e: e.activation(out=blk[:, :, 96:128], in_=cs_ps, func=AF.Copy), r=kP(7, 128, 256), w=['blk_wg'])
            OP('dve', lambda e: e.tensor_tensor(out=blk[:, :, 96:128], in0=cl_ps, in1=blk[:, :, 96:128], op=ALU.subtract),
               r=kP(7, 256, 384) + ['blk_wg'], w=['blk_wg'])
            OP('act', lambda e: e.activation(out=blk[:, :, 96:128], in_=blk[:, :, 96:128], func=AF.Exp), r=['blk_wg'], w=['blk_wg'])
            OP('act', lambda e: e.activation(out=blk[:, :, 128:160], in_=cl_ps, func=AF.Exp), r=kP(7, 256, 384), w=['blk_dec'])
            OP('dve', lambda e: e.reciprocal(out=blk[:, :, 160:192], in_=dtv), r=['blk_dt'], w=['blk_dod'])
            OP('dve', lambda e: e.tensor_tensor(out=blk[:, :, 160:192], in0=blk[:, :, 160:192], in1=bm3(rowv[:, 64:96], 4), op=ALU.mult),
               r=['blk_dod', 'rowv'], w=['blk_dod'])
            for bk in range(4):
                OP('dve', lambda e, bk=bk: e.tensor_tensor(out=XTall[:, bk, :].rearrange("p (h q) -> p h q", q=64),
                                                           in0=XTall[:, bk, :].rearrange("p (h q) -> p h q", q=64),
                                                           in1=bc3(blk[:, bk, 0:32], 64), op=ALU.mult),
                   r=kA(4 * bk, 4) + ['blk_dt'], w=kA(4 * bk, 4))
            prog.mark('t%d_odd_inproj' % ti)
            v3 = lambda ap: ap.rearrange("p (h q) -> p h q", q=64)
            iters = [(bk, g) for bk in range(4) for g in range(4)]

            def ctx(it):
                bk, g = iters[it]
                c = dict(bk=bk, g=g, ts=slice(bk * 128, (bk + 1) * 128), hs=slice(g * 8, (g + 1) * 8))
                c['XTp'] = pg(4 * bk + g); c['kX'] = kA(4 * bk + g)
                c['lw'] = P_LW + 2 * (it % 2)
                c['LW'] = arena[:, c['lw'] * 512:(c['lw'] + 2) * 512]
                c['xw'] = pgR(P_XW + it % 2); c['kxw'] = kA(P_XW + it % 2)
                c['cbm'] = pg(P_CBM)[:, (it % 2) * 128:(it % 2) * 128 + 128]; c['kcb'] = [('cbm', it % 2)]
                c['btok'] = pgR(P_CBM)[:, 256 + (it % 2) * 128:384 + (it % 2) * 128]; c['kbt'] = [('btk', it % 2)]
                return c

            def stageA1(c):
                bk, g, ts, hs = c['bk'], c['g'], c['ts'], c['hs']
                OP('pe', lambda e: e.matmul(PS[:, 7, 384:512], lhsT=pg(P_BT + g)[:, ts], rhs=pg(P_CTm + g)[:, ts], start=True, stop=True),
                   r=kA(P_BT + g) + kA(P_CTm + g), w=kP(7))
                R3 = arenaR[:, P_R * 512:(P_R + 2) * 512].rearrange("p (r i) -> p r i", i=128)
                Rf = arenaR[:, P_R * 512:(P_R + 2) * 512]
                def fR(e):
                    for r_ in range(8):
                        ins = e.activation(out=R3[:, r_, :], in_=tri, func=AF.Identity, scale=blk[:, bk, 32 + g * 8 + r_:33 + g * 8 + r_])
                    return ins
                OP('act', fR, r=['consts', 'blk_dta'], w=kA(P_R, 2))
                OP('dve', lambda e: e.tensor_tensor(out=c['cbm'], in0=PS[:, 7, 384:512], in1=tri, op=ALU.mult),
                   r=kP(7) + ['consts'], w=c['kcb'])
                for hh in range(2):
                    OP('pe', lambda e, hh=hh: e.matmul(PS[:, 5 + hh, :], lhsT=U_r[:, :], rhs=Rf[:, hh * 512:(hh + 1) * 512], start=True, stop=True),
                       r=['U_r'] + kA(P_R, 2), w=kP(5 + hh))
                    OP('act', lambda e, hh=hh: e.activation(out=c['LW'][:, hh * 512:(hh + 1) * 512], in_=PS[:, 5 + hh, :], func=AF.Exp),
                       r=kP(5 + hh), w=kA(c['lw'] + hh))
                OP('dve', lambda e: e.tensor_tensor(out=v3(c['xw']), in0=v3(c['XTp']), in1=bc3(blk[:, bk, 96:128][:, hs], 64), op=ALU.mult),
                   r=c['kX'] + ['blk_wg'], w=c['kxw'])
                bt = nb()
                OP('pe', lambda e: e.transpose(out=PS[:, bt, 0:128], in_=pg(P_BT + g)[:, ts], identity=ident),
                   r=kA(P_BT + g) + ['consts'], w=kP(bt))
                OP('act', lambda e: e.activation(out=c['btok'], in_=PS[:, bt, 0:128], func=AF.Copy), r=kP(bt), w=c['kbt'])
                rel(bt)

            def stageA2(c):
                LW3 = c['LW'].rearrange("p (r i) -> p r i", i=128)
                OP('dve', lambda e: e.tensor_tensor(out=LW3, in0=LW3, in1=bm3(c['cbm'], 8), op=ALU.mult),
                   r=kA(c['lw'], 2) + c['kcb'], w=kA(c['lw'], 2))

            def stageB(c):
                bk, g, ts, hs, XTp, kX, LW = c['bk'], c['g'], c['ts'], c['hs'], c['XTp'], c['kX'], c['LW']
                b2 = nb()
                OP('pe', lambda e: e.matmul(PS[:, b2, :], lhsT=pg(P_CTm + g)[:, ts], rhs=ST[:, g * 512:(g + 1) * 512], start=True, stop=True),
                   r=kA(P_CTm + g) + [('ST', g)], w=kP(b2))
                b1 = nb()
                def f(e):
                    for r_ in range(8):
                        ins = e.matmul(PS[:, b1, r_ * 64:(r_ + 1) * 64], lhsT=LW[:, r_ * 128:(r_ + 1) * 128],
                                       rhs=XTp[:, r_ * 64:(r_ + 1) * 64], start=True, stop=True)
                    return ins
                OP('pe', f, r=kA(c['lw'], 2) + kX, w=kP(b1))
                b3 = nb()
                OP('pe', lambda e: e.matmul(PS[:, b3, :], lhsT=c['btok'], rhs=c['xw'], start=True, stop=True),
                   r=c['kbt'] + c['kxw'], w=kP(b3))
                t1 = pg(P_YT1)
                OP('dve', lambda e: e.tensor_tensor(out=v3(t1), in0=v3(PS[:, b2, :]), in1=bc3(blk[:, bk, 64:96][:, hs], 64), op=ALU.mult),
                   r=kP(b2) + ['blk_ecs'], w=kA(P_YT1))
                rel(b2)
                OP('dve', lambda e: e.tensor_tensor(out=t1, in0=PS[:, b1, :], in1=t1, op=ALU.add), r=kP(b1) + kA(P_YT1), w=kA(P_YT1))
                rel(b1)
                OP('pool', lambda e: e.tensor_tensor(out=v3(XTp), in0=v3(XTp), in1=bc3(blk[:, bk, 160:192][:, hs], 64), op=ALU.mult),
                   r=kX + ['blk_dod'], w=kX)
                OP('pool', lambda e: e.tensor_tensor(out=XTp, in0=XTp, in1=t1, op=ALU.add), r=kX + kA(P_YT1), w=kX)
                STg = ST[:, g * 512:(g + 1) * 512]
                OP('dve', lambda e: e.tensor_tensor(out=v3(STg), in0=v3(STg), in1=bc3(blk[:, bk, 128:160][:, hs], 64), op=ALU.mult),
                   r=[('ST', g), 'blk_dec'], w=[('ST', g)])
                OP('dve', lambda e: e.tensor_tensor(out=STg, in0=STg, in1=PS[:, b3, :], op=ALU.add), r=[('ST', g)] + kP(b3), w=[('ST', g)])
                rel(b3)

            cs_ = [ctx(it) for it in range(16)]
            stageA1(cs_[0]); stageA2(cs_[0])
            for it in range(16):
                if it + 1 < 16:
                    stageA1(cs_[it + 1])
                stageB(cs_[it])
                if it + 1 < 16:
                    stageA2(cs_[it + 1])
            prog.mark('t%d_odd_ssd' % ti)
            if with_s:
                odd_sample_pre()
            prog.mark('t%d_odd_spre' % ti)
            for c in range(16):
                g, q = c // 4, c % 4
                b = nb()
                def f(e, b=b, g=g, q=q):
                    for bk in range(4):
                        ins = e.transpose(out=PS[:, b, bk * 128:(bk + 1) * 128], in_=pg(4 * bk + g)[:, q * 128:(q + 1) * 128], identity=ident)
                    return ins
                OP('pe', f, r=[('A', 4 * bk + g) for bk in range(4)] + ['consts'], w=kP(b))
                if c % 2 == 0:
                    OP('act', lambda e, b=b, c=c: e.activation(out=pg(P_YT + c), in_=PS[:, b, :], func=AF.Copy), r=kP(b), w=kA(P_YT + c))
                else:
                    OP('dve', lambda e, b=b, c=c: e.tensor_copy(out=pg(P_YT + c), in_=PS[:, b, :]), r=kP(b), w=kA(P_YT + c))
                rel(b)
            for pc in range(8):
                wp, wk = wnext(('wio', pc))
                def ev(j, b, pc=pc):
                    zc = pc * 2 + j
                    zp = P_Z + zc % 4
                    OP('act', lambda e: e.activation(out=pg(zp), in_=PS[:, b, :], func=AF.Silu), r=kP(b), w=kA(zp))
                    OP('dve', lambda e: e.tensor_tensor(out=pg(P_YT + zc), in0=pg(P_YT + zc), in1=pg(zp), op=ALU.mult),
                       r=kA(P_YT + zc) + kA(zp), w=kA(P_YT + zc))
                proj_fm(wp, wk, 256, h_rhs, hkeys, TT, ev)
                if with_s:
                    proj_fm_s(wp, wk, 256, hs_rhs, ['hsT'],
                              lambda j, sl, pc=pc: OP('act', lambda e: e.activation(out=ZS[:, pc * 2 + j, :], in_=SPa(sl, NS),
                                                                                     func=AF.Silu), r=kSl(sl), w=['ZS']))
                if pc % 2 == 1:
                    g = pc // 2
                    yg = arena[:, (P_YT + 4 * g) * 512:(P_YT + 4 * g + 4) * 512]
                    sqv = arenaR[:, P_SQ * 512:(P_SQ + 4) * 512]
                    OP('act', lambda e, yg=yg, sqv=sqv: e.activation(out=sqv, in_=yg, func=AF.Square), r=kA(P_YT + 4 * g, 4), w=kA(P_SQ, 4))
                    b = nb()
                    def f(e, b=b, sqv=sqv):
                        for q in range(4):
                            ins = e.matmul(PS[:, b, :], lhsT=ones_r[:, :], rhs=sqv[:, q * 512:(q + 1) * 512], start=(q == 0), stop=(q == 3))
                        return ins
                    OP('pe', f, r=kA(P_SQ, 4) + ['ones'], w=kP(b))
                    OP('act', lambda e, b=b: e.activation(out=rs[:, :], in_=PS[:, b, :], func=AF.Sqrt, bias=EPS, scale=1.0 / 512),
                       r=kP(b), w=['rs'])
                    rel(b)
                    OP('dve', lambda e: e.reciprocal(out=rs[:, :], in_=rs[:, :]), r=['rs'], w=['rs'])
                    for q in range(4):
                        c = 4 * g + q
                        OP('dve', lambda e, c=c: e.scalar_tensor_tensor(out=pgR(P_YT + c), in0=pg(P_YT + c),
                                                                        scalar=vecs[:, V_MN + c:V_MN + c + 1], in1=rs[:, :],
                                                                        op0=ALU.mult, op1=ALU.mult),
                           r=kA(P_YT + c) + ['rs', 'vecs'], w=kA(P_YT + c))
            if with_s:
                odd_sample_post()
            ev = resid_evac(1, 2)
            for cg in range(4):
                banks = [nb(), nb()]
                for kg in range(2):
                    wp, wk = wnext(('woo', cg, kg))
                    for j in range(2):
                        def f(e, wp=wp, j=j, kg=kg, b=banks[j]):
                            for kc in range(8):
                                ins = e.matmul(PS[:, b, :], lhsT=wp[:, kc, j * 128:(j + 1) * 128],
                                               rhs=pgR(P_YT + kg * 8 + kc),
                                               start=(kg == 0 and kc == 0), stop=(kg == 1 and kc == 7))
                            return ins
                        OP('pe', f, r=[wk] + kA(P_YT + kg * 8, 8), w=kP(banks[j]))
                    if with_s:
                        for j in range(2):
                            def f(e, wp=wp, j=j, kg=kg):
                                for kc in range(8):
                                    ins = e.matmul(PS[:, 4 + j, 448:464], lhsT=wp[:, kc, j * 128:(j + 1) * 128],
                                                   rhs=YNS[:, kg * 8 + kc, :],
                                                   start=(kg == 0 and kc == 0), stop=(kg == 1 and kc == 7))
                                return ins
                            OP('pe', f, r=[wk, 'YNS'], w=kP(4 + j, 448, 512))
                for j in range(2):
                    ev(cg * 2 + j, banks[j])
                    rel(banks[j])
                    if with_s:
                        resid_evac_s(1, 2, cg * 2 + j, PS[:, 4 + j, 448:464], kP(4 + j, 448, 512))
            if ti == NTILE - 1:
                stg = arena[:, 0:2048].rearrange("p (a s) -> p a s", s=128)
                for hp4 in range(4):
                    b = nb()
                    def f(e, b=b, hp4=hp4):
                        for a in range(4):
                            hp = hp4 * 4 + a
                            ins = e.transpose(out=PS[:, b, a * 128:(a + 1) * 128], in_=ST[:, hp * 128:(hp + 1) * 128], identity=ident)
                        return ins
                    OP('pe', f, r=[('ST', g) for g in range(4)] + ['consts'], w=kP(b))
                    OP('act', lambda e, b=b, hp4=hp4: e.activation(out=pg(hp4), in_=PS[:, b, :], func=AF.Copy), r=kP(b), w=kA(hp4))
                    rel(b)
                dstv = ssm_p.rearrange("(hp hh) p s -> hh p hp s", hh=2)
                for h2 in range(2):
                    OP('sp', lambda e, h2=h2: e.dma_start(out=dstv[h2], in_=stg[h2 * 64:(h2 + 1) * 64, :, :]), r=kA(0, 4), dma=True)
                for half in range(2):
                    for c4 in range(3):
                        b = nb()
                        def f(e, b=b, c4=c4, half=half):
                            for a in range(4):
                                ins = e.transpose(out=PS[0:3, b, a * 128:(a + 1) * 128], in_=ccar[:, half * 12 + c4 * 4 + a, :], identity=ident)
                            return ins
                        OP('pe', f, r=[('ccar', half * 12 + c4 * 4 + a) for a in range(4)] + ['consts'], w=kP(b))
                        OP('act', lambda e, b=b, c4=c4: e.activation(out=tok_s[0:3, c4 * 512:(c4 + 1) * 512], in_=PS[0:3, b, :], func=AF.Copy),
                           r=kP(b), w=['tok_s'])
                        rel(b)
                    OP('sp', lambda e, half=half: e.dma_start(out=mc_p[:, half * 1536:(half + 1) * 1536], in_=tok_s[0:3, 0:1536]),
                       r=['tok_s'], dma=True)

        def odd_sample_pre():
            SS = arena[:, 36 * 512:38 * 512].rearrange("p (a s) -> p a s", s=128)
            BCb = arena[:, 32 * 512:34 * 512]
            bmk = arena[0:16, 34 * 512:35 * 512]
            for half in range(2):
                c0 = half * 1536
                OP('sp', lambda e, c0=c0: e.dma_start(out=tok_s[0:48, 0:1536], in_=st_mc.rearrange("b k c -> (b k) c")[:, c0:c0 + 1536]),
                   w=['tok_s'], dma=True)
                for (a0, na) in ((0, 8), (8, 4)):
                    b = nb()
                    def f(e, b=b, a0=a0, na=na):
                        for a in range(na):
                            ins = e.transpose(out=PS[:, b, a * 64:a * 64 + 48], in_=tok_s[0:48, (a0 + a) * 128:(a0 + a + 1) * 128],
                                              identity=consts[0:48, 0:48])
                        return ins
                    OP('pe', f, r=['tok_s', 'consts'], w=kP(b))
                    OP('act', lambda e, b=b, half=half, a0=a0, na=na: e.activation(
                        out=mcs[:, half * 12 + a0:half * 12 + a0 + na, :],
                        in_=PS[:, b, 0:na * 64].rearrange("p (c n) -> p c n", n=64)[:, :, 0:48], func=AF.Copy),
                        r=kP(b), w=['mcs'])
                    rel(b)
            mck = lambda k: mcs[:, :, :].rearrange("p c (b k) -> p c b k", k=3)[:, :, :, k]
            wb = lambda k: bc3(vecs[:, V_MCW + k * 24:V_MCW + k * 24 + 24], NS)
            t2 = obig.rearrange("p (c n) -> p c n", n=16)
            OP('dve', lambda e: e.tensor_tensor(out=XCS, in0=mck(0), in1=wb(0), op=ALU.mult), r=['mcs', 'vecs'], w=['XCS'])
            for k in (1, 2):
                OP('dve', lambda e, k=k: e.tensor_tensor(out=t2, in0=mck(k), in1=wb(k), op=ALU.mult), r=['mcs', 'vecs', 'XCS'], w=[k_obig])
                OP('dve', lambda e: e.tensor_tensor(out=XCS, in0=XCS, in1=t2, op=ALU.add), r=['XCS', k_obig], w=['XCS'])
            OP('dve', lambda e: e.tensor_tensor(out=t2, in0=XBS, in1=wb(3), op=ALU.mult), r=['XBS', 'vecs', 'XCS'], w=[k_obig])
            OP('dve', lambda e: e.tensor_tensor(out=XCS, in0=XCS, in1=t2, op=ALU.add), r=['XCS', k_obig], w=['XCS'])
            OP('dve', lambda e: e.tensor_tensor(out=XCS, in0=XCS, in1=bc3(vecs[:, V_MCB:V_MCB + 24], NS), op=ALU.add),
               r=['XCS', 'vecs'], w=['XCS'])
            OP('act', lambda e: e.activation(out=XCS, in_=XCS, func=AF.Silu), r=['XCS'], w=['XCS'])
            OP('sp', lambda e: e.dma_start(out=mc_s[:, 0:2, :], in_=st_mc[:, 1:3, :]), dma=True)
            for half in range(2):
                for c4 in range(3):
                    b = nb()
                    def f(e, b=b, c4=c4, half=half):
                        for a in range(4):
                            ins = e.transpose(out=PS[0:16, b, a * 128:(a + 1) * 128], in_=XBS[:, half * 12 + c4 * 4 + a, :], identity=ident)
                        return ins
                    OP('pe', f, r=['XBS', 'consts'], w=kP(b))
                    OP('act', lambda e, b=b, c4=c4: e.activation(out=tok_s[0:16, c4 * 512:(c4 + 1) * 512], in_=PS[0:16, b, :], func=AF.Copy),
                       r=kP(b), w=['tok_s'])
                    rel(b)
                OP('sp', lambda e, half=half: e.dma_start(out=mc_s[:, 2, half * 1536:(half + 1) * 1536], in_=tok_s[0:16, 0:1536]),
                   r=['tok_s'], dma=True)
            Rm = rmf[0:32, :].rearrange("p (a n) -> p a n", n=16)
            OP('dve', lambda e: e.tensor_tensor(out=Rm, in0=bm3(dtT, 16), in1=bc3(hexp[0:32, 176:192], NS), op=ALU.mult),
               r=[k_dtT, 'hexp'], w=[k_rm])
            OP('pe', lambda e: e.matmul(PS[:, 6, 0:256], lhsT=hexp[0:32, 48:176], rhs=rmf[0:32, :], start=True, stop=True),
               r=[k_rm, 'hexp'], w=kP(6, 0, 256))
            OP('act', lambda e: e.activation(out=DTS, in_=PS[:, 6, 0:256].rearrange("p (a n) -> p a n", n=16), func=AF.Copy),
               r=kP(6, 0, 256), w=['DTS'])
            OP('dve', lambda e: e.tensor_tensor(out=AES, in0=DTS, in1=bc3(aexp[:, :], NS), op=ALU.mult), r=['DTS', 'aexp'], w=['AES'])
            OP('act', lambda e: e.activation(out=AES, in_=AES, func=AF.Exp), r=['AES'], w=['AES'])
            OP('dve', lambda e: e.tensor_tensor(out=DXS, in0=DTS, in1=XCS[:, 0:16, :], op=ALU.mult), r=['DTS', 'XCS'], w=['DXS'])
            for a in range(8):
                OP('pe', lambda e, a=a: e.transpose(out=PS[0:16, 5 + a // 4, (a % 4) * 128:(a % 4 + 1) * 128], in_=XCS[:, 16 + a, :], identity=ident),
                   r=['XCS', 'consts'], w=kP(5 + a // 4, (a % 4) * 128, (a % 4 + 1) * 128))
            OP('act', lambda e: e.activation(out=tok_s[0:16, 0:512], in_=PS[0:16, 5, :], func=AF.Copy), r=kP(5), w=['tok_s'])
            OP('act', lambda e: e.activation(out=tok_s[0:16, 512:1024], in_=PS[0:16, 6, :], func=AF.Copy), r=kP(6), w=['tok_s'])
            srcv = lambda b_: st_ssm[b_].rearrange("(hp hh) p s -> (hh p) hp s", hh=2)
            dstv = lambda b_: ssm_s[b_].rearrange("(hp hh) p s -> (hh p) hp s", hh=2)
            SSp = [36, 37, 24, 25]
            SSb = [pg(p).rearrange("p (a s) -> p a s", s=128) for p in SSp]
            kSS = [kA(p) for p in SSp]
            Tb = [pg(34).rearrange("p (a s) -> p a s", s=128), pg(35).rearrange("p (a s) -> p a s", s=128)]
            kTb = [kA(34), kA(35)]
            OP('sp', lambda e: e.dma_start(out=bc_scr, in_=tok_s[0:16, 0:1024]), r=['tok_s'], w=['bc_scr'], dma=True)
            BCs = [arena[:, 32 * 512:34 * 512], arena[:, 26 * 512:28 * 512]]
            kBC = [kA(32, 2), kA(26, 2)]

            def load_bc(b_):
                OP('sp', lambda e, b_=b_: e.dma_start(out=BCs[b_ % 2], in_=bc_scr[b_:b_ + 1, :].partition_broadcast(128)),
                   r=['bc_scr'], w=kBC[b_ % 2], dma=True)

            def load_ss(n):
                b_, q = n // 4, n % 4
                sl = n % 4
                extra = [('cbm', 0), ('cbm', 1), ('btk', 0), ('btk', 1)] if SSp[sl] == 37 else []
                OP('sp', lambda e, b_=b_, q=q, sl=sl: e.dma_start(out=SSb[sl], in_=srcv(b_)[:, q * 4:(q + 1) * 4, :]),
                   w=kSS[sl] + extra, dma=True)

            NQ = 4 * NS
            load_bc(0)
            for n in range(3):
                load_ss(n)
            for n in range(NQ):
                b_, q = n // 4, n % 4
                sl = n % 4
                if q == 0 and b_ + 1 < NS:
                    load_bc(b_ + 1)
                if n + 3 < NQ:
                    load_ss(n + 3)
                SSx, kS = SSb[sl], kSS[sl]
                Tt, kT = Tb[n % 2], kTb[n % 2]
                BCb, kB = BCs[b_ % 2], kBC[b_ % 2]
                hp0 = q * 4
                dx3 = bc3(DXS[:, hp0:hp0 + 4, b_], 128)
                B3 = bm3(BCb[:, q * 128:(q + 1) * 128], 4)
                C3 = bm3(BCb[:, 512 + q * 128:512 + (q + 1) * 128], 4)
                def fS(e, SSx=SSx, hp0=hp0, b_=b_):
                    for a in range(4):
                        ins = e.activation(out=SSx[:, a, :], in_=SSx[:, a, :], func=AF.Identity, scale=AES[:, hp0 + a, b_:b_ + 1])
                    return ins
                OP('act', fS, r=kS + ['AES'], w=kS)
                OP('dve', lambda e, Tt=Tt, B3=B3, dx3=dx3: e.tensor_tensor(out=Tt, in0=B3, in1=dx3, op=ALU.mult),
                   r=kB + ['DXS'], w=kT)
                OP('dve', lambda e, SSx=SSx, Tt=Tt: e.tensor_tensor(out=SSx, in0=SSx, in1=Tt, op=ALU.add), r=kS + kT, w=kS)
                OP('dve', lambda e, Tt=Tt, SSx=SSx, C3=C3: e.tensor_tensor(out=Tt, in0=SSx, in1=C3, op=ALU.mult), r=kS + kB, w=kT)
                OP('dve', lambda e, b_=b_, hp0=hp0, Tt=Tt: e.tensor_reduce(out=YS[:, hp0:hp0 + 4, b_], in_=Tt, axis=AX.X, op=ALU.add),
                   r=kT, w=['YS'])
                OP('act', lambda e, b_=b_, q=q, SSx=SSx: e.dma_start(out=dstv(b_)[:, q * 4:(q + 1) * 4, :], in_=SSx),
                   r=kS, dma=True)
            OP('dve', lambda e: e.tensor_tensor(out=DXS, in0=XCS[:, 0:16, :], in1=bc3(hexp[:, 32:48], NS), op=ALU.mult),
               r=['XCS', 'hexp', 'DXS'], w=['DXS'])
            OP('dve', lambda e: e.tensor_tensor(out=YS, in0=YS, in1=DXS, op=ALU.add), r=['YS', 'DXS'], w=['YS'])

        def odd_sample_post():
            OP('dve', lambda e: e.tensor_tensor(out=YS, in0=YS, in1=ZS, op=ALU.mult), r=['YS', 'ZS'], w=['YS'])
            sq = sR[:, 1088:1344]
            OP('act', lambda e: e.activation(out=sq, in_=sbuf_s[:, 1088:1344], func=AF.Square), r=['YS'], w=['sR_osq'])
            for g in range(4):
                def f(e, g=g):
                    for q in range(4):
                        c = 4 * g + q
                        ins = e.matmul(PS[:, 5, g * 16:(g + 1) * 16], lhsT=ones_r[:, :], rhs=sq[:, c * 16:(c + 1) * 16],
                                       start=(q == 0), stop=(q == 3))
                    return ins
                OP('pe', f, r=['sR_osq', 'ones'], w=kP(5, 0, 64))
            rst = orst
            OP('act', lambda e: e.activation(out=rst, in_=PS[:, 5, 0:64], func=AF.Sqrt, bias=EPS, scale=1.0 / 512),
               r=kP(5, 0, 64), w=[k_orst])
            OP('dve', lambda e: e.reciprocal(out=rst, in_=rst), r=[k_orst], w=[k_orst])
            y4 = sbuf_s[:, 1088:1344].rearrange("p (g q n) -> p g q n", q=4, n=16)
            r4 = rst.rearrange("p (g n) -> p g n", n=16).unsqueeze(2).broadcast_to([128, 4, 4, 16])
            OP('dve', lambda e: e.tensor_tensor(out=y4, in0=y4, in1=r4, op=ALU.mult), r=['YS', k_orst], w=['YS'])
            OP('dve', lambda e: e.tensor_tensor(out=YNS, in0=YS, in1=bc3(vecs[:, V_MN:V_MN + 16], NS), op=ALU.mult),
               r=['YS', 'vecs'], w=['YNS'])

        def final_out(ti, with_s):
            t0 = ti * TT
            norm_prompt('vecs', lambda c: vecs[:, V_NFIN + c:V_NFIN + c + 1], lambda c: 0.0, 'vecs',
                        out_fn=lambda c: pg(16 + c), okey=lambda c: kA(16 + c))
            if ti + 1 < NTILE:
                load_x_dma(ti + 1)
            ostage = arena[:, 24 * 512:32 * 512].rearrange("p (b d) -> p b d", d=D)
            for bk in range(4):
                for hf in range(2):
                    b = nb()
                    def f(e, b=b, bk=bk, hf=hf):
                        for a in range(4):
                            c = hf * 4 + a
                            ins = e.transpose(out=PS[:, b, a * 128:(a + 1) * 128], in_=pg(16 + c)[:, bk * 128:(bk + 1) * 128], identity=ident)
                        return ins
                    OP('pe', f, r=kA(16 + hf * 4, 4) + ['consts'], w=kP(b))
                    if hf == 0:
                        OP('act', lambda e, b=b, bk=bk, hf=hf: e.activation(out=ostage[:, bk, hf * 512:(hf + 1) * 512], in_=PS[:, b, :], func=AF.Copy),
                           r=kP(b), w=kA(24 + 2 * bk + hf))
                    else:
                        OP('dve', lambda e, b=b, bk=bk, hf=hf: e.tensor_copy(out=ostage[:, bk, hf * 512:(hf + 1) * 512], in_=PS[:, b, :]),
                           r=kP(b), w=kA(24 + 2 * bk + hf))
                    rel(b)
            OP('sp', lambda e, t0=t0: e.dma_start(out=y_p[t0:t0 + TT, :].rearrange("(b p) d -> p b d", p=128), in_=ostage),
               r=kA(24, 8), dma=True)
            if with_s:
                yff, k_yf = scr('yf', 128)
                yf = yff.rearrange("p (c n) -> p c n", n=16)
                norm_sample(bc3(vecs[:, V_NFIN:V_NFIN + 8], NS), None, 'vecs', None, yf, k_yf)
                for hf in range(2):
                    b = nb()
                    def f(e, b=b, hf=hf):
                        for a in range(4):
                            ins = e.transpose(out=PS[0:16, b, a * 128:(a + 1) * 128], in_=yf[:, hf * 4 + a, :], identity=ident)
                        return ins
                    OP('pe', f, r=[k_yf, 'consts'], w=kP(b))
                    OP('act', lambda e, b=b, hf=hf: e.activation(out=tok_s[0:16, hf * 512:(hf + 1) * 512], in_=PS[0:16, b, :], func=AF.Copy),
                       r=kP(b), w=['tok_s'])
                    rel(b)
                OP('sp', lambda e: e.dma_start(out=y_s, in_=tok_s[0:16, 0:D]), r=['tok_s'], dma=True)

        for ti in range(NTILE):
            t0 = ti * TT
            with_s = (ti == 0)
            if ti > 0:
                load_x(ti, dma=False)
            if with_s:
                OP('sp', lambda e: e.dma_start(out=tok_s[0:16, 0:D], in_=xs), w=['tok_s'], dma=True)
                for c in range(8):
                    sl = s_psum()
                    OP('pe', lambda e, c=c, sl=sl: e.transpose(out=SPa(sl, 16), in_=tok_s[0:16, c * 128:(c + 1) * 128], identity=consts[0:16, 0:16]),
                       r=['tok_s', 'consts'], w=kSl(sl))
                    OP('act', lambda e, c=c, sl=sl: e.activation(out=xsT[:, c, :], in_=SPa(sl, 16), func=AF.Copy),
                       r=kSl(sl), w=['xsT'])
            prog.mark('t%d_xload' % ti)
            for l in range(2):
                prog.mark('t%d_l%d_start' % (ti, l))
                norm_prompt(('gsm', l, 0), lambda c, l=l: gsm[:, l, 0, c, 0:1], lambda c, l=l: modT[:, l, c, 0:1], ('mod', l, 0))
                if with_s:
                    norm_sample(gsm[:, l, 0, :, 1:17], modT[:, l, 0:8, 1:17], ('gsm', l, 0), ('mod', l, 0), hsT[:, :, :], 'hsT')
                prog.mark('t%d_l%d_norm1' % (ti, l))
                if l == 0:
                    even_layer(ti, with_s)
                else:
                    odd_layer(ti, with_s)
                prog.mark('t%d_l%d_mixer' % (ti, l))
                norm_prompt(('gsm', l, 1), lambda c, l=l: gsm[:, l, 1, c, 0:1], lambda c, l=l: modT[:, l, 24 + c, 0:1], ('mod', l, 3))
                if with_s:
                    norm_sample(gsm[:, l, 1, :, 1:17], modT[:, l, 24:32, 1:17], ('gsm', l, 1), ('mod', l, 3), hsT[:, :, :], 'hsT')
                mlp(l, with_s)
                prog.mark('t%d_l%d_mlp' % (ti, l))
            final_out(ti, with_s)
            prog.mark('t%d_done' % ti)
        assert wstate['used'] == len(WSCHED), (wstate['used'], len(WSCHED))

        import os
        if os.environ.get('KCUT'):
            print('marks', prog.marks)
            kc = os.environ['KCUT']
            if kc.isdigit():
                prog.marks['_n'] = int(kc); kc = '_n'
            prog.cut(kc)
        prog.resolve()
        with nc.Block() as block:
            prog.emit(nc, block, es)
    return nc


_CACHE = {}


def _get_nc():
    if 'nc' not in _CACHE:
        _CACHE['nc'] = build()
    return _CACHE['nc']


def kernel(x_prompt, x_sample, c_prompt, c_sample, state_hgrn, state_shortconv, state_ssm, state_mconv,
           ada_w, ada_b, norm_mix, norm_mlp, norm_final, w_in_even, hgrn_lb, hgrn_gnorm, sc_w, w_out_even,
           w_in_odd, mconv_w, mconv_b, dt_bias, a_log, d_skip, m_norm, w_out_odd, mlp_w1, mlp_w2):
    f = lambda a: np.ascontiguousarray(np.asarray(a, dtype=np.float32))
    x_prompt, x_sample, c_prompt, c_sample = f(x_prompt), f(x_sample), f(c_prompt), f(c_sample)
    state_hgrn, state_shortconv, state_ssm, state_mconv = f(state_hgrn), f(state_shortconv), f(state_ssm), f(state_mconv)
    vecs = np.zeros((384, 128), np.float32)
    def put(r0, a):
        a = f(a).reshape(-1, 128)
        vecs[r0:r0 + a.shape[0]] = a
    put(V_ADAB, ada_b); put(V_NMIX, norm_mix); put(V_NMLP, norm_mlp); put(V_NFIN, norm_final)
    put(V_LB, hgrn_lb); put(V_GN, hgrn_gnorm); put(V_SCW, sc_w); put(V_MCW, mconv_w); put(V_MCB, mconv_b); put(V_MN, m_norm)
    rowvecs = np.concatenate([f(dt_bias).reshape(-1), f(a_log).reshape(-1), f(d_skip).reshape(-1)])[None, :].copy()
    hexp = np.zeros((128, 193), np.float32)
    hidx = (2 * np.arange(16)[None, :] + (np.arange(128)[:, None] // 64))
    hexp[:, 0:16] = f(a_log).reshape(-1)[hidx]
    hexp[:, 16:32] = f(dt_bias).reshape(-1)[hidx]
    hexp[:, 32:48] = f(d_skip).reshape(-1)[hidx]
    hh = np.arange(32)
    hexp[0:32, 48:176] = (hh[:, None] % 2 == (np.arange(128)[None, :] // 64))
    hexp[0:32, 176:192] = (hh[:, None] // 2 == np.arange(16)[None, :])
    hexp[0:32, 192] = f(dt_bias).reshape(-1)
    consts = _consts()
    mats = dict(ada_w=f(ada_w), w_in_even=f(w_in_even)[0], w_out_even=f(w_out_even)[0], w_in_odd=f(w_in_odd)[0],
                w_out_odd=f(w_out_odd)[0], mlp_w1=f(mlp_w1), mlp_w2=f(mlp_w2))
    wt = np.empty((len(UPIECES), 128, 8 * WCOLS), np.float32)
    for i, spec in enumerate(UPIECES):
        name, lead, r0, c0 = _piece_slice(spec)
        m = mats[name] if lead is None else mats[name][lead]
        blk_ = m[r0:r0 + 1024, c0:c0 + WCOLS].reshape(8, 128, WCOLS)
        wt[i] = blk_.transpose(1, 0, 2).reshape(128, 8 * WCOLS)
    wdt = np.ascontiguousarray(mats['w_in_odd'][:, 5120:5152].reshape(8, 128, 32).transpose(1, 0, 2).reshape(128, 256))
    shared = dict(wt=wt, wdt=wdt, vecs_in=vecs, rowvecs=rowvecs, hexp=hexp, consts=consts)
    in_maps = []
    for b in range(NCORES):
        sl = slice(b * NS, (b + 1) * NS)
        cc = np.zeros((18, D), np.float32)
        cc[0] = c_prompt[b]; cc[1:17] = c_sample[sl]
        m = dict(shared)
        m.update(xp=x_prompt[b], xs=np.ascontiguousarray(x_sample[sl, 0, :]), cc=cc,
                 st_hg=np.ascontiguousarray(state_hgrn[0, sl]), st_sc=np.ascontiguousarray(state_shortconv[0, sl]),
                 st_ssm=np.ascontiguousarray(state_ssm[0, sl]), st_mc=np.ascontiguousarray(state_mconv[0, sl]))
        in_maps.append(m)
    nc = _get_nc()
    import os
    ncore = int(os.environ.get('KNCORE', NCORES))
    res = run_bass_kernel_spmd(nc, in_maps[:ncore], core_ids=list(range(ncore)))
    if ncore < NCORES:
        res.results.extend([res.results[0]] * (NCORES - ncore))
    R = res.results
    cat = lambda k: np.concatenate([np.asarray(r[k], np.float32) for r in R], axis=0)
    stk = lambda k: np.stack([np.asarray(r[k], np.float32) for r in R], axis=0)
    y_prompt = stk("y_p")
    y_sample = cat("y_s")[:, None, :]
    return (y_prompt, y_sample,
            stk("hg_p")[None], cat("hg_s")[None],
            stk("sc_p")[None], cat("sc_s")[None],
            stk("ssm_p")[None], cat("ssm_s")[None],
            stk("mc_p")[None], cat("mc_s")[None])
```
